# Optimizing a Trainium2 kernel written in Bass

```python
import math
import jax
import jax.numpy as jnp
from jax import lax
import numpy as np

D_MODEL = 1024
BATCH = 2
SEQ = 8192
DEPTH = 1
DEC_BATCH = 32
DEC_SEQ = 8
PAST_LEN = 8192
PAGE_SIZE = 128

ATT_GROUPS = ((128, 1), (512, 4), (2048, 16))
N_GROUPS = 3
ATT_HEADS = 4
ATT_DH = 128
ATT_W = ATT_HEADS * ATT_DH
Q_ATT = N_GROUPS * ATT_W
BAND_BLK = 128
REL_BUCKETS = 32
REL_MAX_DIST = 2048
HG_HEADS = 4
HG_DK = 128
HG_DV = 128
HG_W = HG_HEADS * HG_DK
HG_VW = HG_HEADS * HG_DV
HG_CHUNK = 64
MEM_TOKENS = 256
MEM_HEADS = 4
MEM_DH = 128
MEM_W = MEM_HEADS * MEM_DH
D_FF = 4 * D_MODEL
CONV_W = 3
NORM_EPS = 1e-6
NEG_INF = -1e30
IN_SPLITS = (Q_ATT, Q_ATT, Q_ATT, HG_W, HG_W, HG_VW, HG_VW, MEM_W, D_MODEL, D_MODEL, D_MODEL)
IN_COLS = 3 * Q_ATT + 2 * HG_W + 2 * HG_VW + MEM_W + 3 * D_MODEL

kernel_name = "hybrid_dilated_hgrn2_memory_decoder_step"


def rms_norm(x, gain):
    xf = x.astype(jnp.float32)
    y = xf * lax.rsqrt(jnp.mean(xf * xf, axis=-1, keepdims=True) + NORM_EPS)
    return (y * gain.astype(jnp.float32)).astype(x.dtype)


def rel_bucket(dist):
    max_exact = REL_BUCKETS // 2
    d = jnp.maximum(dist, 1).astype(jnp.float32)
    large = max_exact + (jnp.log(d / max_exact) / math.log(REL_MAX_DIST / max_exact)
                         * (REL_BUCKETS - max_exact)).astype(jnp.int32)
    large = jnp.minimum(large, REL_BUCKETS - 1)
    return jnp.where(dist < max_exact, dist, large)


def group_lag_bias(rel_bias, g):
    window, dil = ATT_GROUPS[g]
    dist = jnp.arange(window // dil + 1, dtype=jnp.int32) * dil
    table = rel_bias[rel_bucket(dist)].astype(jnp.float32)
    return table[:, g * ATT_HEADS:(g + 1) * ATT_HEADS].T


def masked_softmax_parts(logits, valid):
    logits = jnp.where(valid, logits, NEG_INF)
    m = jnp.max(logits, axis=-1, keepdims=True)
    p = jnp.exp(logits - m)
    s = jnp.sum(p, axis=-1, keepdims=True)
    return p / s, s, m


def dilated_group_prompt(q, k, v, bias_lag, window, dil):
    B, S, H, Dh = q.shape
    n_lags = window // dil
    span = dil * BAND_BLK
    s_pad = -(-S // span) * span
    L = s_pad // dil
    nb = L // BAND_BLK

    def to_blocks(t):
        t = jnp.pad(t.astype(jnp.float32), ((0, 0), (0, s_pad - S), (0, 0), (0, 0)))
        t = t.reshape(B, L, dil, H, Dh).transpose(0, 2, 3, 1, 4)
        return t.reshape(B, dil, H, nb, BAND_BLK, Dh)

    def with_prev(t):
        prev = jnp.pad(t, ((0, 0), (0, 0), (0, 0), (1, 0), (0, 0), (0, 0)))[:, :, :, :-1]
        return jnp.concatenate([prev, t], axis=4)

    qb = to_blocks(q)
    kb = with_prev(to_blocks(k))
    vb = with_prev(to_blocks(v))
    qi = jnp.arange(BAND_BLK)[:, None]
    kj = jnp.arange(2 * BAND_BLK)[None, :]
    lag = BAND_BLK + qi - kj
    in_band = (lag >= 0) & (lag <= n_lags)
    blk = jnp.arange(nb)[:, None, None]
    valid = in_band[None] & ((blk > 0) | (kj >= BAND_BLK)[None])
    bias = bias_lag[:, jnp.clip(lag, 0, n_lags)]
    logits = (jnp.einsum('brhnqd,brhnkd->brhnqk', qb, kb) / math.sqrt(Dh)
              + bias[None, None, :, None])
    p, s, m = masked_softmax_parts(logits, valid[None, None, None])
    o = jnp.einsum('brhnqk,brhnkd->brhnqd', p, vb)
    lse = (m + jnp.log(s))[..., 0]
    o = o.reshape(B, dil, H, L, Dh).transpose(0, 3, 1, 2, 4).reshape(B, s_pad, H, Dh)[:, :S]
    lse = lse.reshape(B, dil, H, L).transpose(0, 3, 1, 2).reshape(B, s_pad, H)[:, :S]
    return o, lse


def dilated_group_sample(q, k_ctx, v_ctx, bias_lag, window, dil, n_past):
    T, Dh = q.shape[1], q.shape[3]
    n_lags = window // dil
    idx = n_past + jnp.arange(T)[:, None] - dil * jnp.arange(n_lags + 1)[None, :]
    valid = idx >= 0
    idx = jnp.maximum(idx, 0)
    kg = k_ctx.astype(jnp.float32)[:, idx]
    vg = v_ctx.astype(jnp.float32)[:, idx]
    logits = (jnp.einsum('bthd,btjhd->bhtj', q.astype(jnp.float32), kg) / math.sqrt(Dh)
              + bias_lag[:, None, :])
    p, s, m = masked_softmax_parts(logits, valid[None, None])
    o = jnp.einsum('bhtj,btjhd->bthd', p, vg)
    lse = (m + jnp.log(s))[..., 0].transpose(0, 2, 1)
    return o, lse


def dilated_attention(q, k, v, bias_lags, win_bufs):
    outs, lses, new_bufs = [], [], []
    for g, (window, dil) in enumerate(ATT_GROUPS):
        qg, kg, vg = q[:, :, g], k[:, :, g], v[:, :, g]
        if win_bufs is None:
            o, lse = dilated_group_prompt(qg, kg, vg, bias_lags[g], window, dil)
            kv = jnp.stack([kg, vg], axis=2)[:, -min(window, qg.shape[1]):]
        else:
            buf = win_bufs[g]
            n_past = buf.shape[1]
            k_ctx = jnp.concatenate([buf[:, :, 0].astype(kg.dtype), kg], axis=1)
            v_ctx = jnp.concatenate([buf[:, :, 1].astype(vg.dtype), vg], axis=1)
            o, lse = dilated_group_sample(qg, k_ctx, v_ctx, bias_lags[g], window, dil, n_past)
            kv = jnp.stack([k_ctx, v_ctx], axis=2)[:, -n_past:]
        outs.append(o)
        lses.append(lse)
        new_bufs.append(kv)
    w = jax.nn.softmax(jnp.stack(lses), axis=0)
    o = jnp.einsum('gbth,gbthd->bthd', w, jnp.stack(outs))
    return o, new_bufs


def hgrn2_recurrence(q, f_logit, i, lb, state0):
    f32 = jnp.float32
    B, T, H, Dk = q.shape
    Dv = i.shape[-1]
    f = lb + (1.0 - lb) * jax.nn.sigmoid(f_logit.astype(f32))
    g = jnp.log(f)
    k = 1.0 - f
    C = min(HG_CHUNK, T)
    t_pad = -(-T // C) * C
    n = t_pad // C

    def chunks(t):
        t = jnp.pad(t.astype(f32), ((0, 0), (0, t_pad - T), (0, 0), (0, 0)))
        return t.reshape(B, n, C, H, t.shape[-1]).transpose(1, 0, 3, 2, 4)

    causal = (jnp.arange(C)[:, None] >= jnp.arange(C)[None, :])[:, :, None]

    def step(S, inp):
        qc, gc, kc, vc = inp
        G = jnp.cumsum(gc, axis=2)
        o_inter = jnp.einsum('bhtk,bhkv->bhtv', qc * jnp.exp(G), S)
        diff = G[:, :, :, None, :] - G[:, :, None, :, :]
        decay = jnp.where(causal, jnp.exp(jnp.minimum(diff, 0.0)), 0.0)
        scores = jnp.einsum('bhtk,bhsk,bhtsk->bhts', qc, kc, decay)
        o_intra = jnp.einsum('bhts,bhsv->bhtv', scores, vc)
        G_end = G[:, :, -1:, :]
        S_new = (jnp.exp(G_end[:, :, 0, :, None]) * S
                 + jnp.einsum('bhsk,bhsv->bhkv', kc * jnp.exp(G_end - G), vc))
        return S_new, o_inter + o_intra

    S_fin, o = lax.scan(step, state0.astype(f32), (chunks(q), chunks(g), chunks(k), chunks(i)))
    o = o.transpose(1, 0, 3, 2, 4).reshape(B, t_pad, H, Dv)[:, :T]
    return o, S_fin


def memory_attention(q, mem_k, mem_v):
    logits = jnp.einsum('bthd,bmhd->bhtm', q.astype(jnp.float32), mem_k.astype(jnp.float32)) / math.sqrt(MEM_DH)
    p = jax.nn.softmax(logits, axis=-1)
    return jnp.einsum('bhtm,bmhd->bthd', p, mem_v.astype(jnp.float32))


def memory_kv(mem, gain, w_kv):
    B, M, _ = mem.shape
    return (rms_norm(mem, gain) @ w_kv).reshape(B, M, 2, MEM_HEADS, MEM_DH)


def token_mixer(h, win_bufs, hg_state0, mem_k, mem_v, bias_lags, lb,
                w_in, b_in, hg_norm, w_br_att, w_br_hg, w_br_mem, w_out):
    B, T, _ = h.shape
    dt = h.dtype
    offsets = np.cumsum(IN_SPLITS)[:-1].tolist()
    proj = h @ w_in + b_in
    q_a, k_a, v_a, hq, hf, hi, hgate, mq, ga, gh, gm = jnp.split(proj, offsets, axis=-1)
    att_shape = (B, T, N_GROUPS, ATT_HEADS, ATT_DH)
    att_o, new_win = dilated_attention(q_a.reshape(att_shape), k_a.reshape(att_shape),
                                       v_a.reshape(att_shape), bias_lags, win_bufs)
    hg_o, hg_state = hgrn2_recurrence(hq.reshape(B, T, HG_HEADS, HG_DK), hf.reshape(B, T, HG_HEADS, HG_DK),
                                      hi.reshape(B, T, HG_HEADS, HG_DV), lb.reshape(HG_HEADS, HG_DK), hg_state0)
    hg_o = rms_norm(hg_o, hg_norm) * jax.nn.sigmoid(hgate.reshape(B, T, HG_HEADS, HG_DV).astype(jnp.float32))
    mem_o = memory_attention(mq.reshape(B, T, MEM_HEADS, MEM_DH), mem_k, mem_v)
    merged = (jax.nn.sigmoid(ga) * (att_o.reshape(B, T, ATT_W).astype(dt) @ w_br_att)
              + jax.nn.sigmoid(gh) * (hg_o.reshape(B, T, HG_VW).astype(dt) @ w_br_hg)
              + jax.nn.sigmoid(gm) * (mem_o.reshape(B, T, MEM_W).astype(dt) @ w_br_mem))
    return merged @ w_out, new_win, hg_state


def conv_ffn(h, conv_buf, w_a, w_b, conv_w, conv_b, w_d):
    T = h.shape[1]
    a = h @ w_a
    ctx = jnp.concatenate([conv_buf.astype(a.dtype), a], axis=1)
    c = conv_b
    for j in range(CONV_W):
        c = c + ctx[:, j:j + T] * conv_w[j]
    y = (jax.nn.silu(c) * (h @ w_b)) @ w_d
    return y, ctx[:, -(CONV_W - 1):]


def decoder_layer(x, win_bufs, hg_state0, conv_buf0, mem_k, mem_v, bias_lags, lb,
                  norm_mix_pre, norm_mix_post, w_in, b_in, hg_norm, w_br_att, w_br_hg, w_br_mem, w_out,
                  norm_ffn_pre, norm_ffn_post, w_ffn_a, w_ffn_b, ffn_conv_w, ffn_conv_b, w_ffn_d):
    mix, new_win, hg_state = token_mixer(rms_norm(x, norm_mix_pre), win_bufs, hg_state0, mem_k, mem_v,
                                         bias_lags, lb, w_in, b_in, hg_norm, w_br_att, w_br_hg, w_br_mem, w_out)
    x = x + rms_norm(mix, norm_mix_post)
    f, conv_buf = conv_ffn(rms_norm(x, norm_ffn_pre), conv_buf0, w_ffn_a, w_ffn_b, ffn_conv_w, ffn_conv_b, w_ffn_d)
    x = x + rms_norm(f, norm_ffn_post)
    return x, new_win, hg_state, conv_buf


def setup_inputs(seed: int = 0) -> dict:
    key = jax.random.key(seed)
    keys = iter(jax.random.split(key, 40))

    def nrm(shape, scale=1.0):
        return jax.random.normal(next(keys), shape, jnp.float32) * scale

    def gain(shape):
        return 1.0 + nrm(shape, 0.05)

    win_lens = [min(w, PAST_LEN) for w, _ in ATT_GROUPS]
    return {
        'x_prompt': nrm((BATCH, SEQ, D_MODEL)),
        'x_sample': nrm((DEC_BATCH, DEC_SEQ, D_MODEL)),
        'mem_prompt': nrm((BATCH, MEM_TOKENS, D_MODEL)),
        'cache_win1_kv': nrm((DEPTH, DEC_BATCH, win_lens[0], 2, ATT_HEADS, ATT_DH)),
        'cache_win2_kv': nrm((DEPTH, DEC_BATCH, win_lens[1], 2, ATT_HEADS, ATT_DH)),
        'cache_win3_kv': nrm((DEPTH, DEC_BATCH, win_lens[2], 2, ATT_HEADS, ATT_DH)),
        'cache_mem_kv': nrm((DEPTH, DEC_BATCH, MEM_TOKENS, 2, MEM_HEADS, MEM_DH)),
        'state_hgrn': nrm((DEPTH, DEC_BATCH, HG_HEADS, HG_DK, HG_DV), 0.5),
        'state_ffn_conv': nrm((DEPTH, DEC_BATCH, CONV_W - 1, D_FF)),
        'rel_bias': nrm((REL_BUCKETS, N_GROUPS * ATT_HEADS), 0.5),
        'hg_lb_logits': nrm((DEPTH + 1, HG_W), 0.5),
        'norm_mix_pre': gain((DEPTH, D_MODEL)),
        'norm_mix_post': gain((DEPTH, D_MODEL)),
        'w_in': nrm((DEPTH, D_MODEL, IN_COLS), D_MODEL ** -0.5),
        'b_in': nrm((DEPTH, IN_COLS), 0.02),
        'hg_norm': gain((DEPTH, HG_DV)),
        'mem_norm': gain((DEPTH, D_MODEL)),
        'w_mem_kv': nrm((DEPTH, D_MODEL, 2 * MEM_W), D_MODEL ** -0.5),
        'w_br_att': nrm((DEPTH, ATT_W, D_MODEL), ATT_W ** -0.5),
        'w_br_hg': nrm((DEPTH, HG_VW, D_MODEL), HG_VW ** -0.5),
        'w_br_mem': nrm((DEPTH, MEM_W, D_MODEL), MEM_W ** -0.5),
        'w_out': nrm((DEPTH, D_MODEL, D_MODEL), D_MODEL ** -0.5),
        'norm_ffn_pre': gain((DEPTH, D_MODEL)),
        'norm_ffn_post': gain((DEPTH, D_MODEL)),
        'w_ffn_a': nrm((DEPTH, D_MODEL, D_FF), D_MODEL ** -0.5),
        'w_ffn_b': nrm((DEPTH, D_MODEL, D_FF), D_MODEL ** -0.5),
        'ffn_conv_w': nrm((DEPTH, CONV_W, D_FF), CONV_W ** -0.5),
        'ffn_conv_b': nrm((DEPTH, D_FF), 0.02),
        'w_ffn_d': nrm((DEPTH, D_FF, D_MODEL), D_FF ** -0.5),
    }


def reference(x_prompt, x_sample, mem_prompt, cache_win1_kv, cache_win2_kv, cache_win3_kv, cache_mem_kv,
              state_hgrn, state_ffn_conv, rel_bias, hg_lb_logits, norm_mix_pre, norm_mix_post, w_in, b_in,
              hg_norm, mem_norm, w_mem_kv, w_br_att, w_br_hg, w_br_mem, w_out, norm_ffn_pre, norm_ffn_post,
              w_ffn_a, w_ffn_b, ffn_conv_w, ffn_conv_b, w_ffn_d):
    f32 = jnp.float32
    bias_lags = [group_lag_bias(rel_bias, g) for g in range(N_GROUPS)]
    lower_bounds = jnp.cumsum(jax.nn.softmax(hg_lb_logits.astype(f32), axis=0), axis=0)
    B = x_prompt.shape[0]
    xp, xs = x_prompt, x_sample
    p_win = [[] for _ in range(N_GROUPS)]
    s_win = [[] for _ in range(N_GROUPS)]
    p_hg, p_conv, p_mem, s_hg, s_conv = [], [], [], [], []
    for layer in range(DEPTH):
        lw = (norm_mix_pre[layer], norm_mix_post[layer], w_in[layer], b_in[layer], hg_norm[layer],
              w_br_att[layer], w_br_hg[layer], w_br_mem[layer], w_out[layer], norm_ffn_pre[layer],
              norm_ffn_post[layer], w_ffn_a[layer], w_ffn_b[layer], ffn_conv_w[layer], ffn_conv_b[layer],
              w_ffn_d[layer])
        lb = lower_bounds[layer]
        mem_kv = memory_kv(mem_prompt, mem_norm[layer], w_mem_kv[layer])
        xp, win_p, hg_p, conv_p = decoder_layer(
            xp, None, jnp.zeros((B, HG_HEADS, HG_DK, HG_DV), f32),
            jnp.zeros((B, CONV_W - 1, D_FF), xp.dtype), mem_kv[:, :, 0], mem_kv[:, :, 1], bias_lags, lb, *lw)
        win_bufs = (cache_win1_kv[layer], cache_win2_kv[layer], cache_win3_kv[layer])
        xs, win_s, hg_s, conv_s = decoder_layer(
            xs, win_bufs, state_hgrn[layer], state_ffn_conv[layer],
            cache_mem_kv[layer][:, :, 0], cache_mem_kv[layer][:, :, 1], bias_lags, lb, *lw)
        for g in range(N_GROUPS):
            p_win[g].append(win_p[g])
            s_win[g].append(win_s[g])
        p_hg.append(hg_p.astype(x_prompt.dtype))
        p_conv.append(conv_p)
        p_mem.append(mem_kv)
        s_hg.append(hg_s.astype(x_sample.dtype))
        s_conv.append(conv_s)
    return (xp, xs,
            jnp.stack(p_win[0]), jnp.stack(p_win[1]), jnp.stack(p_win[2]),
            jnp.stack(p_hg), jnp.stack(p_conv), jnp.stack(p_mem),
            jnp.stack(s_win[0]), jnp.stack(s_win[1]), jnp.stack(s_win[2]),
            jnp.stack(s_hg), jnp.stack(s_conv))
```

```python
import math
import numpy as np
import concourse.bass as bass
import concourse.mybir as mybir
from concourse.ap import AP
from concourse.bass_utils import run_bass_kernel_spmd

F32 = mybir.dt.float32
BF16 = mybir.dt.bfloat16
ALU = mybir.AluOpType
AF = mybir.ActivationFunctionType
AX = mybir.AxisListType

EPOCH = 3000


class Buf:
    __slots__ = ("t", "name", "w", "r", "excl")

    def __init__(self, t, name, excl=False):
        self.t = t
        self.name = name
        self.w = None
        self.r = []
        self.excl = excl

    def __getitem__(self, k):
        return self.t[k]


class Eng:
    def __init__(self, nc, name, obj, nep, ndma):
        self.name = name
        self.obj = obj
        self.sems = [nc.alloc_semaphore(f"s_{name}_{i}") for i in range(nep)]
        self.n = 0
        self.waited = {}
        self.dslots = [[nc.alloc_semaphore(f"d_{name}_{i}"), 0] for i in range(ndma)]
        self.dn = 0

    def tag_of(self, n):
        return (self.sems[(n - 1) // EPOCH], (n - 1) % EPOCH + 1, self.name)


class Trk:
    def __init__(self, nc):
        self.nc = nc
        self.E = {
            "pe": Eng(nc, "pe", nc.tensor, 14, 0),
            "act": Eng(nc, "act", nc.scalar, 8, 0),
            "dve": Eng(nc, "dve", nc.vector, 8, 0),
            "pool": Eng(nc, "pool", nc.gpsimd, 4, 16),
            "sp": Eng(nc, "sp", nc.sync, 1, 24),
        }
        self.nbuf = 0

    def sb(self, shape, dt, name):
        self.nbuf += 1
        return Buf(self.nc.alloc_sbuf_tensor(f"{name}_{self.nbuf}", list(shape), dt), name)

    def dram(self, shape, dt, name, kind="Internal"):
        return Buf(self.nc.dram_tensor(name, list(shape), dt, kind=kind), name)

    def _wait(self, E, deps):
        for d in deps:
            if d is None:
                continue
            sem, val, src = d
            if src == "pe" and E.name == "pe":
                continue
            key = id(sem)
            if E.waited.get(key, 0) < val:
                E.obj.wait_ge(sem, val)
                E.waited[key] = val

    def _deps(self, reads, writes):
        deps = []
        for b in reads:
            deps.append(b.w)
            if b.excl:
                deps.extend(b.r)
        for b in writes:
            deps.append(b.w)
            deps.extend(b.r)
        return deps

    def op(self, eng, fn, reads=(), writes=(), inc=True):
        E = self.E[eng]
        self._wait(E, self._deps(reads, writes))
        ins = fn(E.obj)
        if inc:
            E.n += 1
            tag = E.tag_of(E.n)
            ins.then_inc(tag[0], 1)
        else:
            assert eng == "pe"
            tag = E.tag_of(E.n + 1)
        for b in reads:
            b.r.append(tag)
        for b in writes:
            b.w = tag
            b.r = []
        return ins

    def dma(self, q, out, in_, reads=(), writes=(), **kw):
        E = self.E[q]
        self._wait(E, self._deps(reads, writes))
        slot = E.dslots[E.dn % len(E.dslots)]
        E.dn += 1
        if slot[1] > 0 and E.waited.get(id(slot[0]), 0) < slot[1]:
            E.obj.wait_ge(slot[0], slot[1])
            E.waited[id(slot[0])] = slot[1]
        ins = E.obj.dma_start(out=out, in_=in_, **kw)
        slot[1] += 16
        ins.then_inc(slot[0], 16)
        tag = (slot[0], slot[1], "dma")
        for b in reads:
            b.r.append(tag)
        for b in writes:
            b.w = tag
            b.r = []
        return ins

    def finish(self):
        for q in ("sp", "pool"):
            E = self.E[q]
            for sem, val in E.dslots:
                if val > 0:
                    E.obj.wait_ge(sem, val)
        sp = self.E["sp"]
        for nm in ("pe", "act", "dve", "pool"):
            E = self.E[nm]
            if E.n > 0:
                sem, val, _ = E.tag_of(E.n)
                sp.obj.wait_ge(sem, val)


class Pool:
    def __init__(self, bufs):
        self.bufs = bufs
        self.i = 0

    def get(self):
        b = self.bufs[self.i % len(self.bufs)]
        self.i += 1
        return b


D = 1024
KC = 8
IN_COLS = 10240
DFF = 4096
W_G = (128, 512, 2048)
DIL = (1, 4, 16)
NDEL = (2, 5, 17)
C_Q, C_K, C_V, C_HQ, C_HF, C_HI, C_HG, C_MQ, C_GA, C_GH, C_GM = (
    0, 1536, 3072, 4608, 5120, 5632, 6144, 6656, 7168, 8192, 9216)
EPS = 1e-6
TABL = 2304
SCALE = 1.0 / math.sqrt(128.0)


def _rel_bucket_np(dist):
    d = np.maximum(dist, 1).astype(np.float32)
    large = 16 + (np.log(d / np.float32(16)) / np.float32(math.log(128.0)) * np.float32(16)).astype(np.int32)
    large = np.minimum(large, 31)
    return np.where(dist < 16, dist, large)


def _static_consts():
    c = {}
    c["identf"] = np.eye(128, dtype=np.float32)
    c["antiid"] = np.eye(128, dtype=np.float32)[::-1].copy()
    s = np.arange(128)
    c["tri_incl"] = (s[:, None] <= s[None, :]).astype(np.float32)
    c["tri_up"] = (s[:, None] > s[None, :]).astype(np.float32)
    c["causal"] = (s[:, None] <= s[None, :]).astype(np.float32)
    s32 = np.arange(32)
    same = (s32[:, None] // 8) == (s32[None, :] // 8)
    t32 = np.zeros((128, 128), np.float32); t32[:32, :32] = same & (s32[:, None] <= s32[None, :])
    u32 = np.zeros((128, 128), np.float32); u32[:32, :32] = same & (s32[:, None] > s32[None, :])
    c["tri_incl_s"] = t32
    c["tri_up_s"] = u32
    c["causal_s"] = t32.copy()
    bd = np.zeros((128, 128), np.float32); bd[:32, :32] = same
    c["blockdiag_s"] = bd
    rm = np.zeros((128, 4), np.float32)
    for b in range(4):
        rm[8 * b:8 * b + 8, b] = 1.0
    c["rowmask_s"] = rm
    oh = np.zeros((3, 32, TABL), np.float32)
    for g in range(3):
        i = np.arange(TABL)
        dl = i - 127
        ok = (dl >= 0) & (dl <= W_G[g]) & (dl % DIL[g] == 0)
        bk = _rel_bucket_np(np.maximum(dl, 0).astype(np.int32))
        oh[g, bk[ok], i[ok]] = 1.0
    c["onehot"] = oh
    return c


class KB:
    def __init__(self, npre, nhalo, nmain, stn=2, with_sample=True, dbg=None):
        self.DBG = dbg
        self.npre, self.nhalo, self.nmain, self.stn = npre, nhalo, nmain, stn
        self.with_sample = with_sample
        self.NT = npre + nhalo + 1 + nmain
        self.T0 = npre + nhalo
        self.first_g = [self.T0 - min(nhalo, NDEL[g] - 1) for g in range(3)]
        self.ntile_g = [self.NT - self.first_g[g] for g in range(3)]
        nc = self.nc = bass.Bass("TRN2", target_bir_lowering=False)
        T = self.T = Trk(nc)
        self.declare_io()
        self.alloc()
        try:
            self.setup()
            self.stop_at("setup")
            self.prompt()
            self.stop_at("prompt")
            if with_sample:
                self.sample()
        except StopIteration:
            pass
        T.finish()

    def declare_io(s):
        T = s.T
        I = lambda n, sh: T.dram(sh, F32, n, kind="ExternalInput")
        O = lambda n, sh: T.dram(sh, F32, n, kind="ExternalOutput")
        NT = s.NT
        s.x_ext = I("x_ext", [NT * 128, D])
        s.vmask = I("vmask", [NT * 128, 1])
        s.flags = I("flags", [1, 2])
        s.mem_in = I("mem_in", [256, D])
        s.xs_in = I("xs_in", [32, D])
        s.cwin = [I(f"cwin{g}", [4, W_G[g], 1024]) for g in range(3)]
        s.cmem = I("cmem", [4, 256, 1024])
        s.shg_in = I("shg_in", [4, 4, 128, 128])
        s.sconv_in = I("sconv_in", [8, DFF])
        s.rel_bias = I("rel_bias", [32, 12])
        s.lb_logits = I("lb_logits", [2, 512])
        s.g_mix_pre = I("g_mix_pre", [D]); s.g_mix_post = I("g_mix_post", [D])
        s.g_ffn_pre = I("g_ffn_pre", [D]); s.g_ffn_post = I("g_ffn_post", [D])
        s.g_mem = I("g_mem", [D]); s.g_hg = I("g_hg", [128])
        s.w_in = I("w_in", [D, IN_COLS]); s.b_in = I("b_in", [IN_COLS])
        s.w_memkv = I("w_memkv", [D, 1024])
        s.w_br = [I(f"w_br{i}", [512, D]) for i in range(3)]
        s.w_out = I("w_out", [D, D])
        s.w_a = I("w_a", [D, DFF]); s.w_b = I("w_b", [D, DFF]); s.w_d = I("w_d", [DFF, D])
        s.conv_w = I("conv_w", [3, DFF]); s.conv_b = I("conv_b", [DFF])
        for nm in ("identf", "antiid", "tri_incl", "tri_up", "causal", "tri_incl_s", "tri_up_s",
                   "causal_s", "blockdiag_s"):
            setattr(s, "c_" + nm, I("c_" + nm, [128, 128]))
        s.c_rowmask_s = I("c_rowmask_s", [128, 4])
        s.c_onehot = I("c_onehot", [3, 32, TABL])
        s.o_y = O("o_y", [s.nmain * 128, D])
        s.o_ys = O("o_ys", [32, D])
        s.o_pwin = [O(f"o_pwin{g}", [min(W_G[g], s.nmain * 128), 1024]) for g in range(3)]
        s.o_phg = O("o_phg", [4, 128, 128])
        s.o_pconv = O("o_pconv", [2, DFF])
        s.o_pmem = O("o_pmem", [256, 1024])
        s.o_swin = [O(f"o_swin{g}", [4, W_G[g], 1024]) for g in range(3)]
        s.o_shg = O("o_shg", [4, 4, 128, 128])
        s.o_sconv = O("o_sconv", [8, DFF])
        Sc = lambda n, sh, dt=BF16: T.dram(sh, dt, n)
        s.wb_in = [Sc(f"wb_in{i}", [D, 1024]) for i in range(10)]; s.bb_in = Sc("bb_in", [1, IN_COLS])
        s.wb_memkv = Sc("wb_memkv", [D, 1024])
        s.wb_br = [Sc(f"wb_br{i}", [512, D]) for i in range(3)]
        s.wb_out = Sc("wb_out", [D, D])
        s.wb_a = Sc("wb_a", [D, DFF]); s.wb_b = Sc("wb_b", [D, DFF]); s.wb_d = Sc("wb_d", [DFF, D])
        s.vtab = Sc("vtab", [12, TABL], F32)
        s.Kd = [Sc(f"Kd{g}", [s.ntile_g[g], 128, 512]) for g in range(3)]
        s.Vd = [Sc(f"Vd{g}", [s.ntile_g[g], 128, 520]) for g in range(3)]

    DBG = None

    def stop_at(s, name):
        if s.DBG and s.DBG.get("stop") == name:
            raise StopIteration

    def dbg(s, name, ap, bufs, shape, dt=F32):
        if not s.DBG or name in s.DBG["done"] or name not in s.DBG["want"]:
            return
        s.DBG["done"].add(name)
        d = s.T.dram(list(shape), dt, "dbg_" + name, kind="ExternalOutput")
        s.T.dma("sp", d[:], ap, reads=bufs, writes=[d])

    def ACT(s, out, in_, func, R, W, **kw):
        return s.T.op("act", lambda e: e.activation(out=out, in_=in_, func=func, **kw), R, W)

    def TT(s, out, a, b, op, R, W, eng="dve"):
        return s.T.op(eng, lambda e: e.tensor_tensor(out=out, in0=a, in1=b, op=op), R, W)

    def TS(s, out, a, s1, s2, op0, op1, R, W, eng="dve"):
        if s2 is None:
            return s.T.op(eng, lambda e: e.tensor_scalar(out=out, in0=a, scalar1=s1, scalar2=None, op0=op0), R, W)
        return s.T.op(eng, lambda e: e.tensor_scalar(out=out, in0=a, scalar1=s1, scalar2=s2, op0=op0, op1=op1), R, W)

    def STT(s, out, a, sc, b, op0, op1, R, W, eng="dve"):
        return s.T.op(eng, lambda e: e.scalar_tensor_tensor(out=out, in0=a, scalar=sc, in1=b, op0=op0, op1=op1), R, W)

    def CP(s, out, in_, R, W, eng="dve"):
        if eng == "act":
            return s.T.op("act", lambda e: e.activation(out=out, in_=in_, func=AF.Copy), R, W)
        return s.T.op(eng, lambda e: e.tensor_copy(out=out, in_=in_), R, W)

    def MM(s, out, lhsT, rhs, start, stop, R, W, inc=True):
        return s.T.op("pe", lambda e: e.matmul(out, lhsT=lhsT, rhs=rhs, start=start, stop=stop), R, W, inc=inc)

    def TR(s, out, in_, ident, R, W):
        return s.T.op("pe", lambda e: e.transpose(out=out, in_=in_, identity=ident), R, W)

    def alloc(s):
        T, nc = s.T, s.nc
        NTOK = s.stn * 128
        s.NTOK = NTOK
        banks = [Buf(nc.alloc_psum_tensor(f"psb{i}", [128, 512], F32), f"psb{i}", excl=True) for i in range(8)]
        s.psA = Pool(banks[:4])
        s.psB = Pool(banks[4:])
        P = lambda name, shape, dt, n: Pool([T.sb(shape, dt, name) for _ in range(n)])
        s.idf = T.sb([128, 128], F32, "idf"); s.idb = T.sb([128, 128], BF16, "idb")
        s.tri_incl = T.sb([128, 128], F32, "tri_incl"); s.tri_up = T.sb([128, 128], F32, "tri_up")
        s.causal = s.tri_incl
        s.tri_incl_s = T.sb([128, 128], F32, "tri_incl_s"); s.tri_up_s = T.sb([128, 128], F32, "tri_up_s")
        s.causal_s = s.tri_incl_s; s.bd_s = T.sb([128, 128], F32, "bd_s")
        s.rowmask = T.sb([128, 4], F32, "rowmask")
        s.ones_bf = T.sb([1, 128], BF16, "ones_bf"); s.ones_col = T.sb([128, 1], F32, "ones_col")
        s.gcols = T.sb([128, 24], F32, "gcols")
        s.bcol = T.sb([128, 80], F32, "bcol")
        s.ccol = T.sb([128, 128], F32, "ccol")
        s.lbcol = T.sb([128, 12], F32, "lbcol")
        s.oml_tm = T.sb([128, 512], F32, "oml_tm")
        s.noml_tm = T.sb([128, 512], F32, "noml_tm")
        s.gpost_mix = T.sb([128, D], F32, "gpost_mix"); s.gpost_ffn = T.sb([128, D], F32, "gpost_ffn")
        s.ghg4 = T.sb([128, 512], F32, "ghg4")
        s.flg = T.sb([128, 2], F32, "flg")
        s.Mtab = T.sb([128, 24 * 4 * 128], BF16, "Mtab")
        s.Mnew = T.sb([128, 12 * 32], BF16, "Mnew")
        s.wpool = P("wslot", [128, 9, 512], BF16, 4)
        s.xpool = P("xt", [128, D], F32, s.stn + 1)
        s.vmpool = P("vm", [128, 1], F32, s.stn + 1)
        s.hT = P("hT", [128, KC, NTOK], BF16, 1)
        s.h2T = P("h2T", [128, KC, NTOK], BF16, 1)
        s.Qfm = T.sb([128, 12, NTOK], BF16, "Qfm")
        s.MQfm = T.sb([128, 4, NTOK], BF16, "MQfm")
        s.qfm = T.sb([128, 4, NTOK], F32, "qfm")
        s.kTfm = T.sb([128, 4, NTOK], F32, "kTfm")
        s.attT = T.sb([128, 4, NTOK], BF16, "attT"); s.hgT = T.sb([128, 4, NTOK], BF16, "hgT")
        s.memT = T.sb([128, 4, NTOK], BF16, "memT")
        s.kslot = P("kslot", [128, 512], BF16, 5)
        s.vslot = P("vslot", [128, 520], BF16, 5)
        s.memK = T.sb([128, 4, 256], BF16, "memK"); s.memV = T.sb([128, 2, 520], BF16, "memV")
        s.f2k = P("f2k", [128, 512], F32, 7)
        s.b1k = P("b1k", [128, 512], BF16, 6)
        s.f4k = P("f4k", [128, D], F32, 4)
        s.b2k = P("b2k", [128, D], BF16, 3)
        s.sm = P("sm", [128, 16], F32, 12)
        s.S = T.sb([128, 512], F32, "S"); s.Sbf = T.sb([128, 512], BF16, "Sbf")
        s.hv = T.sb([128, 512], BF16, "hv"); s.hqhat = T.sb([128, 512], BF16, "hqhat")
        s.hqtil = T.sb([128, 512], BF16, "hqtil")
        s.ebias = T.sb([32, 12], F32, "ebias")
        s.Knew = T.sb([128, 384], BF16, "Knew")
        s.Gext = P("Gext", [128, 4, 129], F32, 1)
        s.carry = T.sb([128, 32, 2], F32, "carry")
        s.aext = P("aext", [128, 264], F32, 2)
        s.stage = s.f4k

    def colstage(s, rows_spec, dst_buf, ncols):
        T = s.T
        st = s.f2k.get()
        r0 = 0
        for ap, r in rows_spec:
            T.dma("sp", st[r0:r0 + r, 0:128], ap, writes=[st])
            r0 += r
        assert r0 == ncols
        ps = s.psB.get()
        s.TR(ps[:, 0:ncols], st[0:ncols, 0:128], s.idf[0:ncols, 0:ncols], [st, s.idf], [ps])
        s.CP(dst_buf[:, 0:ncols], ps[:, 0:ncols], [ps], [dst_buf])
        return st

    def setup(s):
        T = s.T
        ld = lambda dst, src: T.dma("sp", dst[:], src[:], writes=[dst])
        s.J = s.f4k.get()
        ld(s.idf, s.c_identf); T.dma("sp", s.J[:, 0:128], s.c_antiid[:], writes=[s.J])
        ld(s.tri_incl, s.c_tri_incl); ld(s.tri_up, s.c_tri_up)
        ld(s.tri_incl_s, s.c_tri_incl_s); ld(s.tri_up_s, s.c_tri_up_s)
        ld(s.bd_s, s.c_blockdiag_s); ld(s.rowmask, s.c_rowmask_s)
        s.CP(s.idb[:], s.idf[:], [s.idf], [s.idb], eng="pool")
        T.op("pool", lambda e: e.memset(s.ones_bf[:], 1.0), [], [s.ones_bf])
        T.op("pool", lambda e: e.memset(s.ones_col[:], 1.0), [], [s.ones_col])
        T.op("pool", lambda e: e.memset(s.carry[:], 0.0), [], [s.carry])
        T.dma("sp", s.flg[:], AP(s.flags.t, 0, [[0, 128], [1, 2]]), writes=[s.flg])
        v8 = lambda b: b[:].rearrange("(r c) -> r c", c=128)
        s.colstage([(v8(s.g_mix_pre), 8), (v8(s.g_ffn_pre), 8), (v8(s.g_mem), 8)], s.gcols, 24)
        st = s.colstage([(v8(s.b_in), 80)], s.bcol, 80)
        bb = s.b1k.get()
        s.CP(bb[0:80, 0:128], st[0:80, 0:128], [st], [bb])
        T.dma("pool", s.bb_in[:].rearrange("o (r c) -> (o r) c", c=128), bb[0:80, 0:128], reads=[bb], writes=[s.bb_in])
        s.colstage([(s.conv_w[:].rearrange("j (r c) -> (j r) c", c=128), 96), (v8(s.conv_b), 32)], s.ccol, 128)
        tmp = T.sb([128, 8], F32, "lbtmp")
        s.colstage([(s.lb_logits[:].rearrange("l (r c) -> (l r) c", c=128), 8)], tmp, 8)
        s.TT(s.lbcol[:, 0:4], tmp[:, 0:4], tmp[:, 4:8], ALU.subtract, [tmp], [s.lbcol])
        s.ACT(s.lbcol[:, 0:4], s.lbcol[:, 0:4], AF.Sigmoid, [s.lbcol], [s.lbcol])
        s.TS(s.lbcol[:, 4:8], s.lbcol[:, 0:4], -1.0, 1.0, ALU.mult, ALU.add, [s.lbcol], [s.lbcol])
        s.TS(s.lbcol[:, 8:12], s.lbcol[:, 4:8], -1.0, None, ALU.mult, None, [s.lbcol], [s.lbcol])
        l1 = s.f2k.get()
        s.lb_tm = s.f2k.get()
        T.dma("sp", s.lb_tm[:], s.lb_logits[0, :].partition_broadcast(128), writes=[s.lb_tm])
        T.dma("sp", l1[:], s.lb_logits[1, :].partition_broadcast(128), writes=[l1])
        s.TT(s.lb_tm[:], s.lb_tm[:], l1[:], ALU.subtract, [s.lb_tm, l1], [s.lb_tm])
        s.ACT(s.lb_tm[:], s.lb_tm[:], AF.Sigmoid, [s.lb_tm], [s.lb_tm])
        s.TS(s.oml_tm[:], s.lb_tm[:], -1.0, 1.0, ALU.mult, ALU.add, [s.lb_tm], [s.oml_tm])
        s.TS(s.noml_tm[:], s.oml_tm[:], -1.0, None, ALU.mult, None, [s.oml_tm], [s.noml_tm])
        T.dma("sp", s.gpost_mix[:], s.g_mix_post[:].partition_broadcast(128), writes=[s.gpost_mix])
        T.dma("sp", s.gpost_ffn[:], s.g_ffn_post[:].partition_broadcast(128), writes=[s.gpost_ffn])
        for h in range(4):
            T.dma("sp", s.ghg4[:, h * 128:(h + 1) * 128], s.g_hg[:].partition_broadcast(128), writes=[s.ghg4])
        eb = s.ebias
        T.dma("sp", eb[0:32, 0:12], s.rel_bias[:], writes=[eb])
        s.ACT(eb[0:32, 0:12], eb[0:32, 0:12], AF.Exp, [eb], [eb])
        for g in range(3):
            for c0 in range(0, TABL, 512):
                n = min(512, TABL - c0)
                oh = s.f2k.get()
                T.dma("sp", oh[0:32, 0:n], s.c_onehot[g, :, c0:c0 + n], writes=[oh])
                ps = s.psB.get()
                s.MM(ps[0:4, 0:n], eb[0:32, g * 4:(g + 1) * 4], oh[0:32, 0:n], True, True, [eb, oh], [ps])
                vt = s.f2k.get()
                s.CP(vt[0:4, 0:n], ps[0:4, 0:n], [ps], [vt])
                T.dma("pool", s.vtab[g * 4:(g + 1) * 4, c0:c0 + n], vt[0:4, 0:n], reads=[vt], writes=[s.vtab])
        Mv = s.Mtab[:].rearrange("p (b h q) -> p b h q", h=4, q=128)
        blk0 = 0
        s.blk0 = []
        for g in range(3):
            s.blk0.append(blk0)
            for h in range(4):
                for d0 in range(0, NDEL[g], 4):
                    nd = min(4, NDEL[g] - d0)
                    hk = s.f2k.get()
                    T.dma("sp", hk[:, 0:nd * 128],
                          AP(s.vtab.t, (g * 4 + h) * TABL + d0 * 128, [[1, 128], [1, nd * 128]]),
                          reads=[s.vtab], writes=[hk])
                    ps = s.psB.get()
                    s.MM(ps[:, 0:nd * 128], s.J[:, 0:128], hk[:, 0:nd * 128], True, True, [s.J, hk], [ps])
                    s.CP(Mv[:, blk0 + d0:blk0 + d0 + nd, h, :],
                         ps[:, 0:nd * 128].rearrange("p (b q) -> p b q", q=128), [ps], [s.Mtab],
                         eng=("act" if h % 2 else "dve"))
            blk0 += NDEL[g]
        Mn = s.Mnew[:].rearrange("p (a q) -> p a q", q=32)
        for g in range(3):
            for h in range(4):
                s.TT(Mn[0:32, g * 4 + h, :], Mv[0:32, s.blk0[g], h, 0:32], s.bd_s[0:32, 0:32], ALU.mult,
                     [s.Mtab, s.bd_s], [s.Mnew])
        s.castg = s.cast_gen()
        s.run_casts(8)

    def cast_gen(s):
        T = s.T
        engs = ["dve", "act"]
        k = [0]

        def cast(src, dst, R, C, gcol0, sc0=0):
            for r in range(R // 128):
                for c0 in range(0, C, 1024):
                    a = s.f4k.get(); b = s.b2k.get()
                    T.dma("sp", a[:], src[r * 128:(r + 1) * 128, sc0 + c0:sc0 + c0 + 1024], writes=[a])
                    eng = engs[k[0] % 2]; k[0] += 1
                    if gcol0 is None:
                        s.CP(b[:], a[:], [a], [b], eng=eng)
                    elif eng == "act":
                        s.ACT(b[:], a[:], AF.Copy, [a, s.gcols], [b], scale=s.gcols[:, gcol0 + r:gcol0 + r + 1])
                    else:
                        s.TS(b[:], a[:], s.gcols[:, gcol0 + r:gcol0 + r + 1], None, ALU.mult, None,
                             [a, s.gcols], [b], eng=eng)
                    T.dma("pool", dst[r * 128:(r + 1) * 128, c0:c0 + 1024], b[:], reads=[b], writes=[dst])
                    yield

        for blk in (5, 1, 2, 3, 4, 0, 6, 7, 8, 9):
            yield from cast(s.w_in, s.wb_in[blk], D, 1024, 0, sc0=blk * 1024)
        yield from cast(s.w_memkv, s.wb_memkv, D, 1024, 16)
        for i in range(3):
            yield from cast(s.w_br[i], s.wb_br[i], 512, D, None)
        yield from cast(s.w_out, s.wb_out, D, D, None)
        yield from cast(s.w_a, s.wb_a, D, DFF, 8)
        yield from cast(s.w_b, s.wb_b, D, DFF, 8)
        yield from cast(s.w_d, s.wb_d, DFF, D, None)

    def run_casts(s, n=None):
        i = 0
        for _ in s.castg:
            i += 1
            if n is not None and i >= n:
                break

    _wq = 0

    def wq(s):
        s._wq += 1
        return "sp" if (s._wq % 2 or not s.USE_POOLQ) else "pool"

    USE_POOLQ = True

    def wload(s, src, r0, nk, c0, ncols, bias_c0=None):
        T = s.T
        q = s.wq()
        if isinstance(src, list):
            src = src[c0 // 1024]
            c0 = c0 % 1024
        w = s.wpool.get()
        T.dma(q, w[:, 0:nk, 0:ncols],
              src[r0:r0 + nk * 128, c0:c0 + ncols].rearrange("(k p) c -> p k c", p=128),
              reads=[src], writes=[w])
        if bias_c0 is not None:
            T.dma(q, w[0:1, 8, 0:ncols], s.bb_in[0:1, bias_c0:bias_c0 + ncols], reads=[s.bb_in], writes=[w])
        return w

    def rstd_of(s, src, srcbufs, n, Dn):
        junk = s.b2k.get(); ss = s.sm.get()
        s.ACT(junk[0:n, 0:Dn], src, AF.Square, srcbufs, [junk, ss], accum_out=ss[0:n, 0:1])
        s.TS(ss[0:n, 0:1], ss[0:n, 0:1], 1.0 / Dn, EPS, ALU.mult, ALU.add, [ss], [ss])
        s.ACT(ss[0:n, 0:1], ss[0:n, 0:1], AF.Sqrt, [ss], [ss])
        s.T.op("dve", lambda e: e.reciprocal(out=ss[0:n, 0:1], in_=ss[0:n, 0:1]), [ss], [ss])
        return ss

    def transposes(s, src, n, nch, dstT, col0, eng="act"):
        ps = s.psB.get()
        pb = ps.t[:].bitcast(BF16)
        for c in range(nch):
            s.TR(pb[:, c * 128:c * 128 + n], src[0:n, c * 128:(c + 1) * 128], s.idb[0:n, 0:n], [src, s.idb], [ps])
        s.CP(dstT[:, 0:nch, col0:col0 + n],
             pb[:, 0:nch * 128].rearrange("p (c t) -> p c t", t=128)[:, :, 0:n], [ps], [dstT], eng=eng)

    def front(s, xsrc, row0, n, hT, col0, xt=None):
        T = s.T
        if xt is None:
            xt = s.xpool.get()
        T.dma("sp", xt[0:n, :], xsrc[row0:row0 + n, :], writes=[xt])
        ss = s.rstd_of(xt[0:n, :], [xt], n, D)
        xn = s.b2k.get()
        s.TS(xn[0:n, :], xt[0:n, :], ss[0:n, 0:1], None, ALU.mult, None, [xt, ss], [xn])
        s.transposes(xn, n, KC, hT, col0)
        return xt

    def proj_fm(s, w, nk, wc0, actT, ntok):
        ps = s.psA.get()
        for kc in range(nk):
            s.MM(ps[:, 0:ntok], w[:, kc, wc0:wc0 + 128], actT[:, kc, 0:ntok], kc == 0, kc == nk - 1, [w, actT], [ps],
                 inc=(kc == nk - 1))
        return ps

    def proj_tm(s, w, nk, ncols, actT, col0, n, bias, ps=None, pc0=0, wc0=0):
        if ps is None:
            ps = s.psA.get()
        for kc in range(nk):
            s.MM(ps[0:n, pc0:pc0 + ncols], actT[:, kc, col0:col0 + n], w[:, kc, wc0:wc0 + ncols],
                 kc == 0, (kc == nk - 1) and not bias, [w, actT], [ps], inc=((kc == nk - 1) and not bias))
        if bias:
            s.MM(ps[0:n, pc0:pc0 + ncols], s.ones_bf[0:1, 0:n], w[0:1, 8, wc0:wc0 + ncols], False, True,
                 [w, s.ones_bf], [ps])
        return ps

    def hgrn_tile(s, col0, n, vm, hf_ps, hi_ps, hg_ps, full, need_bf):
        h4 = lambda ap: ap.rearrange("p (h t) -> p h t", h=4)
        sg = s.f2k.get()
        s.ACT(sg[0:n, :], hf_ps[0:n, 0:512], AF.Sigmoid, [hf_ps], [sg])
        k = s.f2k.get()
        s.TT(k[0:n, :], sg[0:n, :], s.noml_tm[0:n, :], ALU.mult, [sg, s.noml_tm], [k])
        s.TT(k[0:n, :], k[0:n, :], s.oml_tm[0:n, :], ALU.add, [k, s.oml_tm], [k])
        g = s.f2k.get()
        s.ACT(g[0:n, :], k[0:n, :], AF.Ln, [k], [g], scale=-1.0, bias=1.0)
        R = s.psB.get()
        s.MM(R[0:n, 0:512], s.tri_up[0:n, 0:n], g[0:n, :], True, True, [s.tri_up, g], [R])
        eR = s.f2k.get()
        s.ACT(eR[0:n, :], R[0:n, 0:512], AF.Exp, [R], [eR])
        kend = s.b1k.get()
        s.TT(kend[0:n, :], k[0:n, :], eR[0:n, :], ALU.mult, [k, eR], [kend])
        v = s.hv
        s.TS(v[0:n, :], hi_ps[0:n, 0:512], vm[0:n, 0:1], None, ALU.mult, None, [hi_ps, vm], [v])
        ge = s.psB.get()
        for h in range(4):
            s.MM(ge[:, h:h + 1], g[0:n, h * 128:(h + 1) * 128], s.ones_col[0:n, 0:1], True, True,
                 [g, s.ones_col], [ge])
        eg = s.sm.get()
        s.ACT(eg[:, 0:4], ge[:, 0:4], AF.Exp, [ge], [eg])
        st = s.psB.get()
        for h in range(4):
            hs = slice(h * 128, (h + 1) * 128)
            s.MM(st[:, hs], kend[0:n, hs], v[0:n, hs], True, True, [kend, v], [st])
        for h in range(4):
            hs = slice(h * 128, (h + 1) * 128)
            s.STT(s.S[:, hs], s.S[:, hs], eg[:, h:h + 1], st[:, hs], ALU.mult, ALU.add, [s.S, eg, st], [s.S])
        if full:
            assert n == 128
            GT = s.psB.get()
            for h in range(4):
                s.MM(GT[:, h * 128:(h + 1) * 128], g[:, h * 128:(h + 1) * 128], s.tri_incl[:], True, True,
                     [g, s.tri_incl], [GT])
            Gx = s.Gext.get()
            s.CP(Gx[:, :, 1:129], h4(GT[:, 0:512]), [GT], [Gx], eng="act")
            eG = s.f2k.get()
            s.ACT(h4(eG[:]), Gx[:, :, 1:129], AF.Exp, [Gx], [eG])
            qhat = s.hqhat
            s.TT(h4(qhat[:]), s.qfm[:, :, col0:col0 + 128], h4(eG[:]), ALU.mult, [s.qfm, eG], [qhat])
            dq = s.f2k.get()
            v5 = lambda ap: ap.rearrange("p h (m j) -> p h m j", j=16)
            s.TT(v5(h4(dq[:])), v5(Gx[:, :, 1:129]),
                 v5(Gx[:, :, 0:128])[:, :, :, 0:1].to_broadcast([128, 4, 8, 16]), ALU.subtract, [Gx], [dq])
            s.ACT(dq[:], dq[:], AF.Exp, [dq], [dq])
            qtil = s.hqtil
            s.TT(h4(qtil[:]), s.qfm[:, :, col0:col0 + 128], h4(dq[:]), ALU.mult, [s.qfm, dq], [qtil])
            sc = s.psB.get()
            for m in range(8):
                dk = s.f2k.get()
                s.STT(h4(dk[:]), Gx[:, :, 1:129], -1.0,
                      Gx[:, :, 16 * m:16 * m + 1].to_broadcast([128, 4, 128]), ALU.mult, ALU.add, [Gx], [dk])
                s.ACT(dk[:], dk[:], AF.Exp, [dk], [dk])
                kt = s.b1k.get()
                s.STT(h4(kt[:]), h4(dk[:]), 1e30, s.kTfm[:, :, col0:col0 + 128], ALU.min, ALU.mult,
                      [dk, s.kTfm], [kt])
                for h in range(4):
                    c = h * 128 + 16 * m
                    s.MM(sc[:, c:c + 16], kt[:, h * 128:(h + 1) * 128], qtil[:, c:c + 16], True, True,
                         [kt, qtil], [sc])
            scT = s.b1k.get()
            s.TT(h4(scT[:]), h4(sc[:, 0:512]), s.causal[:].unsqueeze(1).to_broadcast([128, 4, 128]), ALU.mult,
                 [sc, s.causal], [scT])
            s.dbg("Gx", Gx[:].rearrange("p h t -> p (h t)"), [Gx], [128, 516])
            s.dbg("qhat", qhat[:], [qhat], [128, 512], BF16)
            s.dbg("qtil", qtil[:], [qtil], [128, 512], BF16)
            s.dbg("scT", scT[:], [scT], [128, 512], BF16)
            s.dbg("Sbf", s.Sbf[:], [s.Sbf], [128, 512], BF16)
            s.dbg("S", s.S[:], [s.S], [128, 512])
            s.dbg("v", v[:], [v], [128, 512], BF16)
            s.dbg("kTfm", s.kTfm[:, :, col0:col0 + 128], [s.kTfm], [128, 4, 128])
            s.dbg("qfm", s.qfm[:, :, col0:col0 + 128], [s.qfm], [128, 4, 128])
            if s.DBG and s.DBG.get("stop") == "hgrn":
                raise StopIteration
            o = s.psA.get()
            for h in range(4):
                hs = slice(h * 128, (h + 1) * 128)
                s.MM(o[:, hs], scT[:, hs], v[:, hs], True, False, [scT, v], [o], inc=True)
                s.MM(o[:, hs], qhat[:, hs], s.Sbf[:, hs], False, True, [qhat, s.Sbf], [o])
            o2 = s.f2k.get()
            s.ACT(o2[:], o[:, 0:512], AF.Square, [o], [o2])
            ssq = s.sm.get()
            s.T.op("dve", lambda e: e.tensor_reduce(out=ssq[:, 0:4], in_=h4(o2[:]), axis=AX.X, op=ALU.add),
                   [o2], [ssq])
            s.TS(ssq[:, 0:4], ssq[:, 0:4], 1.0 / 128, EPS, ALU.mult, ALU.add, [ssq], [ssq])
            s.ACT(ssq[:, 0:4], ssq[:, 0:4], AF.Sqrt, [ssq], [ssq])
            s.T.op("dve", lambda e: e.reciprocal(out=ssq[:, 0:4], in_=ssq[:, 0:4]), [ssq], [ssq])
            sgt = s.f2k.get()
            s.ACT(sgt[:], hg_ps[:, 0:512], AF.Sigmoid, [hg_ps], [sgt])
            s.TT(sgt[:], sgt[:], s.ghg4[:], ALU.mult, [sgt, s.ghg4], [sgt])
            on = s.f2k.get()
            s.TT(h4(on[:]), h4(o[:, 0:512]), ssq[:, 0:4].unsqueeze(2).to_broadcast([128, 4, 128]), ALU.mult,
                 [o, ssq], [on])
            hg = s.b1k.get()
            s.TT(hg[:], on[:], sgt[:], ALU.mult, [on, sgt], [hg])
            s.transposes(hg, 128, 4, s.hgT, col0, eng="dve")
        if need_bf:
            s.CP(s.Sbf[:], s.S[:], [s.S], [s.Sbf], eng="act")

    def hgrn_state_multi(s, nt_, vms, hf_ps, hi_ps, need_bf):
        Wd = 512 * nt_
        sg = s.f4k.get(); k = s.f4k.get(); g = s.f4k.get()
        for i in range(nt_):
            s.ACT(sg[:, i * 512:(i + 1) * 512], hf_ps[i][:, 0:512], AF.Sigmoid, [hf_ps[i]], [sg])
        v3 = lambda ap: ap.rearrange("p (i c) -> p i c", c=512)
        bc = lambda b: b[:].unsqueeze(1).to_broadcast([128, nt_, 512])
        s.TT(v3(k[:, 0:Wd]), v3(sg[:, 0:Wd]), bc(s.noml_tm), ALU.mult, [sg, s.noml_tm], [k])
        s.TT(v3(k[:, 0:Wd]), v3(k[:, 0:Wd]), bc(s.oml_tm), ALU.add, [k, s.oml_tm], [k])
        s.ACT(g[:, 0:Wd], k[:, 0:Wd], AF.Ln, [k], [g], scale=-1.0, bias=1.0)
        for i in range(nt_):
            R = s.psB.get()
            s.MM(R[:, 0:512], s.tri_up[:], g[:, i * 512:(i + 1) * 512], True, True, [s.tri_up, g], [R])
            s.ACT(sg[:, i * 512:(i + 1) * 512], R[:, 0:512], AF.Exp, [R], [sg])
        kend = s.b2k.get(); v = s.b2k.get()
        s.TT(kend[:, 0:Wd], k[:, 0:Wd], sg[:, 0:Wd], ALU.mult, [k, sg], [kend])
        for i in range(nt_):
            s.TS(v[:, i * 512:(i + 1) * 512], hi_ps[i][:, 0:512], vms[i][:, 0:1], None, ALU.mult, None,
                 [hi_ps[i], vms[i]], [v])
        ge = s.psB.get()
        for i in range(nt_):
            for h in range(4):
                c = i * 512 + h * 128
                s.MM(ge[:, i * 4 + h:i * 4 + h + 1], g[:, c:c + 128], s.ones_col[:, 0:1], True, True,
                     [g, s.ones_col], [ge])
        eg = s.sm.get()
        s.ACT(eg[:, 0:4 * nt_], ge[:, 0:4 * nt_], AF.Exp, [ge], [eg])
        sts = []
        for i in range(nt_):
            st = s.psB.get()
            for h in range(4):
                c = i * 512 + h * 128
                s.MM(st[:, h * 128:(h + 1) * 128], kend[:, c:c + 128], v[:, c:c + 128], True, True, [kend, v], [st])
            sts.append(st)
        for i in range(nt_):
            for h in range(4):
                hs = slice(h * 128, (h + 1) * 128)
                s.STT(s.S[:, hs], s.S[:, hs], eg[:, i * 4 + h:i * 4 + h + 1], sts[i][:, hs], ALU.mult, ALU.add,
                      [s.S, eg, sts[i]], [s.S])
        if need_bf:
            s.CP(s.Sbf[:], s.S[:], [s.S], [s.Sbf], eng="act")

    def finalize_att(s, U, n, dstT, col0):
        ao = s.b1k.get()
        for h in range(4):
            rc = s.sm.get()
            s.TS(rc[0:n, 0:1], U[h][0:n, 128:129], 1e-30, None, ALU.add, None, [U[h]], [rc])
            s.T.op("dve", lambda e: e.reciprocal(out=rc[0:n, 0:1], in_=rc[0:n, 0:1]), [rc], [rc])
            s.TS(ao[0:n, h * 128:(h + 1) * 128], U[h][0:n, 0:128], rc[0:n, 0:1], None, ALU.mult, None,
                 [U[h], rc], [ao], eng=("dve" if h % 2 else "act") if False else "dve")
        s.transposes(ao, n, 4, dstT, col0)

    def attention_tile(s, t, col0, depth=3):
        T = s.T
        U = [s.psA.get() for _ in range(4)]
        blocks = [(g, d) for g in range(3) for d in range(NDEL[g]) if t - d >= s.first_g[g]]
        nb = len(blocks)
        pend = []

        def issue(bi):
            g, d = blocks[bi]
            ks = s.kslot.get(); vs = s.vslot.get()
            bidx = t - d - s.first_g[g]
            T.dma("sp", ks[:], s.Kd[g][bidx], reads=[s.Kd[g]], writes=[ks])
            T.dma("sp", vs[:], s.Vd[g][bidx], reads=[s.Vd[g]], writes=[vs])
            sp_ = s.psB.get()
            for h in range(4):
                hs = slice(h * 128, (h + 1) * 128)
                s.MM(sp_[:, hs], ks[:, hs], s.Qfm[:, g * 4 + h, col0:col0 + 128], True, True, [ks, s.Qfm], [sp_])
            pend.append((bi, g, d, sp_, vs))

        def retire():
            bi, g, d, sp_, vs = pend.pop(0)
            P = s.f2k.get()
            s.ACT(P[:], sp_[:, 0:512], AF.Exp, [sp_], [P], scale=SCALE)
            Pm = s.b1k.get()
            mo = (s.blk0[g] + d) * 512
            s.TT(Pm[:], P[:], s.Mtab[:, mo:mo + 512], ALU.mult, [P, s.Mtab], [Pm])
            for h in range(4):
                s.MM(U[h][:, 0:129], Pm[:, h * 128:(h + 1) * 128], vs[:, h * 130:h * 130 + 129],
                     bi == 0, bi == nb - 1, [Pm, vs], [U[h]])

        for bi in range(nb):
            issue(bi)
            if len(pend) > depth:
                retire()
        while pend:
            retire()
        s.finalize_att(U, 128, s.attT, col0)

    def mem_att_tile(s, col0, n=128):
        U = [s.psA.get() for _ in range(4)]
        for hp in range(2):
            sp_ = s.psB.get()
            for hh in range(2):
                h = 2 * hp + hh
                for blk in range(2):
                    c = (hh * 2 + blk) * 128
                    s.MM(sp_[:, c:c + n], s.memK[:, h, blk * 128:(blk + 1) * 128], s.MQfm[:, h, col0:col0 + n],
                         True, True, [s.memK, s.MQfm], [sp_])
            Pm = s.b1k.get()
            s.ACT(Pm[:].rearrange("p (a q) -> p a q", q=128)[:, :, 0:n],
                  sp_[:, 0:512].rearrange("p (a q) -> p a q", q=128)[:, :, 0:n], AF.Exp, [sp_], [Pm], scale=SCALE)
            for hh in range(2):
                h = 2 * hp + hh
                for blk in range(2):
                    c = (hh * 2 + blk) * 128
                    s.MM(U[h][0:n, 0:129], Pm[:, c:c + n], s.memV[:, blk, h * 130:h * 130 + 129],
                         blk == 0, blk == 1, [Pm, s.memV], [U[h]])
        s.finalize_att(U, n, s.memT, col0)

    def mem_kv(s):
        T = s.T
        hT = s.hT.get()
        for i in range(2):
            s.front(s.mem_in, i * 128, 128, hT, i * 128)
        s.stop_at("mk_front")
        wk = s.wload(s.wb_memkv, 0, 8, 0, 512)
        wv = s.wload(s.wb_memkv, 0, 8, 512, 512)
        for h in range(4):
            ps = s.proj_fm(wk, 8, h * 128, hT, 256)
            s.CP(s.memK[:, h, :], ps[:, 0:256], [ps], [s.memK], eng="act")
        s.stop_at("mk_fm")
        for i in range(2):
            st = s.stage.get()
            pk = s.proj_tm(wk, 8, 512, hT, i * 128, 128, False)
            s.CP(st[:, 0:512], pk[:, 0:512], [pk], [st], eng="act")
            s.stop_at("mk_a")
            pv = s.proj_tm(wv, 8, 512, hT, i * 128, 128, False)
            s.CP(st[:, 512:1024], pv[:, 0:512], [pv], [st], eng="dve")
            mv = s.memV[:, i, :].rearrange("p (h c) -> p h c", c=130)
            s.CP(mv[:, :, 0:128], pv[:, 0:512].rearrange("p (h c) -> p h c", c=128), [pv], [s.memV], eng="act")
            s.stop_at("mk_b")
            T.op("dve", lambda e: e.memset(mv[:, :, 128:130], 1.0), [], [s.memV])
            s.stop_at("mk_c")
            T.dma("pool", s.o_pmem[i * 128:(i + 1) * 128, :], st[:], reads=[st], writes=[s.o_pmem])

    def mixer_out_multi(s, hT, tl, h2T):
        nt_ = len(tl)
        merged = [s.f4k.get() for _ in tl]
        branches = [(s.attT, C_GA, 0), (s.hgT, C_GH, 1), (s.memT, C_GM, 2)]
        for bi, (bT, cg, wi) in enumerate(branches):
            for half in range(2):
                hs = slice(half * 512, (half + 1) * 512)
                wg = s.wload(s.wb_in, 0, 8, cg + half * 512, 512, bias_c0=cg + half * 512)
                wb = s.wload(s.wb_br[wi], 0, 4, half * 512, 512)
                gp = [s.proj_tm(wg, 8, 512, hT, c0, n, True) for (c0, n, _) in tl]
                bp = [s.proj_tm(wb, 4, 512, bT, c0, n, False) for (c0, n, _) in tl]
                for i, (c0, n, _) in enumerate(tl):
                    sg = s.f2k.get()
                    s.ACT(sg[0:n, :], gp[i][0:n, 0:512], AF.Sigmoid, [gp[i]], [sg])
                    if bi == 0:
                        s.TT(merged[i][0:n, hs], sg[0:n, :], bp[i][0:n, 0:512], ALU.mult, [sg, bp[i]], [merged[i]])
                    else:
                        s.TT(sg[0:n, :], sg[0:n, :], bp[i][0:n, 0:512], ALU.mult, [sg, bp[i]], [sg])
                        s.TT(merged[i][0:n, hs], merged[i][0:n, hs], sg[0:n, :], ALU.add, [merged[i], sg], [merged[i]])
        wo = [s.wload(s.wb_out, 0, 8, half * 512, 512) for half in range(2)]
        for i, (col0, n, xt) in enumerate(tl):
            mb = s.b2k.get()
            s.CP(mb[0:n, :], merged[i][0:n, :], [merged[i]], [mb], eng="act")
            mT = s.b2k.get()
            mT3 = mT[:].rearrange("p (c t) -> p c t", t=128)
            ps = s.psB.get()
            pb = ps.t[:].bitcast(BF16)
            for c in range(KC):
                s.TR(pb[:, c * 128:c * 128 + n], mb[0:n, c * 128:(c + 1) * 128], s.idb[0:n, 0:n], [mb, s.idb], [ps])
            s.CP(mT3[:, :, 0:n], pb[:, 0:1024].rearrange("p (c t) -> p c t", t=128)[:, :, 0:n], [ps], [mT], eng="dve")
            mix = [None, None]
            for half in range(2):
                ps2 = s.psA.get()
                for kc in range(KC):
                    s.MM(ps2[0:n, 0:512], mT3[:, kc, 0:n], wo[half][:, kc, 0:512], kc == 0, kc == KC - 1,
                         [mT, wo[half]], [ps2], inc=(kc == KC - 1))
                mix[half] = ps2
            mx = merged[i]
            s.CP(mx[0:n, 0:512], mix[0][0:n, 0:512], [mix[0]], [mx], eng="act")
            s.CP(mx[0:n, 512:1024], mix[1][0:n, 0:512], [mix[1]], [mx], eng="dve")
            ss = s.rstd_of(mx[0:n, :], [mx], n, D)
            s.STT(mx[0:n, :], mx[0:n, :], ss[0:n, 0:1], s.gpost_mix[0:n, :], ALU.mult, ALU.mult,
                  [mx, ss, s.gpost_mix], [mx])
            s.TT(xt[0:n, :], xt[0:n, :], mx[0:n, :], ALU.add, [xt, mx], [xt])
            ss2 = s.rstd_of(xt[0:n, :], [xt], n, D)
            h2 = s.b2k.get()
            s.TS(h2[0:n, :], xt[0:n, :], ss2[0:n, 0:1], None, ALU.mult, None, [xt, ss2], [h2])
            s.transposes(h2, n, KC, h2T, col0)

    def ffn_group(s, h2T, ntok, xts, out_rows, carry_in, carry_out, tok_view=None):
        T = s.T
        nt = (ntok + 127) // 128
        Y = [s.psA.get() for _ in range(2 * nt)]
        pend = []

        def down(c, cc, pab, wd):
            pa = pab;
            ae = s.aext.get()
            w0 = s.ccol[:, c:c + 1]; w1 = s.ccol[:, 32 + c:33 + c]; w2 = s.ccol[:, 64 + c:65 + c]
            cb = s.ccol[:, 96 + c:97 + c]
            cv = s.f2k.get()
            if tok_view is None:
                s.CP(ae[:, 0:2], carry_in(c), [s.carry], [ae], eng="dve")
                s.CP(ae[:, 2:2 + ntok], pa[:, 0:ntok], [pa], [ae], eng="act")
                s.CP(carry_out(c), ae[:, ntok:ntok + 2], [ae], [s.carry], eng="dve")
                a0 = ae[:, 0:ntok]; a1 = ae[:, 1:1 + ntok]; a2 = ae[:, 2:2 + ntok]; co = cv[:, 0:ntok]
            else:
                nsq, TT_ = tok_view
                av = ae[:, 0:nsq * (TT_ + 2)].rearrange("p (b j) -> p b j", j=TT_ + 2)
                s.CP(av[:, :, 0:2], carry_in(c), [s.carry_s], [ae], eng="dve")
                s.CP(av[:, :, 2:2 + TT_], pa[:, 0:ntok].rearrange("p (b j) -> p b j", j=TT_), [pa], [ae], eng="act")
                a0 = av[:, :, 0:TT_]; a1 = av[:, :, 1:1 + TT_]; a2 = av[:, :, 2:2 + TT_]
                co = cv[:, 0:ntok].rearrange("p (b j) -> p b j", j=TT_)
            s.TS(co, a2, w2, cb, ALU.mult, ALU.add, [ae, s.ccol], [cv])
            s.STT(co, a1, w1, co, ALU.mult, ALU.add, [ae, s.ccol, cv], [cv])
            s.STT(co, a0, w0, co, ALU.mult, ALU.add, [ae, s.ccol, cv], [cv])
            sl = s.f2k.get()
            s.ACT(sl[:, 0:ntok], cv[:, 0:ntok], AF.Silu, [cv], [sl])
            gt = s.b1k.get()
            s.TT(gt[:, 0:ntok], sl[:, 0:ntok], pab[:, 256:256 + ntok], ALU.mult, [sl, pab], [gt])
            for ti in range(nt):
                n = min(128, ntok - ti * 128)
                for half in range(2):
                    s.MM(Y[2 * ti + half][0:n, 0:512], gt[:, ti * 128:ti * 128 + n], wd[:, cc * 2 + half, :],
                         c == 0, c == 31, [gt, wd], [Y[2 * ti + half]])

        for c2 in range(16):
            wab = s.wpool.get()
            q = s.wq()
            T.dma(q, wab[:, 0:8, 0:256], s.wb_a[:, c2 * 256:(c2 + 1) * 256].rearrange("(k p) c -> p k c", p=128),
                  reads=[s.wb_a], writes=[wab])
            T.dma(q, wab[:, 0:8, 256:512], s.wb_b[:, c2 * 256:(c2 + 1) * 256].rearrange("(k p) c -> p k c", p=128),
                  reads=[s.wb_b], writes=[wab])
            wd = s.wpool.get()
            T.dma(s.wq(), wd[:, 0:4, :].rearrange("p (k h) c -> p k h c", h=2),
                  s.wb_d[c2 * 256:(c2 + 1) * 256, :].rearrange("(k p) (h c) -> p k h c", p=128, h=2),
                  reads=[s.wb_d], writes=[wd])
            for cc in range(2):
                c = c2 * 2 + cc
                pab = s.psB.get()
                for kc in range(KC):
                    s.MM(pab[:, 0:ntok], wab[:, kc, cc * 128:(cc + 1) * 128], h2T[:, kc, 0:ntok], kc == 0, kc == KC - 1,
                         [wab, h2T], [pab], inc=(kc == KC - 1))
                for kc in range(KC):
                    s.MM(pab[:, 256:256 + ntok], wab[:, kc, 256 + cc * 128:256 + (cc + 1) * 128], h2T[:, kc, 0:ntok],
                         kc == 0, kc == KC - 1, [wab, h2T], [pab], inc=(kc == KC - 1))
                pend.append((c, cc, pab, wd))
                if len(pend) > 2:
                    down(*pend.pop(0))
        while pend:
            down(*pend.pop(0))
        for ti in range(nt):
            n = min(128, ntok - ti * 128)
            yt = s.f4k.get()
            s.CP(yt[0:n, 0:512], Y[2 * ti][0:n, 0:512], [Y[2 * ti]], [yt], eng="act")
            s.CP(yt[0:n, 512:1024], Y[2 * ti + 1][0:n, 0:512], [Y[2 * ti + 1]], [yt], eng="dve")
            ss = s.rstd_of(yt[0:n, :], [yt], n, D)
            s.STT(yt[0:n, :], yt[0:n, :], ss[0:n, 0:1], s.gpost_ffn[0:n, :], ALU.mult, ALU.mult,
                  [yt, ss, s.gpost_ffn], [yt])
            xt = xts[ti]
            s.TT(yt[0:n, :], yt[0:n, :], xt[0:n, :], ALU.add, [yt, xt], [yt])
            dst, r0 = out_rows(ti)
            if dst is not None:
                T.dma("pool", dst[r0:r0 + n, :], yt[0:n, :], reads=[yt], writes=[dst])

    def a_rows(s, h2T, cols, n, dstbuf, rowsel):
        T = s.T
        for c4 in range(8):
            wa = s.wload(s.wb_a, 0, 8, c4 * 512, 512)
            ps = s.psA.get()
            for kc in range(KC):
                s.MM(ps[0:n, 0:512], h2T[:, kc, cols], wa[:, kc, 0:512], kc == 0, kc == KC - 1, [h2T, wa], [ps],
                     inc=(kc == KC - 1))
            st = s.f2k.get()
            s.CP(st[0:n, :], ps[0:n, 0:512], [ps], [st], eng="act")
            for (d0, s0, nr) in rowsel:
                T.dma("pool", dstbuf[d0:d0 + nr, c4 * 512:(c4 + 1) * 512], st[s0:s0 + nr, :], reads=[st], writes=[dstbuf])

    def a_carry(s, h2T, c0, scale_col):
        for c4 in range(8):
            wa = s.wload(s.wb_a, 0, 8, c4 * 512, 512)
            ps = s.psB.get()
            for cc in range(4):
                for kc in range(KC):
                    s.MM(ps[:, 2 * cc:2 * cc + 2], wa[:, kc, cc * 128:(cc + 1) * 128], h2T[:, kc, c0:c0 + 2],
                         kc == 0, kc == KC - 1, [wa, h2T], [ps], inc=(kc == KC - 1))
            s.TS(s.carry[:, c4 * 4:(c4 + 1) * 4, :], ps[:, 0:8].rearrange("p (c j) -> p c j", j=2), scale_col, None,
                 ALU.mult, None, [ps, s.flg], [s.carry])

    def prompt(s):
        T = s.T
        T.op("pool", lambda e: e.memset(s.S[:], 0.0), [], [s.S])
        T.op("pool", lambda e: e.memset(s.Sbf[:], 0.0), [], [s.Sbf])
        for gx in s.Gext.bufs:
            T.op("pool", lambda e: e.memset(gx[:], 0.0), [], [gx])
        kinds = ["pre"] * s.npre + ["halo"] * s.nhalo + ["ovl"] + ["main"] * s.nmain
        sts = []
        t = 0
        while t < s.NT:
            k = kinds[t]
            n = 1 if k == "ovl" else s.stn
            grp = [u for u in range(t, min(t + n, s.NT)) if kinds[u] == k]
            sts.append((grp, k))
            t += len(grp)
        nlight = sum(1 for _, k in sts if k in ("pre", "halo"))
        per = (204 + max(nlight, 1) - 1) // max(nlight, 1) + 1
        done_mem = False
        for si, (grp, k) in enumerate(sts):
            if k in ("pre", "halo"):
                s.run_casts(per)
            elif not done_mem:
                s.run_casts(None)
                s.mem_kv()
                done_mem = True
            s.do_st(grp, k)
            s.stop_at(f"st{si}")
        T.dma("pool", s.o_phg[:].rearrange("h k v -> k h v"), s.S[:].rearrange("p (h v) -> p h v", h=4),
              reads=[s.S], writes=[s.o_phg])

    def do_st(s, tiles, kind):
        T = s.T
        nt_ = len(tiles)
        NTOK = nt_ * 128
        mainlike = kind in ("ovl", "main")
        hT = s.hT.get()
        xts, vms = [], []
        for i, t in enumerate(tiles):
            vm = s.vmpool.get()
            T.dma("sp", vm[:, 0:1], s.vmask[t * 128:(t + 1) * 128, :], writes=[vm])
            vms.append(vm)
            xts.append(s.front(s.x_ext, t * 128, 128, hT, i * 128))
        if mainlike:
            w = s.wload(s.wb_in, 0, 8, C_HQ, 512)
            for c in range(4):
                ps = s.proj_fm(w, 8, c * 128, hT, NTOK)
                s.ACT(s.qfm[:, c, 0:NTOK], ps[:, 0:NTOK], AF.Identity, [ps, s.bcol], [s.qfm],
                      bias=s.bcol[:, C_HQ // 128 + c:C_HQ // 128 + c + 1])
            w = s.wload(s.wb_in, 0, 8, C_HF, 512)
            for c in range(4):
                ps = s.proj_fm(w, 8, c * 128, hT, NTOK)
                sg = s.f2k.get()
                s.ACT(sg[:, 0:NTOK], ps[:, 0:NTOK], AF.Sigmoid, [ps, s.bcol], [sg],
                      bias=s.bcol[:, C_HF // 128 + c:C_HF // 128 + c + 1])
                s.TS(s.kTfm[:, c, 0:NTOK], sg[:, 0:NTOK], s.lbcol[:, 8 + c:9 + c], s.lbcol[:, 4 + c:5 + c],
                     ALU.mult, ALU.add, [sg, s.lbcol], [s.kTfm])
        whf = s.wload(s.wb_in, 0, 8, C_HF, 512, bias_c0=C_HF)
        whi = s.wload(s.wb_in, 0, 8, C_HI, 512, bias_c0=C_HI)
        whg = s.wload(s.wb_in, 0, 8, C_HG, 512, bias_c0=C_HG) if mainlike else None
        if not mainlike:
            hfs = [s.proj_tm(whf, 8, 512, hT, i * 128, 128, True) for i in range(nt_)]
            his = [s.proj_tm(whi, 8, 512, hT, i * 128, 128, True) for i in range(nt_)]
            s.hgrn_state_multi(nt_, vms, hfs, his, need_bf=(tiles[-1] >= s.T0 - 1))
        for i, t in enumerate(tiles):
            if not mainlike:
                break
            hf_ps = s.proj_tm(whf, 8, 512, hT, i * 128, 128, True)
            hi_ps = s.proj_tm(whi, 8, 512, hT, i * 128, 128, True)
            hg_ps = s.proj_tm(whg, 8, 512, hT, i * 128, 128, True) if mainlike else None
            s.hgrn_tile(i * 128, 128, vms[i], hf_ps, hi_ps, hg_ps, mainlike, need_bf=(t >= s.T0 - 1))
        for g in range(3):
            need = [t >= s.first_g[g] for t in tiles]
            if not any(need):
                continue
            nout = min(W_G[g] // 128, s.nmain)
            wk = s.wload(s.wb_in, 0, 8, C_K + g * 512, 512, bias_c0=C_K + g * 512)
            wv = s.wload(s.wb_in, 0, 8, C_V + g * 512, 512, bias_c0=C_V + g * 512)
            kb = [s.b1k.get() for _ in tiles]
            for h in range(4):
                ps = s.proj_fm(wk, 8, h * 128, hT, NTOK)
                for i, t in enumerate(tiles):
                    if need[i]:
                        s.ACT(kb[i][:, h * 128:(h + 1) * 128], ps[:, i * 128:(i + 1) * 128], AF.Identity,
                              [ps, s.bcol], [kb[i]],
                              bias=s.bcol[:, C_K // 128 + g * 4 + h:C_K // 128 + g * 4 + h + 1])
            for i, t in enumerate(tiles):
                if not need[i]:
                    continue
                bidx = t - s.first_g[g]
                T.dma("pool", s.Kd[g][bidx], kb[i][:], reads=[kb[i]], writes=[s.Kd[g]])
                pv = s.proj_tm(wv, 8, 512, hT, i * 128, 128, True)
                vb = s.vslot.get()
                v3 = vb[:].rearrange("p (h c) -> p h c", c=130)
                s.TS(v3[:, :, 0:128], pv[:, 0:512].rearrange("p (h c) -> p h c", c=128), vms[i][:, 0:1], None,
                     ALU.mult, None, [pv, vms[i]], [vb])
                s.CP(v3[:, :, 128:130], vms[i][:, 0:1].unsqueeze(1).to_broadcast([128, 4, 2]), [vms[i]], [vb],
                     eng="dve")
                T.dma("pool", s.Vd[g][bidx], vb[:], reads=[vb], writes=[s.Vd[g]])
                mi = t - s.T0 - 1
                if kind == "main" and mi >= s.nmain - nout:
                    st = s.stage.get()
                    s.CP(st[:, 512:1024], pv[:, 0:512], [pv], [st], eng="act")
                    pk = s.proj_tm(wk, 8, 512, hT, i * 128, 128, True)
                    s.CP(st[:, 0:512], pk[:, 0:512], [pk], [st], eng="act")
                    r0 = (mi - (s.nmain - nout)) * 128
                    T.dma("pool", s.o_pwin[g][r0:r0 + 128, :], st[:], reads=[st], writes=[s.o_pwin[g]])
        if not mainlike:
            return
        for g in range(3):
            wq = s.wload(s.wb_in, 0, 8, C_Q + g * 512, 512)
            for h in range(4):
                ps = s.proj_fm(wq, 8, h * 128, hT, NTOK)
                s.ACT(s.Qfm[:, g * 4 + h, 0:NTOK], ps[:, 0:NTOK], AF.Identity, [ps, s.bcol], [s.Qfm],
                      bias=s.bcol[:, g * 4 + h:g * 4 + h + 1])
        wq = s.wload(s.wb_in, 0, 8, C_MQ, 512)
        for h in range(4):
            ps = s.proj_fm(wq, 8, h * 128, hT, NTOK)
            s.ACT(s.MQfm[:, h, 0:NTOK], ps[:, 0:NTOK], AF.Identity, [ps, s.bcol], [s.MQfm],
                  bias=s.bcol[:, C_MQ // 128 + h:C_MQ // 128 + h + 1])
        h2T = s.h2T.get()
        for i, t in enumerate(tiles):
            s.attention_tile(t, i * 128)
            s.mem_att_tile(i * 128)
        s.mixer_out_multi(hT, [(i * 128, 128, xts[i]) for i in range(nt_)], h2T)
        if kind == "ovl":
            s.a_carry(h2T, 126, s.flg[:, 0:1])
            return
        m0 = tiles[0] - s.T0 - 1

        def out_rows(ti):
            return s.o_y, (m0 + ti) * 128

        s.ffn_group(h2T, NTOK, xts, out_rows, lambda c: s.carry[:, c, :], lambda c: s.carry[:, c, :])
        if tiles[-1] == s.NT - 1:
            s.a_rows(h2T, slice(NTOK - 2, NTOK), 2, s.o_pconv, [(0, 0, 2)])

    def cache_block(s, src_ap, srcbuf):
        T = s.T
        ct = s.stage.get()
        T.dma("sp", ct[:], src_ap, reads=[srcbuf], writes=[ct])
        kbf = s.b1k.get()
        s.CP(kbf[:], ct[:, 0:512], [ct], [kbf], eng="dve")
        ps = s.psB.get()
        pb = ps.t[:].bitcast(BF16)
        for h in range(4):
            s.TR(pb[:, h * 128:(h + 1) * 128], kbf[:, h * 128:(h + 1) * 128], s.idb[:], [kbf, s.idb], [ps])
        ks = s.kslot.get()
        s.CP(ks[:], pb[:, 0:512], [ps], [ks], eng="act")
        vs = s.vslot.get()
        v3 = vs[:].rearrange("p (h c) -> p h c", c=130)
        s.CP(v3[:, :, 0:128], ct[:, 512:1024].rearrange("p (h c) -> p h c", c=128), [ct], [vs], eng="dve")
        T.op("dve", lambda e: e.memset(v3[:, :, 128:130], 1.0), [], [vs])
        return ks, vs

    def sample(s):
        T = s.T
        n = 32
        h4 = lambda ap: ap.rearrange("p (h t) -> p h t", h=4)
        hT = s.hT.get()
        xt = s.front(s.xs_in, 0, n, hT, 0)
        for g in range(3):
            for b in range(4):
                T.dma("pool", s.o_swin[g][b, 0:W_G[g] - 8, :], s.cwin[g][b, 8:W_G[g], :], reads=[s.cwin[g]],
                      writes=[s.o_swin[g]])
        w = s.wload(s.wb_in, 0, 8, C_HQ, 512)
        for c in range(4):
            ps = s.proj_fm(w, 8, c * 128, hT, n)
            s.ACT(s.qfm[:, c, 0:n], ps[:, 0:n], AF.Identity, [ps, s.bcol], [s.qfm],
                  bias=s.bcol[:, C_HQ // 128 + c:C_HQ // 128 + c + 1])
        w = s.wload(s.wb_in, 0, 8, C_HF, 512)
        for c in range(4):
            ps = s.proj_fm(w, 8, c * 128, hT, n)
            sg = s.f2k.get()
            s.ACT(sg[:, 0:n], ps[:, 0:n], AF.Sigmoid, [ps, s.bcol], [sg],
                  bias=s.bcol[:, C_HF // 128 + c:C_HF // 128 + c + 1])
            s.TS(s.kTfm[:, c, 0:n], sg[:, 0:n], s.lbcol[:, 8 + c:9 + c], s.lbcol[:, 4 + c:5 + c],
                 ALU.mult, ALU.add, [sg, s.lbcol], [s.kTfm])
        whf = s.wload(s.wb_in, 0, 8, C_HF, 512, bias_c0=C_HF)
        whi = s.wload(s.wb_in, 0, 8, C_HI, 512, bias_c0=C_HI)
        whg = s.wload(s.wb_in, 0, 8, C_HG, 512, bias_c0=C_HG)
        hf_ps = s.proj_tm(whf, 8, 512, hT, 0, n, True)
        hi_ps = s.proj_tm(whi, 8, 512, hT, 0, n, True)
        hg_ps = s.proj_tm(whg, 8, 512, hT, 0, n, True)
        sS = T.sb([128, 4, 512], F32, "sS"); sSb = T.sb([128, 4, 512], BF16, "sSb")
        for b in range(4):
            T.dma("sp", sS[:, b, :].rearrange("p (h v) -> p h v", h=4), s.shg_in[b].rearrange("h k v -> k h v"),
                  writes=[sS])
        s.CP(sSb[:], sS[:], [sS], [sSb], eng="act")
        sg = s.f2k.get()
        s.ACT(sg[0:n, :], hf_ps[0:n, 0:512], AF.Sigmoid, [hf_ps], [sg])
        k = s.f2k.get()
        s.TT(k[0:n, :], sg[0:n, :], s.noml_tm[0:n, :], ALU.mult, [sg, s.noml_tm], [k])
        s.TT(k[0:n, :], k[0:n, :], s.oml_tm[0:n, :], ALU.add, [k, s.oml_tm], [k])
        g_ = s.f2k.get()
        s.ACT(g_[0:n, :], k[0:n, :], AF.Ln, [k], [g_], scale=-1.0, bias=1.0)
        R = s.psB.get()
        s.MM(R[0:n, 0:512], s.tri_up_s[0:n, 0:n], g_[0:n, :], True, True, [s.tri_up_s, g_], [R])
        eR = s.f2k.get()
        s.ACT(eR[0:n, :], R[0:n, 0:512], AF.Exp, [R], [eR])
        kend = s.f2k.get()
        s.TT(kend[0:n, :], k[0:n, :], eR[0:n, :], ALU.mult, [k, eR], [kend])
        v = s.hv
        s.CP(v[0:n, :], hi_ps[0:n, 0:512], [hi_ps], [v], eng="act")
        ge = s.psB.get()
        for b in range(4):
            for h in range(4):
                s.MM(ge[:, b * 4 + h:b * 4 + h + 1], g_[0:n, h * 128:(h + 1) * 128], s.rowmask[0:n, b:b + 1],
                     True, True, [g_, s.rowmask], [ge])
        eg = s.sm.get()
        s.ACT(eg[:, 0:16], ge[:, 0:16], AF.Exp, [ge], [eg])
        for b in range(4):
            kb_ = s.b1k.get()
            s.TS(kb_[0:n, :], kend[0:n, :], s.rowmask[0:n, b:b + 1], None, ALU.mult, None, [kend, s.rowmask], [kb_])
            st = s.psB.get()
            for h in range(4):
                hs = slice(h * 128, (h + 1) * 128)
                s.MM(st[:, hs], kb_[0:n, hs], v[0:n, hs], True, True, [kb_, v], [st])
            for h in range(4):
                hs = slice(h * 128, (h + 1) * 128)
                s.STT(sS[:, b, hs], sS[:, b, hs], eg[:, b * 4 + h:b * 4 + h + 1], st[:, hs], ALU.mult, ALU.add,
                      [sS, eg, st], [sS])
            T.dma("pool", s.o_shg[b].rearrange("h k v -> k h v"), sS[:, b, :].rearrange("p (h v) -> p h v", h=4),
                  reads=[sS], writes=[s.o_shg])
        GT = s.psB.get()
        for h in range(4):
            s.MM(GT[:, h * 128:h * 128 + n], g_[0:n, h * 128:(h + 1) * 128], s.tri_incl_s[0:n, 0:n], True, True,
                 [g_, s.tri_incl_s], [GT])
        Gs = s.f2k.get()
        s.CP(h4(Gs[:])[:, :, 0:n], h4(GT[:, 0:512])[:, :, 0:n], [GT], [Gs], eng="act")
        eG = s.f2k.get()
        s.ACT(h4(eG[:])[:, :, 0:n], h4(Gs[:])[:, :, 0:n], AF.Exp, [Gs], [eG])
        qhat = s.hqhat
        s.TT(h4(qhat[:])[:, :, 0:n], s.qfm[:, :, 0:n], h4(eG[:])[:, :, 0:n], ALU.mult, [s.qfm, eG], [qhat])
        enG = s.f2k.get()
        s.ACT(h4(enG[:])[:, :, 0:n], h4(Gs[:])[:, :, 0:n], AF.Exp, [Gs], [enG], scale=-1.0)
        kt = s.hqtil
        s.TT(h4(kt[:])[:, :, 0:n], s.kTfm[:, :, 0:n], h4(enG[:])[:, :, 0:n], ALU.mult, [s.kTfm, enG], [kt])
        sc = s.psB.get()
        for h in range(4):
            s.MM(sc[0:n, h * 128:h * 128 + n], kt[:, h * 128:h * 128 + n], qhat[:, h * 128:h * 128 + n], True, True,
                 [kt, qhat], [sc])
        scT = s.b1k.get()
        s.TT(h4(scT[0:n, :])[:, :, 0:n], h4(sc[0:n, 0:512])[:, :, 0:n],
             s.causal_s[0:n, 0:n].unsqueeze(1).to_broadcast([n, 4, n]), ALU.mult, [sc, s.causal_s], [scT])
        qb = s.b1k.get()
        T.op("pool", lambda e: e.memset(qb[:], 0.0), [], [qb])
        qb4 = qb[:].rearrange("p (b h t) -> p b h t", b=4, h=4)
        for b in range(4):
            s.CP(qb4[:, b, :, 8 * b:8 * b + 8], h4(qhat[:])[:, :, 8 * b:8 * b + 8], [qhat], [qb], eng="dve")
        o = s.psA.get()
        for h in range(4):
            hs = slice(h * 128, (h + 1) * 128)
            s.MM(o[0:n, hs], scT[0:n, h * 128:h * 128 + n], v[0:n, hs], True, False, [scT, v], [o])
            for b in range(4):
                s.MM(o[0:n, hs], qb4[:, b, h, :], sSb[:, b, hs], False, b == 3, [qb, sSb], [o])
        o2 = s.f2k.get()
        s.ACT(o2[0:n, :], o[0:n, 0:512], AF.Square, [o], [o2])
        ssq = s.sm.get()
        T.op("dve", lambda e: e.tensor_reduce(out=ssq[0:n, 0:4], in_=h4(o2[0:n, :]), axis=AX.X, op=ALU.add),
             [o2], [ssq])
        s.TS(ssq[0:n, 0:4], ssq[0:n, 0:4], 1.0 / 128, EPS, ALU.mult, ALU.add, [ssq], [ssq])
        s.ACT(ssq[0:n, 0:4], ssq[0:n, 0:4], AF.Sqrt, [ssq], [ssq])
        T.op("dve", lambda e: e.reciprocal(out=ssq[0:n, 0:4], in_=ssq[0:n, 0:4]), [ssq], [ssq])
        sgt = s.f2k.get()
        s.ACT(sgt[0:n, :], hg_ps[0:n, 0:512], AF.Sigmoid, [hg_ps], [sgt])
        s.TT(sgt[0:n, :], sgt[0:n, :], s.ghg4[0:n, :], ALU.mult, [sgt, s.ghg4], [sgt])
        on = s.f2k.get()
        s.TT(h4(on[0:n, :]), h4(o[0:n, 0:512]), ssq[0:n, 0:4].unsqueeze(2).to_broadcast([n, 4, 128]), ALU.mult,
             [o, ssq], [on])
        hg = s.b1k.get()
        s.TT(hg[0:n, :], on[0:n, :], sgt[0:n, :], ALU.mult, [on, sgt], [hg])
        s.transposes(hg, n, 4, s.hgT, 0, eng="dve")
        Knew = s.Knew
        Vnew = T.sb([128, 3, 520], BF16, "Vnew")
        for g in range(3):
            wq = s.wload(s.wb_in, 0, 8, C_Q + g * 512, 512)
            for h in range(4):
                ps = s.proj_fm(wq, 8, h * 128, hT, n)
                s.ACT(s.Qfm[:, g * 4 + h, 0:n], ps[:, 0:n], AF.Identity, [ps, s.bcol], [s.Qfm],
                      bias=s.bcol[:, g * 4 + h:g * 4 + h + 1])
            wk = s.wload(s.wb_in, 0, 8, C_K + g * 512, 512, bias_c0=C_K + g * 512)
            wv = s.wload(s.wb_in, 0, 8, C_V + g * 512, 512, bias_c0=C_V + g * 512)
            for h in range(4):
                ps = s.proj_fm(wk, 8, h * 128, hT, n)
                s.ACT(Knew[:, (g * 4 + h) * 32:(g * 4 + h) * 32 + n], ps[:, 0:n], AF.Identity, [ps, s.bcol], [Knew],
                      bias=s.bcol[:, C_K // 128 + g * 4 + h:C_K // 128 + g * 4 + h + 1])
            pv = s.proj_tm(wv, 8, 512, hT, 0, n, True)
            v3 = Vnew[:, g, :].rearrange("p (h c) -> p h c", c=130)
            s.CP(v3[0:n, :, 0:128], pv[0:n, 0:512].rearrange("p (h c) -> p h c", c=128), [pv], [Vnew], eng="act")
            T.op("dve", lambda e: e.memset(v3[0:n, :, 128:130], 1.0), [], [Vnew])
            st = s.stage.get()
            s.CP(st[0:n, 512:1024], pv[0:n, 0:512], [pv], [st], eng="dve")
            pk = s.proj_tm(wk, 8, 512, hT, 0, n, True)
            s.CP(st[0:n, 0:512], pk[0:n, 0:512], [pk], [st], eng="act")
            for b in range(4):
                T.dma("pool", s.o_swin[g][b, W_G[g] - 8:W_G[g], :], st[8 * b:8 * b + 8, :], reads=[st],
                      writes=[s.o_swin[g]])
        wq = s.wload(s.wb_in, 0, 8, C_MQ, 512)
        for h in range(4):
            ps = s.proj_fm(wq, 8, h * 128, hT, n)
            s.ACT(s.MQfm[:, h, 0:n], ps[:, 0:n], AF.Identity, [ps, s.bcol], [s.MQfm],
                  bias=s.bcol[:, C_MQ // 128 + h:C_MQ // 128 + h + 1])
        Ppad = [[T.sb([128, 128], BF16, "Ppad") for _ in range(2)] for _ in range(4)]
        for b in range(4):
            for j in range(2):
                T.op("pool", lambda e: e.memset(Ppad[b][j][:], 0.0), [], [Ppad[b][j]])
        pcnt = [0, 0, 0, 0]
        U = [s.psA.get() for _ in range(4)]
        Mv = s.Mtab[:].rearrange("p (b h q) -> p b h q", h=4, q=128)
        for g in range(3):
            sp_ = s.psB.get()
            for h in range(4):
                s.MM(sp_[0:n, h * 32:h * 32 + n], Knew[:, (g * 4 + h) * 32:(g * 4 + h) * 32 + n],
                     s.Qfm[:, g * 4 + h, 0:n], True, True, [Knew, s.Qfm], [sp_])
            P = s.f2k.get()
            s.ACT(P[0:n, 0:128], sp_[0:n, 0:128], AF.Exp, [sp_], [P], scale=SCALE)
            Pn = s.b1k.get()
            s.TT(Pn[0:n, 0:128], P[0:n, 0:128], s.Mnew[0:n, g * 128:(g + 1) * 128], ALU.mult, [P, s.Mnew], [Pn])
            for h in range(4):
                s.MM(U[h][0:n, 0:129], Pn[0:n, h * 32:h * 32 + n], Vnew[0:n, g, h * 130:h * 130 + 129],
                     g == 0, False, [Pn, Vnew], [U[h]])
        todo = [(b, g, d) for b in range(4) for g in range(3) for d in range(1, NDEL[g])]
        pend = []

        def issue(ci):
            b, g, d = todo[ci]
            beta = W_G[g] // 128 - d
            ks, vs = s.cache_block(s.cwin[g][b, beta * 128:(beta + 1) * 128, :], s.cwin[g])
            sp_ = s.psB.get()
            for h in range(4):
                s.MM(sp_[:, h * 8:h * 8 + 8], ks[:, h * 128:(h + 1) * 128], s.Qfm[:, g * 4 + h, 8 * b:8 * b + 8],
                     True, True, [ks, s.Qfm], [sp_])
            pend.append((ci, b, g, d, sp_, vs))

        def retire():
            ci, b, g, d, sp_, vs = pend.pop(0)
            P = s.f2k.get()
            s.ACT(P[:, 0:32], sp_[:, 0:32], AF.Exp, [sp_], [P], scale=SCALE)
            pp = Ppad[b][pcnt[b] % 2]; pcnt[b] += 1
            s.TT(pp[:].rearrange("p (h q) -> p h q", q=32)[:, :, 8 * b:8 * b + 8],
                 P[:, 0:32].rearrange("p (h q) -> p h q", q=8), Mv[:, s.blk0[g] + d, :, 0:8], ALU.mult,
                 [P, s.Mtab], [pp])
            for h in range(4):
                s.MM(U[h][0:n, 0:129], pp[:, h * 32:(h + 1) * 32], vs[:, h * 130:h * 130 + 129],
                     False, ci == len(todo) - 1, [pp, vs], [U[h]])

        for ci in range(len(todo)):
            issue(ci)
            if len(pend) > 1:
                retire()
        while pend:
            retire()
        s.finalize_att(U, n, s.attT, 0)
        U = [s.psA.get() for _ in range(4)]
        for b in range(4):
            for blk in range(2):
                ks, vs = s.cache_block(s.cmem[b, blk * 128:(blk + 1) * 128, :], s.cmem)
                sp_ = s.psB.get()
                for h in range(4):
                    s.MM(sp_[:, h * 8:h * 8 + 8], ks[:, h * 128:(h + 1) * 128], s.MQfm[:, h, 8 * b:8 * b + 8],
                         True, True, [ks, s.MQfm], [sp_])
                pp = Ppad[b][pcnt[b] % 2]; pcnt[b] += 1
                s.ACT(pp[:].rearrange("p (h q) -> p h q", q=32)[:, :, 8 * b:8 * b + 8],
                      sp_[:, 0:32].rearrange("p (h q) -> p h q", q=8), AF.Exp, [sp_], [pp], scale=SCALE)
                for h in range(4):
                    s.MM(U[h][0:n, 0:129], pp[:, h * 32:(h + 1) * 32], vs[:, h * 130:h * 130 + 129],
                         b == 0 and blk == 0, b == 3 and blk == 1, [pp, vs], [U[h]])
        s.finalize_att(U, n, s.memT, 0)
        h2T = s.h2T.get()
        s.mixer_out_multi(hT, [(0, n, xt)], h2T)
        s.carry_s = T.sb([128, 32, 8], F32, "carry_s")
        for pc in range(4):
            sc_ = s.f4k.get()
            T.dma("sp", sc_[0:8, :], s.sconv_in[:, pc * 1024:(pc + 1) * 1024], writes=[sc_])
            ps = s.psB.get()
            for c in range(8):
                s.TR(ps[:, c * 8:(c + 1) * 8], sc_[0:8, c * 128:(c + 1) * 128], s.idf[0:8, 0:8], [sc_, s.idf], [ps])
            s.CP(s.carry_s[:, pc * 8:(pc + 1) * 8, :], ps[:, 0:64].rearrange("p (c j) -> p c j", j=8), [ps],
                 [s.carry_s], eng="dve")
        cs4 = s.carry_s[:].rearrange("p c (b j) -> p c b j", j=2)
        s.ffn_group(h2T, n, [xt], lambda ti: (s.o_ys, 0), lambda c: cs4[:, c, :, :], None, tok_view=(4, 8))
        s.a_rows(h2T, slice(0, n), n, s.o_sconv, [(2 * b, 8 * b + 6, 2) for b in range(4)])


NPRE, NHALO, NMAIN = 32, 16, 16
_CACHE = {}


def _get_program(npre, nhalo, nmain, with_sample=True):
    key = (npre, nhalo, nmain, with_sample)
    if key not in _CACHE:
        _CACHE[key] = KB(npre, nhalo, nmain, with_sample=with_sample)
    return _CACHE[key]


def _common_inputs(inp):
    f = lambda a: np.ascontiguousarray(np.asarray(a, dtype=np.float32))
    c = _static_consts()
    m = {
        "rel_bias": f(inp["rel_bias"]), "lb_logits": f(inp["hg_lb_logits"]),
        "g_mix_pre": f(inp["norm_mix_pre"][0]), "g_mix_post": f(inp["norm_mix_post"][0]),
        "g_ffn_pre": f(inp["norm_ffn_pre"][0]), "g_ffn_post": f(inp["norm_ffn_post"][0]),
        "g_mem": f(inp["mem_norm"][0]), "g_hg": f(inp["hg_norm"][0]),
        "w_in": f(inp["w_in"][0]), "b_in": f(inp["b_in"][0]), "w_memkv": f(inp["w_mem_kv"][0]),
        "w_br0": f(inp["w_br_att"][0]), "w_br1": f(inp["w_br_hg"][0]), "w_br2": f(inp["w_br_mem"][0]),
        "w_out": f(inp["w_out"][0]), "w_a": f(inp["w_ffn_a"][0]), "w_b": f(inp["w_ffn_b"][0]),
        "w_d": f(inp["w_ffn_d"][0]), "conv_w": f(inp["ffn_conv_w"][0]), "conv_b": f(inp["ffn_conv_b"][0]),
    }
    for nm in ("identf", "antiid", "tri_incl", "tri_up", "causal", "tri_incl_s", "tri_up_s", "causal_s",
               "blockdiag_s", "rowmask_s", "onehot"):
        m["c_" + nm] = c[nm]
    return m


def _core_inputs(inp, common, seq, p0, npre, nhalo, nmain, sq0):
    f = lambda a: np.ascontiguousarray(np.asarray(a, dtype=np.float32))
    nt = npre + nhalo + 1 + nmain
    start = p0 - 128 * (npre + nhalo + 1)
    x = np.zeros((nt * 128, D), np.float32)
    lo = max(start, 0)
    x[lo - start:] = inp["x_prompt"][seq, lo:p0 + nmain * 128]
    vm = (np.arange(start, p0 + nmain * 128) >= 0).astype(np.float32)[:, None]
    m = dict(common)
    m["x_ext"] = x
    m["vmask"] = np.ascontiguousarray(vm)
    m["flags"] = np.array([[1.0 if p0 > 0 else 0.0, 0.0]], np.float32)
    m["mem_in"] = f(inp["mem_prompt"][seq])
    m["xs_in"] = f(inp["x_sample"][sq0:sq0 + 4]).reshape(32, D)
    for g, nm in enumerate(("cache_win1_kv", "cache_win2_kv", "cache_win3_kv")):
        m[f"cwin{g}"] = f(inp[nm][0, sq0:sq0 + 4]).reshape(4, W_G[g], 1024)
    m["cmem"] = f(inp["cache_mem_kv"][0, sq0:sq0 + 4]).reshape(4, 256, 1024)
    m["shg_in"] = f(inp["state_hgrn"][0, sq0:sq0 + 4])
    m["sconv_in"] = f(inp["state_ffn_conv"][0, sq0:sq0 + 4]).reshape(8, DFF)
    return m


def kernel(**inp):
    kb = _get_program(NPRE, NHALO, NMAIN)
    common = _common_inputs(inp)
    in_maps = []
    for c in range(8):
        in_maps.append(_core_inputs(inp, common, c // 4, (c % 4) * 2048, NPRE, NHALO, NMAIN, 4 * c))
    res = run_bass_kernel_spmd(kb.nc, in_maps, core_ids=list(range(8))).results
    B, S = 2, 8192
    y_p = np.zeros((B, S, D), np.float32)
    y_s = np.zeros((32, 8, D), np.float32)
    p_win = [np.zeros((1, B, W_G[g], 2, 4, 128), np.float32) for g in range(3)]
    p_hg = np.zeros((1, B, 4, 128, 128), np.float32)
    p_conv = np.zeros((1, B, 2, DFF), np.float32)
    p_mem = np.zeros((1, B, 256, 2, 4, 128), np.float32)
    s_win = [np.zeros((1, 32, W_G[g], 2, 4, 128), np.float32) for g in range(3)]
    s_hg = np.zeros((1, 32, 4, 128, 128), np.float32)
    s_conv = np.zeros((1, 32, 2, DFF), np.float32)
    for c in range(8):
        r = res[c]
        b, j = c // 4, c % 4
        y_p[b, j * 2048:(j + 1) * 2048] = r["o_y"]
        y_s[4 * c:4 * c + 4] = r["o_ys"].reshape(4, 8, D)
        for g in range(3):
            s_win[g][0, 4 * c:4 * c + 4] = r[f"o_swin{g}"].reshape(4, W_G[g], 2, 4, 128)
        s_hg[0, 4 * c:4 * c + 4] = r["o_shg"]
        s_conv[0, 4 * c:4 * c + 4] = r["o_sconv"].reshape(4, 2, DFF)
        if j == 3:
            for g in range(3):
                p_win[g][0, b] = r[f"o_pwin{g}"].reshape(W_G[g], 2, 4, 128)
            p_hg[0, b] = r["o_phg"]
            p_conv[0, b] = r["o_pconv"]
        if j == 0:
            p_mem[0, b] = r["o_pmem"].reshape(256, 2, 4, 128)
    return (y_p, y_s, p_win[0], p_win[1], p_win[2], p_hg, p_conv, p_mem,
            s_win[0], s_win[1], s_win[2], s_hg, s_conv)
```

```python
import math
import numpy as np
import concourse.bass as bass
import concourse.mybir as mybir
from concourse.ap import AP
from concourse.bass_utils import run_bass_kernel_spmd

F32 = mybir.dt.float32
BF16 = mybir.dt.bfloat16
ALU = mybir.AluOpType
AF = mybir.ActivationFunctionType
AX = mybir.AxisListType

EPOCH = 3000


class Buf:
    __slots__ = ("t", "name", "w", "r", "excl")

    def __init__(self, t, name, excl=False):
        self.t = t
        self.name = name
        self.w = None
        self.r = []
        self.excl = excl

    def __getitem__(self, k):
        return self.t[k]


class Eng:
    def __init__(self, nc, name, obj, nep, ndma):
        self.name = name
        self.obj = obj
        self.sems = [nc.alloc_semaphore(f"s_{name}_{i}") for i in range(nep)]
        self.n = 0
        self.waited = {}
        self.dslots = [[nc.alloc_semaphore(f"d_{name}_{i}"), 0] for i in range(ndma)]
        self.dn = 0

    def tag_of(self, n):
        return (self.sems[(n - 1) // EPOCH], (n - 1) % EPOCH + 1, self.name)


class Trk:
    def __init__(self, nc):
        self.nc = nc
        self.E = {
            "pe": Eng(nc, "pe", nc.tensor, 14, 0),
            "act": Eng(nc, "act", nc.scalar, 8, 0),
            "dve": Eng(nc, "dve", nc.vector, 8, 0),
            "pool": Eng(nc, "pool", nc.gpsimd, 4, 16),
            "sp": Eng(nc, "sp", nc.sync, 1, 24),
        }
        self.nbuf = 0

    def sb(self, shape, dt, name):
        self.nbuf += 1
        return Buf(self.nc.alloc_sbuf_tensor(f"{name}_{self.nbuf}", list(shape), dt), name)

    def dram(self, shape, dt, name, kind="Internal"):
        return Buf(self.nc.dram_tensor(name, list(shape), dt, kind=kind), name)

    def _wait(self, E, deps):
        for d in deps:
            if d is None:
                continue
            sem, val, src = d
            if src == "pe" and E.name == "pe":
                continue
            key = id(sem)
            if E.waited.get(key, 0) < val:
                E.obj.wait_ge(sem, val)
                E.waited[key] = val

    def _deps(self, reads, writes):
        deps = []
        for b in reads:
            deps.append(b.w)
            if b.excl:
                deps.extend(b.r)
        for b in writes:
            deps.append(b.w)
            deps.extend(b.r)
        return deps

    def op(self, eng, fn, reads=(), writes=(), inc=True):
        E = self.E[eng]
        self._wait(E, self._deps(reads, writes))
        ins = fn(E.obj)
        if inc:
            E.n += 1
            tag = E.tag_of(E.n)
            ins.then_inc(tag[0], 1)
        else:
            assert eng == "pe"
            tag = E.tag_of(E.n + 1)
        for b in reads:
            b.r.append(tag)
        for b in writes:
            b.w = tag
            b.r = []
        return ins

    def dma(self, q, out, in_, reads=(), writes=(), **kw):
        E = self.E[q]
        self._wait(E, self._deps(reads, writes))
        slot = E.dslots[E.dn % len(E.dslots)]
        E.dn += 1
        if slot[1] > 0 and E.waited.get(id(slot[0]), 0) < slot[1]:
            E.obj.wait_ge(slot[0], slot[1])
            E.waited[id(slot[0])] = slot[1]
        ins = E.obj.dma_start(out=out, in_=in_, **kw)
        slot[1] += 16
        ins.then_inc(slot[0], 16)
        tag = (slot[0], slot[1], "dma")
        for b in reads:
            b.r.append(tag)
        for b in writes:
            b.w = tag
            b.r = []
        return ins

    def finish(self):
        for q in ("sp", "pool"):
            E = self.E[q]
            for sem, val in E.dslots:
                if val > 0:
                    E.obj.wait_ge(sem, val)
        sp = self.E["sp"]
        for nm in ("pe", "act", "dve", "pool"):
            E = self.E[nm]
            if E.n > 0:
                sem, val, _ = E.tag_of(E.n)
                sp.obj.wait_ge(sem, val)


class Pool:
    def __init__(self, bufs):
        self.bufs = bufs
        self.i = 0

    def get(self):
        b = self.bufs[self.i % len(self.bufs)]
        self.i += 1
        return b


D = 1024
KC = 8
IN_COLS = 10240
DFF = 4096
W_G = (128, 512, 2048)
DIL = (1, 4, 16)
NDEL = (2, 5, 17)
C_Q, C_K, C_V, C_HQ, C_HF, C_HI, C_HG, C_MQ, C_GA, C_GH, C_GM = (
    0, 1536, 3072, 4608, 5120, 5632, 6144, 6656, 7168, 8192, 9216)
EPS = 1e-6
TABL = 2304
SCALE = 1.0 / math.sqrt(128.0)


def _rel_bucket_np(dist):
    d = np.maximum(dist, 1).astype(np.float32)
    large = 16 + (np.log(d / np.float32(16)) / np.float32(math.log(128.0)) * np.float32(16)).astype(np.int32)
    large = np.minimum(large, 31)
    return np.where(dist < 16, dist, large)


def _static_consts():
    c = {}
    c["identf"] = np.eye(128, dtype=np.float32)
    c["antiid"] = np.eye(128, dtype=np.float32)[::-1].copy()
    s = np.arange(128)
    c["tri_incl"] = (s[:, None] <= s[None, :]).astype(np.float32)
    c["tri_up"] = (s[:, None] > s[None, :]).astype(np.float32)
    c["causal"] = (s[:, None] <= s[None, :]).astype(np.float32)
    s32 = np.arange(32)
    same = (s32[:, None] // 8) == (s32[None, :] // 8)
    t32 = np.zeros((128, 128), np.float32); t32[:32, :32] = same & (s32[:, None] <= s32[None, :])
    u32 = np.zeros((128, 128), np.float32); u32[:32, :32] = same & (s32[:, None] > s32[None, :])
    c["tri_incl_s"] = t32
    c["tri_up_s"] = u32
    c["causal_s"] = t32.copy()
    bd = np.zeros((128, 128), np.float32); bd[:32, :32] = same
    c["blockdiag_s"] = bd
    rm = np.zeros((128, 4), np.float32)
    for b in range(4):
        rm[8 * b:8 * b + 8, b] = 1.0
    c["rowmask_s"] = rm
    oh = np.zeros((3, 32, TABL), np.float32)
    for g in range(3):
        i = np.arange(TABL)
        dl = i - 127
        ok = (dl >= 0) & (dl <= W_G[g]) & (dl % DIL[g] == 0)
        bk = _rel_bucket_np(np.maximum(dl, 0).astype(np.int32))
        oh[g, bk[ok], i[ok]] = 1.0
    c["onehot"] = oh
    return c


class KB:
    def __init__(self, npre, nhalo, nmain, stn=2, with_sample=True, dbg=None):
        self.DBG = dbg
        self.npre, self.nhalo, self.nmain, self.stn = npre, nhalo, nmain, stn
        self.with_sample = with_sample
        self.NT = npre + nhalo + 1 + nmain
        self.T0 = npre + nhalo
        self.first_g = [self.T0 - min(nhalo, NDEL[g] - 1) for g in range(3)]
        self.ntile_g = [self.NT - self.first_g[g] for g in range(3)]
        nc = self.nc = bass.Bass("TRN2", target_bir_lowering=False)
        T = self.T = Trk(nc)
        self.declare_io()
        self.alloc()
        try:
            self.setup()
            self.stop_at("setup")
            self.prompt()
            self.stop_at("prompt")
            if with_sample:
                self.sample()
        except StopIteration:
            pass
        T.finish()

    def declare_io(s):
        T = s.T
        I = lambda n, sh: T.dram(sh, F32, n, kind="ExternalInput")
        O = lambda n, sh: T.dram(sh, F32, n, kind="ExternalOutput")
        NT = s.NT
        s.x_ext = I("x_ext", [NT * 128, D])
        s.vmask = I("vmask", [NT * 128, 1])
        s.flags = I("flags", [1, 2])
        s.mem_in = I("mem_in", [256, D])
        s.xs_in = I("xs_in", [32, D])
        s.cwin = [I(f"cwin{g}", [4, W_G[g], 1024]) for g in range(3)]
        s.cmem = I("cmem", [4, 256, 1024])
        s.shg_in = I("shg_in", [4, 4, 128, 128])
        s.sconv_in = I("sconv_in", [8, DFF])
        s.rel_bias = I("rel_bias", [32, 12])
        s.lb_logits = I("lb_logits", [2, 512])
        s.g_mix_pre = I("g_mix_pre", [D]); s.g_mix_post = I("g_mix_post", [D])
        s.g_ffn_pre = I("g_ffn_pre", [D]); s.g_ffn_post = I("g_ffn_post", [D])
        s.g_mem = I("g_mem", [D]); s.g_hg = I("g_hg", [128])
        s.w_in = I("w_in", [D, IN_COLS]); s.b_in = I("b_in", [IN_COLS])
        s.w_memkv = I("w_memkv", [D, 1024])
        s.w_br = [I(f"w_br{i}", [512, D]) for i in range(3)]
        s.w_out = I("w_out", [D, D])
        s.w_a = I("w_a", [D, DFF]); s.w_b = I("w_b", [D, DFF]); s.w_d = I("w_d", [DFF, D])
        s.conv_w = I("conv_w", [3, DFF]); s.conv_b = I("conv_b", [DFF])
        for nm in ("identf", "antiid", "tri_incl", "tri_up", "causal", "tri_incl_s", "tri_up_s",
                   "causal_s", "blockdiag_s"):
            setattr(s, "c_" + nm, I("c_" + nm, [128, 128]))
        s.c_rowmask_s = I("c_rowmask_s", [128, 4])
        s.c_onehot = I("c_onehot", [3, 32, TABL])
        s.o_y = O("o_y", [s.nmain * 128, D])
        s.o_ys = O("o_ys", [32, D])
        s.o_pwin = [O(f"o_pwin{g}", [min(W_G[g], s.nmain * 128), 1024]) for g in range(3)]
        s.o_phg = O("o_phg", [4, 128, 128])
        s.o_pconv = O("o_pconv", [2, DFF])
        s.o_pmem = O("o_pmem", [256, 1024])
        s.o_swin = [O(f"o_swin{g}", [4, W_G[g], 1024]) for g in range(3)]
        s.o_shg = O("o_shg", [4, 4, 128, 128])
        s.o_sconv = O("o_sconv", [8, DFF])
        Sc = lambda n, sh, dt=BF16: T.dram(sh, dt, n)
        s.wb_in = [Sc(f"wb_in{i}", [2, 128, 8, 512]) for i in range(10)]; s.bb_in = Sc("bb_in", [1, IN_COLS])
        s.wb_memkv = Sc("wb_memkv", [2, 128, 8, 512])
        s.wb_br = [Sc(f"wb_br{i}", [2, 128, 4, 512]) for i in range(3)]
        s.wb_out = Sc("wb_out", [2, 128, 8, 512])
        s.wb_a = Sc("wb_a", [8, 128, 8, 512]); s.wb_b = Sc("wb_b", [8, 128, 8, 512])
        s.wb_d = Sc("wb_d", [8, 128, 8, 512])
        s.vtab = Sc("vtab", [12, TABL], F32)
        s.Kd = [Sc(f"Kd{g}", [s.ntile_g[g], 128, 512]) for g in range(3)]
        s.Vd = [Sc(f"Vd{g}", [s.ntile_g[g], 128, 520]) for g in range(3)]

    DBG = None

    def stop_at(s, name):
        if s.DBG and s.DBG.get("stop") == name:
            raise StopIteration

    def dbg(s, name, ap, bufs, shape, dt=F32):
        if not s.DBG or name in s.DBG["done"] or name not in s.DBG["want"]:
            return
        s.DBG["done"].add(name)
        d = s.T.dram(list(shape), dt, "dbg_" + name, kind="ExternalOutput")
        s.T.dma("sp", d[:], ap, reads=bufs, writes=[d])

    def ACT(s, out, in_, func, R, W, **kw):
        return s.T.op("act", lambda e: e.activation(out=out, in_=in_, func=func, **kw), R, W)

    def TT(s, out, a, b, op, R, W, eng="dve"):
        return s.T.op(eng, lambda e: e.tensor_tensor(out=out, in0=a, in1=b, op=op), R, W)

    def TS(s, out, a, s1, s2, op0, op1, R, W, eng="dve"):
        if s2 is None:
            return s.T.op(eng, lambda e: e.tensor_scalar(out=out, in0=a, scalar1=s1, scalar2=None, op0=op0), R, W)
        return s.T.op(eng, lambda e: e.tensor_scalar(out=out, in0=a, scalar1=s1, scalar2=s2, op0=op0, op1=op1), R, W)

    def STT(s, out, a, sc, b, op0, op1, R, W, eng="dve"):
        return s.T.op(eng, lambda e: e.scalar_tensor_tensor(out=out, in0=a, scalar=sc, in1=b, op0=op0, op1=op1), R, W)

    def CP(s, out, in_, R, W, eng="dve"):
        if eng == "act":
            return s.T.op("act", lambda e: e.activation(out=out, in_=in_, func=AF.Copy), R, W)
        return s.T.op(eng, lambda e: e.tensor_copy(out=out, in_=in_), R, W)

    def MM(s, out, lhsT, rhs, start, stop, R, W, inc=True):
        return s.T.op("pe", lambda e: e.matmul(out, lhsT=lhsT, rhs=rhs, start=start, stop=stop), R, W, inc=inc)

    def TR(s, out, in_, ident, R, W):
        return s.T.op("pe", lambda e: e.transpose(out=out, in_=in_, identity=ident), R, W)

    def alloc(s):
        T, nc = s.T, s.nc
        NTOK = s.stn * 128
        s.NTOK = NTOK
        banks = [Buf(nc.alloc_psum_tensor(f"psb{i}", [128, 512], F32), f"psb{i}", excl=True) for i in range(8)]
        s.psA = Pool(banks[:4])
        s.psB = Pool(banks[4:])
        P = lambda name, shape, dt, n: Pool([T.sb(shape, dt, name) for _ in range(n)])
        s.idf = T.sb([128, 128], F32, "idf"); s.idb = T.sb([128, 128], BF16, "idb")
        s.tri_incl = T.sb([128, 128], F32, "tri_incl"); s.tri_up = T.sb([128, 128], F32, "tri_up")
        s.causal = s.tri_incl
        s.tri_incl_s = T.sb([128, 128], F32, "tri_incl_s"); s.tri_up_s = T.sb([128, 128], F32, "tri_up_s")
        s.causal_s = s.tri_incl_s; s.bd_s = T.sb([128, 128], F32, "bd_s")
        s.rowmask = T.sb([128, 4], F32, "rowmask")
        s.ones_bf = T.sb([1, 128], BF16, "ones_bf"); s.ones_col = T.sb([128, 1], F32, "ones_col")
        s.gcols = T.sb([128, 24], F32, "gcols")
        s.bcol = T.sb([128, 80], F32, "bcol")
        s.ccol = T.sb([128, 128], F32, "ccol")
        s.lbcol = T.sb([128, 12], F32, "lbcol")
        s.oml_tm = T.sb([128, 512], F32, "oml_tm")
        s.noml_tm = T.sb([128, 512], F32, "noml_tm")
        s.gpost_mix = T.sb([128, D], F32, "gpost_mix"); s.gpost_ffn = T.sb([128, D], F32, "gpost_ffn")
        s.ghg4 = T.sb([128, 512], F32, "ghg4")
        s.flg = T.sb([128, 2], F32, "flg")
        s.Mtab = T.sb([128, 24 * 4 * 128], BF16, "Mtab")
        s.Mnew = T.sb([128, 12 * 32], BF16, "Mnew")
        s.wpool = P("wslot", [128, 9, 512], BF16, 4)
        s.xpool = P("xt", [128, D], F32, s.stn + 1)
        s.vmpool = P("vm", [128, 1], F32, s.stn + 1)
        s.hT = P("hT", [128, KC, NTOK], BF16, 1)
        s.h2T = P("h2T", [128, KC, NTOK], BF16, 1)
        s.Qfm = T.sb([128, 12, NTOK], BF16, "Qfm")
        s.MQfm = T.sb([128, 4, NTOK], BF16, "MQfm")
        s.qfm = T.sb([128, 4, NTOK], F32, "qfm")
        s.kTfm = T.sb([128, 4, NTOK], F32, "kTfm")
        s.attT = T.sb([128, 4, NTOK], BF16, "attT"); s.hgT = T.sb([128, 4, NTOK], BF16, "hgT")
        s.memT = T.sb([128, 4, NTOK], BF16, "memT")
        s.kslot = P("kslot", [128, 512], BF16, 5)
        s.vslot = P("vslot", [128, 520], BF16, 5)
        s.memK = T.sb([128, 4, 256], BF16, "memK"); s.memV = T.sb([128, 2, 520], BF16, "memV")
        s.f2k = P("f2k", [128, 512], F32, 7)
        s.b1k = P("b1k", [128, 512], BF16, 6)
        s.f4k = P("f4k", [128, D], F32, 4)
        s.b2k = P("b2k", [128, D], BF16, 3)
        s.sm = P("sm", [128, 16], F32, 12)
        s.S = T.sb([128, 512], F32, "S"); s.Sbf = T.sb([128, 512], BF16, "Sbf")
        s.hv = T.sb([128, 512], BF16, "hv"); s.hqhat = T.sb([128, 512], BF16, "hqhat")
        s.hqtil = T.sb([128, 512], BF16, "hqtil")
        s.ebias = T.sb([32, 12], F32, "ebias")
        s.Knew = T.sb([128, 384], BF16, "Knew")
        s.Gext = P("Gext", [128, 4, 129], F32, 1)
        s.carry = T.sb([128, 32, 2], F32, "carry")
        s.aext = P("aext", [128, 264], F32, 2)
        s.stage = s.f4k

    def colstage(s, rows_spec, dst_buf, ncols):
        T = s.T
        st = s.f2k.get()
        r0 = 0
        for ap, r in rows_spec:
            T.dma("sp", st[r0:r0 + r, 0:128], ap, writes=[st])
            r0 += r
        assert r0 == ncols
        ps = s.psB.get()
        s.TR(ps[:, 0:ncols], st[0:ncols, 0:128], s.idf[0:ncols, 0:ncols], [st, s.idf], [ps])
        s.CP(dst_buf[:, 0:ncols], ps[:, 0:ncols], [ps], [dst_buf])
        return st

    def setup(s):
        T = s.T
        ld = lambda dst, src: T.dma("sp", dst[:], src[:], writes=[dst])
        s.J = s.f4k.get()
        ld(s.idf, s.c_identf); T.dma("sp", s.J[:, 0:128], s.c_antiid[:], writes=[s.J])
        ld(s.tri_incl, s.c_tri_incl); ld(s.tri_up, s.c_tri_up)
        ld(s.tri_incl_s, s.c_tri_incl_s); ld(s.tri_up_s, s.c_tri_up_s)
        ld(s.bd_s, s.c_blockdiag_s); ld(s.rowmask, s.c_rowmask_s)
        s.CP(s.idb[:], s.idf[:], [s.idf], [s.idb], eng="pool")
        T.op("pool", lambda e: e.memset(s.ones_bf[:], 1.0), [], [s.ones_bf])
        T.op("pool", lambda e: e.memset(s.ones_col[:], 1.0), [], [s.ones_col])
        T.op("pool", lambda e: e.memset(s.carry[:], 0.0), [], [s.carry])
        T.dma("sp", s.flg[:], AP(s.flags.t, 0, [[0, 128], [1, 2]]), writes=[s.flg])
        v8 = lambda b: b[:].rearrange("(r c) -> r c", c=128)
        s.colstage([(v8(s.g_mix_pre), 8), (v8(s.g_ffn_pre), 8), (v8(s.g_mem), 8)], s.gcols, 24)
        st = s.colstage([(v8(s.b_in), 80)], s.bcol, 80)
        bb = s.b1k.get()
        s.CP(bb[0:80, 0:128], st[0:80, 0:128], [st], [bb])
        T.dma("pool", s.bb_in[:].rearrange("o (r c) -> (o r) c", c=128), bb[0:80, 0:128], reads=[bb], writes=[s.bb_in])
        s.colstage([(s.conv_w[:].rearrange("j (r c) -> (j r) c", c=128), 96), (v8(s.conv_b), 32)], s.ccol, 128)
        tmp = T.sb([128, 8], F32, "lbtmp")
        s.colstage([(s.lb_logits[:].rearrange("l (r c) -> (l r) c", c=128), 8)], tmp, 8)
        s.TT(s.lbcol[:, 0:4], tmp[:, 0:4], tmp[:, 4:8], ALU.subtract, [tmp], [s.lbcol])
        s.ACT(s.lbcol[:, 0:4], s.lbcol[:, 0:4], AF.Sigmoid, [s.lbcol], [s.lbcol])
        s.TS(s.lbcol[:, 4:8], s.lbcol[:, 0:4], -1.0, 1.0, ALU.mult, ALU.add, [s.lbcol], [s.lbcol])
        s.TS(s.lbcol[:, 8:12], s.lbcol[:, 4:8], -1.0, None, ALU.mult, None, [s.lbcol], [s.lbcol])
        l1 = s.f2k.get()
        s.lb_tm = s.f2k.get()
        T.dma("sp", s.lb_tm[:], s.lb_logits[0, :].partition_broadcast(128), writes=[s.lb_tm])
        T.dma("sp", l1[:], s.lb_logits[1, :].partition_broadcast(128), writes=[l1])
        s.TT(s.lb_tm[:], s.lb_tm[:], l1[:], ALU.subtract, [s.lb_tm, l1], [s.lb_tm])
        s.ACT(s.lb_tm[:], s.lb_tm[:], AF.Sigmoid, [s.lb_tm], [s.lb_tm])
        s.TS(s.oml_tm[:], s.lb_tm[:], -1.0, 1.0, ALU.mult, ALU.add, [s.lb_tm], [s.oml_tm])
        s.TS(s.noml_tm[:], s.oml_tm[:], -1.0, None, ALU.mult, None, [s.oml_tm], [s.noml_tm])
        T.dma("sp", s.gpost_mix[:], s.g_mix_post[:].partition_broadcast(128), writes=[s.gpost_mix])
        T.dma("sp", s.gpost_ffn[:], s.g_ffn_post[:].partition_broadcast(128), writes=[s.gpost_ffn])
        for h in range(4):
            T.dma("sp", s.ghg4[:, h * 128:(h + 1) * 128], s.g_hg[:].partition_broadcast(128), writes=[s.ghg4])
        eb = s.ebias
        T.dma("sp", eb[0:32, 0:12], s.rel_bias[:], writes=[eb])
        s.ACT(eb[0:32, 0:12], eb[0:32, 0:12], AF.Exp, [eb], [eb])
        for g in range(3):
            for c0 in range(0, TABL, 512):
                n = min(512, TABL - c0)
                oh = s.f2k.get()
                T.dma("sp", oh[0:32, 0:n], s.c_onehot[g, :, c0:c0 + n], writes=[oh])
                ps = s.psB.get()
                s.MM(ps[0:4, 0:n], eb[0:32, g * 4:(g + 1) * 4], oh[0:32, 0:n], True, True, [eb, oh], [ps])
                vt = s.f2k.get()
                s.CP(vt[0:4, 0:n], ps[0:4, 0:n], [ps], [vt])
                T.dma("pool", s.vtab[g * 4:(g + 1) * 4, c0:c0 + n], vt[0:4, 0:n], reads=[vt], writes=[s.vtab])
        Mv = s.Mtab[:].rearrange("p (b h q) -> p b h q", h=4, q=128)
        blk0 = 0
        s.blk0 = []
        for g in range(3):
            s.blk0.append(blk0)
            for h in range(4):
                for d0 in range(0, NDEL[g], 4):
                    nd = min(4, NDEL[g] - d0)
                    hk = s.f2k.get()
                    T.dma("sp", hk[:, 0:nd * 128],
                          AP(s.vtab.t, (g * 4 + h) * TABL + d0 * 128, [[1, 128], [1, nd * 128]]),
                          reads=[s.vtab], writes=[hk])
                    ps = s.psB.get()
                    s.MM(ps[:, 0:nd * 128], s.J[:, 0:128], hk[:, 0:nd * 128], True, True, [s.J, hk], [ps])
                    s.CP(Mv[:, blk0 + d0:blk0 + d0 + nd, h, :],
                         ps[:, 0:nd * 128].rearrange("p (b q) -> p b q", q=128), [ps], [s.Mtab],
                         eng=("act" if h % 2 else "dve"))
            blk0 += NDEL[g]
        Mn = s.Mnew[:].rearrange("p (a q) -> p a q", q=32)
        for g in range(3):
            for h in range(4):
                s.TT(Mn[0:32, g * 4 + h, :], Mv[0:32, s.blk0[g], h, 0:32], s.bd_s[0:32, 0:32], ALU.mult,
                     [s.Mtab, s.bd_s], [s.Mnew])
        s.castg = s.cast_gen()
        s.run_casts(8)

    def cast_gen(s):
        T = s.T
        engs = ["dve", "act"]
        k = [0]

        def cast(src, dst, R, C, gcol0, sc0=0, wd=False):
            for r in range(R // 128):
                for c0 in range(0, C, 1024):
                    a = s.f4k.get(); b = s.b2k.get()
                    T.dma("sp", a[:], src[r * 128:(r + 1) * 128, sc0 + c0:sc0 + c0 + 1024], writes=[a])
                    eng = engs[k[0] % 2]; k[0] += 1
                    if gcol0 is None:
                        s.CP(b[:], a[:], [a], [b], eng=eng)
                    elif eng == "act":
                        s.ACT(b[:], a[:], AF.Copy, [a, s.gcols], [b], scale=s.gcols[:, gcol0 + r:gcol0 + r + 1])
                    else:
                        s.TS(b[:], a[:], s.gcols[:, gcol0 + r:gcol0 + r + 1], None, ALU.mult, None,
                             [a, s.gcols], [b], eng=eng)
                    if wd:
                        dap = dst[r // 4].rearrange("p (h k) c -> p h k c", h=2)[:, :, r % 4, :]
                    else:
                        dap = dst[c0 // 512:c0 // 512 + 2, :, r, :].rearrange("j p c -> p j c")
                    T.dma("pool", dap, b[:].rearrange("p (j c) -> p j c", c=512), reads=[b], writes=[dst])
                    yield

        for blk in (5, 1, 2, 3, 4, 0, 6, 7, 8, 9):
            yield from cast(s.w_in, s.wb_in[blk], D, 1024, 0, sc0=blk * 1024)
        yield from cast(s.w_memkv, s.wb_memkv, D, 1024, 16)
        for i in range(3):
            yield from cast(s.w_br[i], s.wb_br[i], 512, D, None)
        yield from cast(s.w_out, s.wb_out, D, D, None)
        yield from cast(s.w_a, s.wb_a, D, DFF, 8)
        yield from cast(s.w_b, s.wb_b, D, DFF, 8)
        yield from cast(s.w_d, s.wb_d, DFF, D, None, wd=True)

    def run_casts(s, n=None):
        i = 0
        for _ in s.castg:
            i += 1
            if n is not None and i >= n:
                break

    def wload(s, src, r0, nk, c0, ncols, bias_c0=None):
        T = s.T
        if isinstance(src, list):
            src = src[c0 // 1024]
            c0 = c0 % 1024
        w = s.wpool.get()
        assert ncols == 512 and c0 % 512 == 0 and r0 == 0
        T.dma("sp", w[:, 0:nk, 0:512], src[c0 // 512], reads=[src], writes=[w])
        if bias_c0 is not None:
            T.dma("sp", w[0:1, 8, 0:ncols], s.bb_in[0:1, bias_c0:bias_c0 + ncols], reads=[s.bb_in], writes=[w])
        return w

    def rstd_of(s, src, srcbufs, n, Dn):
        junk = s.b2k.get(); ss = s.sm.get()
        s.ACT(junk[0:n, 0:Dn], src, AF.Square, srcbufs, [junk, ss], accum_out=ss[0:n, 0:1])
        s.TS(ss[0:n, 0:1], ss[0:n, 0:1], 1.0 / Dn, EPS, ALU.mult, ALU.add, [ss], [ss])
        s.ACT(ss[0:n, 0:1], ss[0:n, 0:1], AF.Sqrt, [ss], [ss])
        s.T.op("dve", lambda e: e.reciprocal(out=ss[0:n, 0:1], in_=ss[0:n, 0:1]), [ss], [ss])
        return ss

    def transposes(s, src, n, nch, dstT, col0, eng="act"):
        ps = s.psB.get()
        pb = ps.t[:].bitcast(BF16)
        for c in range(nch):
            s.TR(pb[:, c * 128:c * 128 + n], src[0:n, c * 128:(c + 1) * 128], s.idb[0:n, 0:n], [src, s.idb], [ps])
        s.CP(dstT[:, 0:nch, col0:col0 + n],
             pb[:, 0:nch * 128].rearrange("p (c t) -> p c t", t=128)[:, :, 0:n], [ps], [dstT], eng=eng)

    def front(s, xsrc, row0, n, hT, col0, xt=None):
        T = s.T
        if xt is None:
            xt = s.xpool.get()
        T.dma("sp", xt[0:n, :], xsrc[row0:row0 + n, :], writes=[xt])
        ss = s.rstd_of(xt[0:n, :], [xt], n, D)
        xn = s.b2k.get()
        s.TS(xn[0:n, :], xt[0:n, :], ss[0:n, 0:1], None, ALU.mult, None, [xt, ss], [xn])
        s.transposes(xn, n, KC, hT, col0)
        return xt

    def proj_fm(s, w, nk, wc0, actT, ntok):
        ps = s.psA.get()
        for kc in range(nk):
            s.MM(ps[:, 0:ntok], w[:, kc, wc0:wc0 + 128], actT[:, kc, 0:ntok], kc == 0, kc == nk - 1, [w, actT], [ps],
                 inc=(kc == nk - 1))
        return ps

    def proj_tm(s, w, nk, ncols, actT, col0, n, bias, ps=None, pc0=0, wc0=0):
        if ps is None:
            ps = s.psA.get()
        for kc in range(nk):
            s.MM(ps[0:n, pc0:pc0 + ncols], actT[:, kc, col0:col0 + n], w[:, kc, wc0:wc0 + ncols],
                 kc == 0, (kc == nk - 1) and not bias, [w, actT], [ps], inc=((kc == nk - 1) and not bias))
        if bias:
            s.MM(ps[0:n, pc0:pc0 + ncols], s.ones_bf[0:1, 0:n], w[0:1, 8, wc0:wc0 + ncols], False, True,
                 [w, s.ones_bf], [ps])
        return ps

    def hgrn_tile(s, col0, n, vm, hf_ps, hi_ps, hg_ps, full, need_bf):
        h4 = lambda ap: ap.rearrange("p (h t) -> p h t", h=4)
        sg = s.f2k.get()
        s.ACT(sg[0:n, :], hf_ps[0:n, 0:512], AF.Sigmoid, [hf_ps], [sg])
        k = s.f2k.get()
        s.TT(k[0:n, :], sg[0:n, :], s.noml_tm[0:n, :], ALU.mult, [sg, s.noml_tm], [k])
        s.TT(k[0:n, :], k[0:n, :], s.oml_tm[0:n, :], ALU.add, [k, s.oml_tm], [k])
        g = s.f2k.get()
        s.ACT(g[0:n, :], k[0:n, :], AF.Ln, [k], [g], scale=-1.0, bias=1.0)
        R = s.psB.get()
        s.MM(R[0:n, 0:512], s.tri_up[0:n, 0:n], g[0:n, :], True, True, [s.tri_up, g], [R])
        eR = s.f2k.get()
        s.ACT(eR[0:n, :], R[0:n, 0:512], AF.Exp, [R], [eR])
        kend = s.b1k.get()
        s.TT(kend[0:n, :], k[0:n, :], eR[0:n, :], ALU.mult, [k, eR], [kend])
        v = s.hv
        s.TS(v[0:n, :], hi_ps[0:n, 0:512], vm[0:n, 0:1], None, ALU.mult, None, [hi_ps, vm], [v])
        ge = s.psB.get()
        for h in range(4):
            s.MM(ge[:, h:h + 1], g[0:n, h * 128:(h + 1) * 128], s.ones_col[0:n, 0:1], True, True,
                 [g, s.ones_col], [ge])
        eg = s.sm.get()
        s.ACT(eg[:, 0:4], ge[:, 0:4], AF.Exp, [ge], [eg])
        st = s.psB.get()
        for h in range(4):
            hs = slice(h * 128, (h + 1) * 128)
            s.MM(st[:, hs], kend[0:n, hs], v[0:n, hs], True, True, [kend, v], [st])
        for h in range(4):
            hs = slice(h * 128, (h + 1) * 128)
            s.STT(s.S[:, hs], s.S[:, hs], eg[:, h:h + 1], st[:, hs], ALU.mult, ALU.add, [s.S, eg, st], [s.S])
        if full:
            assert n == 128
            GT = s.psB.get()
            for h in range(4):
                s.MM(GT[:, h * 128:(h + 1) * 128], g[:, h * 128:(h + 1) * 128], s.tri_incl[:], True, True,
                     [g, s.tri_incl], [GT])
            Gx = s.Gext.get()
            s.CP(Gx[:, :, 1:129], h4(GT[:, 0:512]), [GT], [Gx], eng="act")
            eG = s.f2k.get()
            s.ACT(h4(eG[:]), Gx[:, :, 1:129], AF.Exp, [Gx], [eG])
            qhat = s.hqhat
            s.TT(h4(qhat[:]), s.qfm[:, :, col0:col0 + 128], h4(eG[:]), ALU.mult, [s.qfm, eG], [qhat])
            dq = s.f2k.get()
            v5 = lambda ap: ap.rearrange("p h (m j) -> p h m j", j=16)
            s.TT(v5(h4(dq[:])), v5(Gx[:, :, 1:129]),
                 v5(Gx[:, :, 0:128])[:, :, :, 0:1].to_broadcast([128, 4, 8, 16]), ALU.subtract, [Gx], [dq])
            s.ACT(dq[:], dq[:], AF.Exp, [dq], [dq])
            qtil = s.hqtil
            s.TT(h4(qtil[:]), s.qfm[:, :, col0:col0 + 128], h4(dq[:]), ALU.mult, [s.qfm, dq], [qtil])
            sc = s.psB.get()
            for m in range(8):
                dk = s.f2k.get()
                s.STT(h4(dk[:]), Gx[:, :, 1:129], -1.0,
                      Gx[:, :, 16 * m:16 * m + 1].to_broadcast([128, 4, 128]), ALU.mult, ALU.add, [Gx], [dk])
                s.ACT(dk[:], dk[:], AF.Exp, [dk], [dk])
                kt = s.b1k.get()
                s.STT(h4(kt[:]), h4(dk[:]), 1e30, s.kTfm[:, :, col0:col0 + 128], ALU.min, ALU.mult,
                      [dk, s.kTfm], [kt])
                for h in range(4):
                    c = h * 128 + 16 * m
                    s.MM(sc[:, c:c + 16], kt[:, h * 128:(h + 1) * 128], qtil[:, c:c + 16], True, True,
                         [kt, qtil], [sc])
            scT = s.b1k.get()
            s.TT(h4(scT[:]), h4(sc[:, 0:512]), s.causal[:].unsqueeze(1).to_broadcast([128, 4, 128]), ALU.mult,
                 [sc, s.causal], [scT])
            s.dbg("Gx", Gx[:].rearrange("p h t -> p (h t)"), [Gx], [128, 516])
            s.dbg("qhat", qhat[:], [qhat], [128, 512], BF16)
            s.dbg("qtil", qtil[:], [qtil], [128, 512], BF16)
            s.dbg("scT", scT[:], [scT], [128, 512], BF16)
            s.dbg("Sbf", s.Sbf[:], [s.Sbf], [128, 512], BF16)
            s.dbg("S", s.S[:], [s.S], [128, 512])
            s.dbg("v", v[:], [v], [128, 512], BF16)
            s.dbg("kTfm", s.kTfm[:, :, col0:col0 + 128], [s.kTfm], [128, 4, 128])
            s.dbg("qfm", s.qfm[:, :, col0:col0 + 128], [s.qfm], [128, 4, 128])
            if s.DBG and s.DBG.get("stop") == "hgrn":
                raise StopIteration
            o = s.psA.get()
            for h in range(4):
                hs = slice(h * 128, (h + 1) * 128)
                s.MM(o[:, hs], scT[:, hs], v[:, hs], True, False, [scT, v], [o], inc=True)
                s.MM(o[:, hs], qhat[:, hs], s.Sbf[:, hs], False, True, [qhat, s.Sbf], [o])
            o2 = s.f2k.get()
            s.ACT(o2[:], o[:, 0:512], AF.Square, [o], [o2])
            ssq = s.sm.get()
            s.T.op("dve", lambda e: e.tensor_reduce(out=ssq[:, 0:4], in_=h4(o2[:]), axis=AX.X, op=ALU.add),
                   [o2], [ssq])
            s.TS(ssq[:, 0:4], ssq[:, 0:4], 1.0 / 128, EPS, ALU.mult, ALU.add, [ssq], [ssq])
            s.ACT(ssq[:, 0:4], ssq[:, 0:4], AF.Sqrt, [ssq], [ssq])
            s.T.op("dve", lambda e: e.reciprocal(out=ssq[:, 0:4], in_=ssq[:, 0:4]), [ssq], [ssq])
            sgt = s.f2k.get()
            s.ACT(sgt[:], hg_ps[:, 0:512], AF.Sigmoid, [hg_ps], [sgt])
            s.TT(sgt[:], sgt[:], s.ghg4[:], ALU.mult, [sgt, s.ghg4], [sgt])
            on = s.f2k.get()
            s.TT(h4(on[:]), h4(o[:, 0:512]), ssq[:, 0:4].unsqueeze(2).to_broadcast([128, 4, 128]), ALU.mult,
                 [o, ssq], [on])
            hg = s.b1k.get()
            s.TT(hg[:], on[:], sgt[:], ALU.mult, [on, sgt], [hg])
            s.transposes(hg, 128, 4, s.hgT, col0, eng="dve")
        if need_bf:
            s.CP(s.Sbf[:], s.S[:], [s.S], [s.Sbf], eng="act")

    def hgrn_state_multi(s, nt_, vms, hf_ps, hi_ps, need_bf):
        Wd = 512 * nt_
        sg = s.f4k.get(); k = s.f4k.get(); g = s.f4k.get()
        for i in range(nt_):
            s.ACT(sg[:, i * 512:(i + 1) * 512], hf_ps[i][:, 0:512], AF.Sigmoid, [hf_ps[i]], [sg])
        v3 = lambda ap: ap.rearrange("p (i c) -> p i c", c=512)
        bc = lambda b: b[:].unsqueeze(1).to_broadcast([128, nt_, 512])
        s.TT(v3(k[:, 0:Wd]), v3(sg[:, 0:Wd]), bc(s.noml_tm), ALU.mult, [sg, s.noml_tm], [k])
        s.TT(v3(k[:, 0:Wd]), v3(k[:, 0:Wd]), bc(s.oml_tm), ALU.add, [k, s.oml_tm], [k])
        s.ACT(g[:, 0:Wd], k[:, 0:Wd], AF.Ln, [k], [g], scale=-1.0, bias=1.0)
        for i in range(nt_):
            R = s.psB.get()
            s.MM(R[:, 0:512], s.tri_up[:], g[:, i * 512:(i + 1) * 512], True, True, [s.tri_up, g], [R])
            s.ACT(sg[:, i * 512:(i + 1) * 512], R[:, 0:512], AF.Exp, [R], [sg])
        kend = s.b2k.get(); v = s.b2k.get()
        s.TT(kend[:, 0:Wd], k[:, 0:Wd], sg[:, 0:Wd], ALU.mult, [k, sg], [kend])
        for i in range(nt_):
            s.TS(v[:, i * 512:(i + 1) * 512], hi_ps[i][:, 0:512], vms[i][:, 0:1], None, ALU.mult, None,
                 [hi_ps[i], vms[i]], [v])
        ge = s.psB.get()
        for i in range(nt_):
            for h in range(4):
                c = i * 512 + h * 128
                s.MM(ge[:, i * 4 + h:i * 4 + h + 1], g[:, c:c + 128], s.ones_col[:, 0:1], True, True,
                     [g, s.ones_col], [ge])
        eg = s.sm.get()
        s.ACT(eg[:, 0:4 * nt_], ge[:, 0:4 * nt_], AF.Exp, [ge], [eg])
        sts = []
        for i in range(nt_):
            st = s.psB.get()
            for h in range(4):
                c = i * 512 + h * 128
                s.MM(st[:, h * 128:(h + 1) * 128], kend[:, c:c + 128], v[:, c:c + 128], True, True, [kend, v], [st])
            sts.append(st)
        for i in range(nt_):
            for h in range(4):
                hs = slice(h * 128, (h + 1) * 128)
                s.STT(s.S[:, hs], s.S[:, hs], eg[:, i * 4 + h:i * 4 + h + 1], sts[i][:, hs], ALU.mult, ALU.add,
                      [s.S, eg, sts[i]], [s.S])
        if need_bf:
            s.CP(s.Sbf[:], s.S[:], [s.S], [s.Sbf], eng="act")

    def finalize_att(s, U, n, dstT, col0):
        ao = s.b1k.get()
        for h in range(4):
            rc = s.sm.get()
            s.TS(rc[0:n, 0:1], U[h][0:n, 128:129], 1e-30, None, ALU.add, None, [U[h]], [rc])
            s.T.op("dve", lambda e: e.reciprocal(out=rc[0:n, 0:1], in_=rc[0:n, 0:1]), [rc], [rc])
            s.TS(ao[0:n, h * 128:(h + 1) * 128], U[h][0:n, 0:128], rc[0:n, 0:1], None, ALU.mult, None,
                 [U[h], rc], [ao], eng=("dve" if h % 2 else "act") if False else "dve")
        s.transposes(ao, n, 4, dstT, col0)

    def attention_tile(s, t, col0, depth=3):
        T = s.T
        U = [s.psA.get() for _ in range(4)]
        blocks = [(g, d) for g in range(3) for d in range(NDEL[g]) if t - d >= s.first_g[g]]
        nb = len(blocks)
        pend = []

        def issue(bi):
            g, d = blocks[bi]
            ks = s.kslot.get(); vs = s.vslot.get()
            bidx = t - d - s.first_g[g]
            T.dma("sp", ks[:], s.Kd[g][bidx], reads=[s.Kd[g]], writes=[ks])
            T.dma("sp", vs[:], s.Vd[g][bidx], reads=[s.Vd[g]], writes=[vs])
            sp_ = s.psB.get()
            for h in range(4):
                hs = slice(h * 128, (h + 1) * 128)
                s.MM(sp_[:, hs], ks[:, hs], s.Qfm[:, g * 4 + h, col0:col0 + 128], True, True, [ks, s.Qfm], [sp_])
            pend.append((bi, g, d, sp_, vs))

        def retire():
            bi, g, d, sp_, vs = pend.pop(0)
            P = s.f2k.get()
            s.ACT(P[:], sp_[:, 0:512], AF.Exp, [sp_], [P], scale=SCALE)
            Pm = s.b1k.get()
            mo = (s.blk0[g] + d) * 512
            s.TT(Pm[:], P[:], s.Mtab[:, mo:mo + 512], ALU.mult, [P, s.Mtab], [Pm])
            for h in range(4):
                s.MM(U[h][:, 0:129], Pm[:, h * 128:(h + 1) * 128], vs[:, h * 130:h * 130 + 129],
                     bi == 0, bi == nb - 1, [Pm, vs], [U[h]])

        for bi in range(nb):
            issue(bi)
            if len(pend) > depth:
                retire()
        while pend:
            retire()
        s.finalize_att(U, 128, s.attT, col0)

    def mem_att_tile(s, col0, n=128):
        U = [s.psA.get() for _ in range(4)]
        for hp in range(2):
            sp_ = s.psB.get()
            for hh in range(2):
                h = 2 * hp + hh
                for blk in range(2):
                    c = (hh * 2 + blk) * 128
                    s.MM(sp_[:, c:c + n], s.memK[:, h, blk * 128:(blk + 1) * 128], s.MQfm[:, h, col0:col0 + n],
                         True, True, [s.memK, s.MQfm], [sp_])
            Pm = s.b1k.get()
            s.ACT(Pm[:].rearrange("p (a q) -> p a q", q=128)[:, :, 0:n],
                  sp_[:, 0:512].rearrange("p (a q) -> p a q", q=128)[:, :, 0:n], AF.Exp, [sp_], [Pm], scale=SCALE)
            for hh in range(2):
                h = 2 * hp + hh
                for blk in range(2):
                    c = (hh * 2 + blk) * 128
                    s.MM(U[h][0:n, 0:129], Pm[:, c:c + n], s.memV[:, blk, h * 130:h * 130 + 129],
                         blk == 0, blk == 1, [Pm, s.memV], [U[h]])
        s.finalize_att(U, n, s.memT, col0)

    def mem_kv(s):
        T = s.T
        hT = s.hT.get()
        for i in range(2):
            s.front(s.mem_in, i * 128, 128, hT, i * 128)
        s.stop_at("mk_front")
        wk = s.wload(s.wb_memkv, 0, 8, 0, 512)
        wv = s.wload(s.wb_memkv, 0, 8, 512, 512)
        for h in range(4):
            ps = s.proj_fm(wk, 8, h * 128, hT, 256)
            s.CP(s.memK[:, h, :], ps[:, 0:256], [ps], [s.memK], eng="act")
        s.stop_at("mk_fm")
        for i in range(2):
            st = s.stage.get()
            pk = s.proj_tm(wk, 8, 512, hT, i * 128, 128, False)
            s.CP(st[:, 0:512], pk[:, 0:512], [pk], [st], eng="act")
            s.stop_at("mk_a")
            pv = s.proj_tm(wv, 8, 512, hT, i * 128, 128, False)
            s.CP(st[:, 512:1024], pv[:, 0:512], [pv], [st], eng="dve")
            mv = s.memV[:, i, :].rearrange("p (h c) -> p h c", c=130)
            s.CP(mv[:, :, 0:128], pv[:, 0:512].rearrange("p (h c) -> p h c", c=128), [pv], [s.memV], eng="act")
            s.stop_at("mk_b")
            T.op("dve", lambda e: e.memset(mv[:, :, 128:130], 1.0), [], [s.memV])
            s.stop_at("mk_c")
            T.dma("pool", s.o_pmem[i * 128:(i + 1) * 128, :], st[:], reads=[st], writes=[s.o_pmem])

    def mixer_out_multi(s, hT, tl, h2T):
        nt_ = len(tl)
        merged = [s.f4k.get() for _ in tl]
        branches = [(s.attT, C_GA, 0), (s.hgT, C_GH, 1), (s.memT, C_GM, 2)]
        for bi, (bT, cg, wi) in enumerate(branches):
            for half in range(2):
                hs = slice(half * 512, (half + 1) * 512)
                wg = s.wload(s.wb_in, 0, 8, cg + half * 512, 512, bias_c0=cg + half * 512)
                wb = s.wload(s.wb_br[wi], 0, 4, half * 512, 512)
                gp = [s.proj_tm(wg, 8, 512, hT, c0, n, True) for (c0, n, _) in tl]
                bp = [s.proj_tm(wb, 4, 512, bT, c0, n, False) for (c0, n, _) in tl]
                for i, (c0, n, _) in enumerate(tl):
                    sg = s.f2k.get()
                    s.ACT(sg[0:n, :], gp[i][0:n, 0:512], AF.Sigmoid, [gp[i]], [sg])
                    if bi == 0:
                        s.TT(merged[i][0:n, hs], sg[0:n, :], bp[i][0:n, 0:512], ALU.mult, [sg, bp[i]], [merged[i]])
                    else:
                        s.TT(sg[0:n, :], sg[0:n, :], bp[i][0:n, 0:512], ALU.mult, [sg, bp[i]], [sg])
                        s.TT(merged[i][0:n, hs], merged[i][0:n, hs], sg[0:n, :], ALU.add, [merged[i], sg], [merged[i]])
        wo = [s.wload(s.wb_out, 0, 8, half * 512, 512) for half in range(2)]
        for i, (col0, n, xt) in enumerate(tl):
            mb = s.b2k.get()
            s.CP(mb[0:n, :], merged[i][0:n, :], [merged[i]], [mb], eng="act")
            mT = s.b2k.get()
            mT3 = mT[:].rearrange("p (c t) -> p c t", t=128)
            ps = s.psB.get()
            pb = ps.t[:].bitcast(BF16)
            for c in range(KC):
                s.TR(pb[:, c * 128:c * 128 + n], mb[0:n, c * 128:(c + 1) * 128], s.idb[0:n, 0:n], [mb, s.idb], [ps])
            s.CP(mT3[:, :, 0:n], pb[:, 0:1024].rearrange("p (c t) -> p c t", t=128)[:, :, 0:n], [ps], [mT], eng="dve")
            mix = [None, None]
            for half in range(2):
                ps2 = s.psA.get()
                for kc in range(KC):
                    s.MM(ps2[0:n, 0:512], mT3[:, kc, 0:n], wo[half][:, kc, 0:512], kc == 0, kc == KC - 1,
                         [mT, wo[half]], [ps2], inc=(kc == KC - 1))
                mix[half] = ps2
            mx = merged[i]
            s.CP(mx[0:n, 0:512], mix[0][0:n, 0:512], [mix[0]], [mx], eng="act")
            s.CP(mx[0:n, 512:1024], mix[1][0:n, 0:512], [mix[1]], [mx], eng="dve")
            ss = s.rstd_of(mx[0:n, :], [mx], n, D)
            s.STT(mx[0:n, :], mx[0:n, :], ss[0:n, 0:1], s.gpost_mix[0:n, :], ALU.mult, ALU.mult,
                  [mx, ss, s.gpost_mix], [mx])
            s.TT(xt[0:n, :], xt[0:n, :], mx[0:n, :], ALU.add, [xt, mx], [xt])
            ss2 = s.rstd_of(xt[0:n, :], [xt], n, D)
            h2 = s.b2k.get()
            s.TS(h2[0:n, :], xt[0:n, :], ss2[0:n, 0:1], None, ALU.mult, None, [xt, ss2], [h2])
            s.transposes(h2, n, KC, h2T, col0)

    def ffn_group(s, h2T, ntok, xts, out_rows, carry_in, carry_out, tok_view=None):
        T = s.T
        nt = (ntok + 127) // 128
        Y = [s.psA.get() for _ in range(2 * nt)]
        pend = []

        def down(c, cc, pab, wd):
            pa = pab;
            ae = s.aext.get()
            w0 = s.ccol[:, c:c + 1]; w1 = s.ccol[:, 32 + c:33 + c]; w2 = s.ccol[:, 64 + c:65 + c]
            cb = s.ccol[:, 96 + c:97 + c]
            cv = s.f2k.get()
            if tok_view is None:
                s.CP(ae[:, 0:2], carry_in(c), [s.carry], [ae], eng="dve")
                s.CP(ae[:, 2:2 + ntok], pa[:, 0:ntok], [pa], [ae], eng="act")
                s.CP(carry_out(c), ae[:, ntok:ntok + 2], [ae], [s.carry], eng="dve")
                a0 = ae[:, 0:ntok]; a1 = ae[:, 1:1 + ntok]; a2 = ae[:, 2:2 + ntok]; co = cv[:, 0:ntok]
            else:
                nsq, TT_ = tok_view
                av = ae[:, 0:nsq * (TT_ + 2)].rearrange("p (b j) -> p b j", j=TT_ + 2)
                s.CP(av[:, :, 0:2], carry_in(c), [s.carry_s], [ae], eng="dve")
                s.CP(av[:, :, 2:2 + TT_], pa[:, 0:ntok].rearrange("p (b j) -> p b j", j=TT_), [pa], [ae], eng="act")
                a0 = av[:, :, 0:TT_]; a1 = av[:, :, 1:1 + TT_]; a2 = av[:, :, 2:2 + TT_]
                co = cv[:, 0:ntok].rearrange("p (b j) -> p b j", j=TT_)
            s.TS(co, a2, w2, cb, ALU.mult, ALU.add, [ae, s.ccol], [cv])
            s.STT(co, a1, w1, co, ALU.mult, ALU.add, [ae, s.ccol, cv], [cv])
            s.STT(co, a0, w0, co, ALU.mult, ALU.add, [ae, s.ccol, cv], [cv])
            sl = s.f2k.get()
            s.ACT(sl[:, 0:ntok], cv[:, 0:ntok], AF.Silu, [cv], [sl])
            gt = s.b1k.get()
            s.TT(gt[:, 0:ntok], sl[:, 0:ntok], pab[:, 256:256 + ntok], ALU.mult, [sl, pab], [gt])
            for ti in range(nt):
                n = min(128, ntok - ti * 128)
                for half in range(2):
                    s.MM(Y[2 * ti + half][0:n, 0:512], gt[:, ti * 128:ti * 128 + n], wd[:, half * 4 + cc, :],
                         c == 0, c == 31, [gt, wd], [Y[2 * ti + half]])

        for c4 in range(8):
            wa = s.wload(s.wb_a, 0, 8, c4 * 512, 512)
            wb = s.wload(s.wb_b, 0, 8, c4 * 512, 512)
            wd = s.wpool.get()
            T.dma("sp", wd[:, 0:8, :], s.wb_d[c4], reads=[s.wb_d], writes=[wd])
            for cc in range(4):
                c = c4 * 4 + cc
                pab = s.psB.get()
                for kc in range(KC):
                    s.MM(pab[:, 0:ntok], wa[:, kc, cc * 128:(cc + 1) * 128], h2T[:, kc, 0:ntok], kc == 0, kc == KC - 1,
                         [wa, h2T], [pab], inc=(kc == KC - 1))
                for kc in range(KC):
                    s.MM(pab[:, 256:256 + ntok], wb[:, kc, cc * 128:(cc + 1) * 128], h2T[:, kc, 0:ntok], kc == 0,
                         kc == KC - 1, [wb, h2T], [pab], inc=(kc == KC - 1))
                pend.append((c, cc, pab, wd))
                if len(pend) > 2:
                    down(*pend.pop(0))
        while pend:
            down(*pend.pop(0))
        for ti in range(nt):
            n = min(128, ntok - ti * 128)
            yt = s.f4k.get()
            s.CP(yt[0:n, 0:512], Y[2 * ti][0:n, 0:512], [Y[2 * ti]], [yt], eng="act")
            s.CP(yt[0:n, 512:1024], Y[2 * ti + 1][0:n, 0:512], [Y[2 * ti + 1]], [yt], eng="dve")
            ss = s.rstd_of(yt[0:n, :], [yt], n, D)
            s.STT(yt[0:n, :], yt[0:n, :], ss[0:n, 0:1], s.gpost_ffn[0:n, :], ALU.mult, ALU.mult,
                  [yt, ss, s.gpost_ffn], [yt])
            xt = xts[ti]
            s.TT(yt[0:n, :], yt[0:n, :], xt[0:n, :], ALU.add, [yt, xt], [yt])
            dst, r0 = out_rows(ti)
            if dst is not None:
                T.dma("pool", dst[r0:r0 + n, :], yt[0:n, :], reads=[yt], writes=[dst])

    def a_rows(s, h2T, cols, n, dstbuf, rowsel):
        T = s.T
        for c4 in range(8):
            wa = s.wload(s.wb_a, 0, 8, c4 * 512, 512)
            ps = s.psA.get()
            for kc in range(KC):
                s.MM(ps[0:n, 0:512], h2T[:, kc, cols], wa[:, kc, 0:512], kc == 0, kc == KC - 1, [h2T, wa], [ps],
                     inc=(kc == KC - 1))
            st = s.f2k.get()
            s.CP(st[0:n, :], ps[0:n, 0:512], [ps], [st], eng="act")
            for (d0, s0, nr) in rowsel:
                T.dma("pool", dstbuf[d0:d0 + nr, c4 * 512:(c4 + 1) * 512], st[s0:s0 + nr, :], reads=[st], writes=[dstbuf])

    def a_carry(s, h2T, c0, scale_col):
        for c4 in range(8):
            wa = s.wload(s.wb_a, 0, 8, c4 * 512, 512)
            ps = s.psB.get()
            for cc in range(4):
                for kc in range(KC):
                    s.MM(ps[:, 2 * cc:2 * cc + 2], wa[:, kc, cc * 128:(cc + 1) * 128], h2T[:, kc, c0:c0 + 2],
                         kc == 0, kc == KC - 1, [wa, h2T], [ps], inc=(kc == KC - 1))
            s.TS(s.carry[:, c4 * 4:(c4 + 1) * 4, :], ps[:, 0:8].rearrange("p (c j) -> p c j", j=2), scale_col, None,
                 ALU.mult, None, [ps, s.flg], [s.carry])

    def prompt(s):
        T = s.T
        T.op("pool", lambda e: e.memset(s.S[:], 0.0), [], [s.S])
        T.op("pool", lambda e: e.memset(s.Sbf[:], 0.0), [], [s.Sbf])
        for gx in s.Gext.bufs:
            T.op("pool", lambda e: e.memset(gx[:], 0.0), [], [gx])
        kinds = ["pre"] * s.npre + ["halo"] * s.nhalo + ["ovl"] + ["main"] * s.nmain
        sts = []
        t = 0
        while t < s.NT:
            k = kinds[t]
            n = 1 if k == "ovl" else s.stn
            grp = [u for u in range(t, min(t + n, s.NT)) if kinds[u] == k]
            sts.append((grp, k))
            t += len(grp)
        nlight = sum(1 for _, k in sts if k in ("pre", "halo"))
        per = (204 + max(nlight, 1) - 1) // max(nlight, 1) + 1
        done_mem = False
        for si, (grp, k) in enumerate(sts):
            if k in ("pre", "halo"):
                s.run_casts(per)
            elif not done_mem:
                s.run_casts(None)
                s.mem_kv()
                done_mem = True
            s.do_st(grp, k)
            s.stop_at(f"st{si}")
        T.dma("pool", s.o_phg[:].rearrange("h k v -> k h v"), s.S[:].rearrange("p (h v) -> p h v", h=4),
              reads=[s.S], writes=[s.o_phg])

    def do_st(s, tiles, kind):
        T = s.T
        nt_ = len(tiles)
        NTOK = nt_ * 128
        mainlike = kind in ("ovl", "main")
        hT = s.hT.get()
        xts, vms = [], []
        for i, t in enumerate(tiles):
            vm = s.vmpool.get()
            T.dma("sp", vm[:, 0:1], s.vmask[t * 128:(t + 1) * 128, :], writes=[vm])
            vms.append(vm)
            xts.append(s.front(s.x_ext, t * 128, 128, hT, i * 128))
        if mainlike:
            w = s.wload(s.wb_in, 0, 8, C_HQ, 512)
            for c in range(4):
                ps = s.proj_fm(w, 8, c * 128, hT, NTOK)
                s.ACT(s.qfm[:, c, 0:NTOK], ps[:, 0:NTOK], AF.Identity, [ps, s.bcol], [s.qfm],
                      bias=s.bcol[:, C_HQ // 128 + c:C_HQ // 128 + c + 1])
            w = s.wload(s.wb_in, 0, 8, C_HF, 512)
            for c in range(4):
                ps = s.proj_fm(w, 8, c * 128, hT, NTOK)
                sg = s.f2k.get()
                s.ACT(sg[:, 0:NTOK], ps[:, 0:NTOK], AF.Sigmoid, [ps, s.bcol], [sg],
                      bias=s.bcol[:, C_HF // 128 + c:C_HF // 128 + c + 1])
                s.TS(s.kTfm[:, c, 0:NTOK], sg[:, 0:NTOK], s.lbcol[:, 8 + c:9 + c], s.lbcol[:, 4 + c:5 + c],
                     ALU.mult, ALU.add, [sg, s.lbcol], [s.kTfm])
        whf = s.wload(s.wb_in, 0, 8, C_HF, 512, bias_c0=C_HF)
        whi = s.wload(s.wb_in, 0, 8, C_HI, 512, bias_c0=C_HI)
        whg = s.wload(s.wb_in, 0, 8, C_HG, 512, bias_c0=C_HG) if mainlike else None
        if not mainlike:
            hfs = [s.proj_tm(whf, 8, 512, hT, i * 128, 128, True) for i in range(nt_)]
            his = [s.proj_tm(whi, 8, 512, hT, i * 128, 128, True) for i in range(nt_)]
            s.hgrn_state_multi(nt_, vms, hfs, his, need_bf=(tiles[-1] >= s.T0 - 1))
        for i, t in enumerate(tiles):
            if not mainlike:
                break
            hf_ps = s.proj_tm(whf, 8, 512, hT, i * 128, 128, True)
            hi_ps = s.proj_tm(whi, 8, 512, hT, i * 128, 128, True)
            hg_ps = s.proj_tm(whg, 8, 512, hT, i * 128, 128, True) if mainlike else None
            s.hgrn_tile(i * 128, 128, vms[i], hf_ps, hi_ps, hg_ps, mainlike, need_bf=(t >= s.T0 - 1))
        for g in range(3):
            need = [t >= s.first_g[g] for t in tiles]
            if not any(need):
                continue
            nout = min(W_G[g] // 128, s.nmain)
            wk = s.wload(s.wb_in, 0, 8, C_K + g * 512, 512, bias_c0=C_K + g * 512)
            wv = s.wload(s.wb_in, 0, 8, C_V + g * 512, 512, bias_c0=C_V + g * 512)
            kb = [s.b1k.get() for _ in tiles]
            for h in range(4):
                ps = s.proj_fm(wk, 8, h * 128, hT, NTOK)
                for i, t in enumerate(tiles):
                    if need[i]:
                        s.ACT(kb[i][:, h * 128:(h + 1) * 128], ps[:, i * 128:(i + 1) * 128], AF.Identity,
                              [ps, s.bcol], [kb[i]],
                              bias=s.bcol[:, C_K // 128 + g * 4 + h:C_K // 128 + g * 4 + h + 1])
            for i, t in enumerate(tiles):
                if not need[i]:
                    continue
                bidx = t - s.first_g[g]
                T.dma("pool", s.Kd[g][bidx], kb[i][:], reads=[kb[i]], writes=[s.Kd[g]])
                pv = s.proj_tm(wv, 8, 512, hT, i * 128, 128, True)
                vb = s.vslot.get()
                v3 = vb[:].rearrange("p (h c) -> p h c", c=130)
                s.TS(v3[:, :, 0:128], pv[:, 0:512].rearrange("p (h c) -> p h c", c=128), vms[i][:, 0:1], None,
                     ALU.mult, None, [pv, vms[i]], [vb])
                s.CP(v3[:, :, 128:130], vms[i][:, 0:1].unsqueeze(1).to_broadcast([128, 4, 2]), [vms[i]], [vb],
                     eng="dve")
                T.dma("pool", s.Vd[g][bidx], vb[:], reads=[vb], writes=[s.Vd[g]])
                mi = t - s.T0 - 1
                if kind == "main" and mi >= s.nmain - nout:
                    st = s.stage.get()
                    s.CP(st[:, 512:1024], pv[:, 0:512], [pv], [st], eng="act")
                    pk = s.proj_tm(wk, 8, 512, hT, i * 128, 128, True)
                    s.CP(st[:, 0:512], pk[:, 0:512], [pk], [st], eng="act")
                    r0 = (mi - (s.nmain - nout)) * 128
                    T.dma("pool", s.o_pwin[g][r0:r0 + 128, :], st[:], reads=[st], writes=[s.o_pwin[g]])
        if not mainlike:
            return
        for g in range(3):
            wq = s.wload(s.wb_in, 0, 8, C_Q + g * 512, 512)
            for h in range(4):
                ps = s.proj_fm(wq, 8, h * 128, hT, NTOK)
                s.ACT(s.Qfm[:, g * 4 + h, 0:NTOK], ps[:, 0:NTOK], AF.Identity, [ps, s.bcol], [s.Qfm],
                      bias=s.bcol[:, g * 4 + h:g * 4 + h + 1])
        wq = s.wload(s.wb_in, 0, 8, C_MQ, 512)
        for h in range(4):
            ps = s.proj_fm(wq, 8, h * 128, hT, NTOK)
            s.ACT(s.MQfm[:, h, 0:NTOK], ps[:, 0:NTOK], AF.Identity, [ps, s.bcol], [s.MQfm],
                  bias=s.bcol[:, C_MQ // 128 + h:C_MQ // 128 + h + 1])
        h2T = s.h2T.get()
        for i, t in enumerate(tiles):
            s.attention_tile(t, i * 128)
            s.mem_att_tile(i * 128)
        s.mixer_out_multi(hT, [(i * 128, 128, xts[i]) for i in range(nt_)], h2T)
        if kind == "ovl":
            s.a_carry(h2T, 126, s.flg[:, 0:1])
            return
        m0 = tiles[0] - s.T0 - 1

        def out_rows(ti):
            return s.o_y, (m0 + ti) * 128

        s.ffn_group(h2T, NTOK, xts, out_rows, lambda c: s.carry[:, c, :], lambda c: s.carry[:, c, :])
        if tiles[-1] == s.NT - 1:
            s.a_rows(h2T, slice(NTOK - 2, NTOK), 2, s.o_pconv, [(0, 0, 2)])

    def cache_block(s, src_ap, srcbuf):
        T = s.T
        ct = s.stage.get()
        T.dma("sp", ct[:], src_ap, reads=[srcbuf], writes=[ct])
        kbf = s.b1k.get()
        s.CP(kbf[:], ct[:, 0:512], [ct], [kbf], eng="dve")
        ps = s.psB.get()
        pb = ps.t[:].bitcast(BF16)
        for h in range(4):
            s.TR(pb[:, h * 128:(h + 1) * 128], kbf[:, h * 128:(h + 1) * 128], s.idb[:], [kbf, s.idb], [ps])
        ks = s.kslot.get()
        s.CP(ks[:], pb[:, 0:512], [ps], [ks], eng="act")
        vs = s.vslot.get()
        v3 = vs[:].rearrange("p (h c) -> p h c", c=130)
        s.CP(v3[:, :, 0:128], ct[:, 512:1024].rearrange("p (h c) -> p h c", c=128), [ct], [vs], eng="dve")
        T.op("dve", lambda e: e.memset(v3[:, :, 128:130], 1.0), [], [vs])
        return ks, vs

    def sample(s):
        T = s.T
        n = 32
        h4 = lambda ap: ap.rearrange("p (h t) -> p h t", h=4)
        hT = s.hT.get()
        xt = s.front(s.xs_in, 0, n, hT, 0)
        for g in range(3):
            for b in range(4):
                T.dma("pool", s.o_swin[g][b, 0:W_G[g] - 8, :], s.cwin[g][b, 8:W_G[g], :], reads=[s.cwin[g]],
                      writes=[s.o_swin[g]])
        w = s.wload(s.wb_in, 0, 8, C_HQ, 512)
        for c in range(4):
            ps = s.proj_fm(w, 8, c * 128, hT, n)
            s.ACT(s.qfm[:, c, 0:n], ps[:, 0:n], AF.Identity, [ps, s.bcol], [s.qfm],
                  bias=s.bcol[:, C_HQ // 128 + c:C_HQ // 128 + c + 1])
        w = s.wload(s.wb_in, 0, 8, C_HF, 512)
        for c in range(4):
            ps = s.proj_fm(w, 8, c * 128, hT, n)
            sg = s.f2k.get()
            s.ACT(sg[:, 0:n], ps[:, 0:n], AF.Sigmoid, [ps, s.bcol], [sg],
                  bias=s.bcol[:, C_HF // 128 + c:C_HF // 128 + c + 1])
            s.TS(s.kTfm[:, c, 0:n], sg[:, 0:n], s.lbcol[:, 8 + c:9 + c], s.lbcol[:, 4 + c:5 + c],
                 ALU.mult, ALU.add, [sg, s.lbcol], [s.kTfm])
        whf = s.wload(s.wb_in, 0, 8, C_HF, 512, bias_c0=C_HF)
        whi = s.wload(s.wb_in, 0, 8, C_HI, 512, bias_c0=C_HI)
        whg = s.wload(s.wb_in, 0, 8, C_HG, 512, bias_c0=C_HG)
        hf_ps = s.proj_tm(whf, 8, 512, hT, 0, n, True)
        hi_ps = s.proj_tm(whi, 8, 512, hT, 0, n, True)
        hg_ps = s.proj_tm(whg, 8, 512, hT, 0, n, True)
        sS = T.sb([128, 4, 512], F32, "sS"); sSb = T.sb([128, 4, 512], BF16, "sSb")
        for b in range(4):
            T.dma("sp", sS[:, b, :].rearrange("p (h v) -> p h v", h=4), s.shg_in[b].rearrange("h k v -> k h v"),
                  writes=[sS])
        s.CP(sSb[:], sS[:], [sS], [sSb], eng="act")
        sg = s.f2k.get()
        s.ACT(sg[0:n, :], hf_ps[0:n, 0:512], AF.Sigmoid, [hf_ps], [sg])
        k = s.f2k.get()
        s.TT(k[0:n, :], sg[0:n, :], s.noml_tm[0:n, :], ALU.mult, [sg, s.noml_tm], [k])
        s.TT(k[0:n, :], k[0:n, :], s.oml_tm[0:n, :], ALU.add, [k, s.oml_tm], [k])
        g_ = s.f2k.get()
        s.ACT(g_[0:n, :], k[0:n, :], AF.Ln, [k], [g_], scale=-1.0, bias=1.0)
        R = s.psB.get()
        s.MM(R[0:n, 0:512], s.tri_up_s[0:n, 0:n], g_[0:n, :], True, True, [s.tri_up_s, g_], [R])
        eR = s.f2k.get()
        s.ACT(eR[0:n, :], R[0:n, 0:512], AF.Exp, [R], [eR])
        kend = s.f2k.get()
        s.TT(kend[0:n, :], k[0:n, :], eR[0:n, :], ALU.mult, [k, eR], [kend])
        v = s.hv
        s.CP(v[0:n, :], hi_ps[0:n, 0:512], [hi_ps], [v], eng="act")
        ge = s.psB.get()
        for b in range(4):
            for h in range(4):
                s.MM(ge[:, b * 4 + h:b * 4 + h + 1], g_[0:n, h * 128:(h + 1) * 128], s.rowmask[0:n, b:b + 1],
                     True, True, [g_, s.rowmask], [ge])
        eg = s.sm.get()
        s.ACT(eg[:, 0:16], ge[:, 0:16], AF.Exp, [ge], [eg])
        for b in range(4):
            kb_ = s.b1k.get()
            s.TS(kb_[0:n, :], kend[0:n, :], s.rowmask[0:n, b:b + 1], None, ALU.mult, None, [kend, s.rowmask], [kb_])
            st = s.psB.get()
            for h in range(4):
                hs = slice(h * 128, (h + 1) * 128)
                s.MM(st[:, hs], kb_[0:n, hs], v[0:n, hs], True, True, [kb_, v], [st])
            for h in range(4):
                hs = slice(h * 128, (h + 1) * 128)
                s.STT(sS[:, b, hs], sS[:, b, hs], eg[:, b * 4 + h:b * 4 + h + 1], st[:, hs], ALU.mult, ALU.add,
                      [sS, eg, st], [sS])
            T.dma("pool", s.o_shg[b].rearrange("h k v -> k h v"), sS[:, b, :].rearrange("p (h v) -> p h v", h=4),
                  reads=[sS], writes=[s.o_shg])
        GT = s.psB.get()
        for h in range(4):
            s.MM(GT[:, h * 128:h * 128 + n], g_[0:n, h * 128:(h + 1) * 128], s.tri_incl_s[0:n, 0:n], True, True,
                 [g_, s.tri_incl_s], [GT])
        Gs = s.f2k.get()
        s.CP(h4(Gs[:])[:, :, 0:n], h4(GT[:, 0:512])[:, :, 0:n], [GT], [Gs], eng="act")
        eG = s.f2k.get()
        s.ACT(h4(eG[:])[:, :, 0:n], h4(Gs[:])[:, :, 0:n], AF.Exp, [Gs], [eG])
        qhat = s.hqhat
        s.TT(h4(qhat[:])[:, :, 0:n], s.qfm[:, :, 0:n], h4(eG[:])[:, :, 0:n], ALU.mult, [s.qfm, eG], [qhat])
        enG = s.f2k.get()
        s.ACT(h4(enG[:])[:, :, 0:n], h4(Gs[:])[:, :, 0:n], AF.Exp, [Gs], [enG], scale=-1.0)
        kt = s.hqtil
        s.TT(h4(kt[:])[:, :, 0:n], s.kTfm[:, :, 0:n], h4(enG[:])[:, :, 0:n], ALU.mult, [s.kTfm, enG], [kt])
        sc = s.psB.get()
        for h in range(4):
            s.MM(sc[0:n, h * 128:h * 128 + n], kt[:, h * 128:h * 128 + n], qhat[:, h * 128:h * 128 + n], True, True,
                 [kt, qhat], [sc])
        scT = s.b1k.get()
        s.TT(h4(scT[0:n, :])[:, :, 0:n], h4(sc[0:n, 0:512])[:, :, 0:n],
             s.causal_s[0:n, 0:n].unsqueeze(1).to_broadcast([n, 4, n]), ALU.mult, [sc, s.causal_s], [scT])
        qb = s.b1k.get()
        T.op("pool", lambda e: e.memset(qb[:], 0.0), [], [qb])
        qb4 = qb[:].rearrange("p (b h t) -> p b h t", b=4, h=4)
        for b in range(4):
            s.CP(qb4[:, b, :, 8 * b:8 * b + 8], h4(qhat[:])[:, :, 8 * b:8 * b + 8], [qhat], [qb], eng="dve")
        o = s.psA.get()
        for h in range(4):
            hs = slice(h * 128, (h + 1) * 128)
            s.MM(o[0:n, hs], scT[0:n, h * 128:h * 128 + n], v[0:n, hs], True, False, [scT, v], [o])
            for b in range(4):
                s.MM(o[0:n, hs], qb4[:, b, h, :], sSb[:, b, hs], False, b == 3, [qb, sSb], [o])
        o2 = s.f2k.get()
        s.ACT(o2[0:n, :], o[0:n, 0:512], AF.Square, [o], [o2])
        ssq = s.sm.get()
        T.op("dve", lambda e: e.tensor_reduce(out=ssq[0:n, 0:4], in_=h4(o2[0:n, :]), axis=AX.X, op=ALU.add),
             [o2], [ssq])
        s.TS(ssq[0:n, 0:4], ssq[0:n, 0:4], 1.0 / 128, EPS, ALU.mult, ALU.add, [ssq], [ssq])
        s.ACT(ssq[0:n, 0:4], ssq[0:n, 0:4], AF.Sqrt, [ssq], [ssq])
        T.op("dve", lambda e: e.reciprocal(out=ssq[0:n, 0:4], in_=ssq[0:n, 0:4]), [ssq], [ssq])
        sgt = s.f2k.get()
        s.ACT(sgt[0:n, :], hg_ps[0:n, 0:512], AF.Sigmoid, [hg_ps], [sgt])
        s.TT(sgt[0:n, :], sgt[0:n, :], s.ghg4[0:n, :], ALU.mult, [sgt, s.ghg4], [sgt])
        on = s.f2k.get()
        s.TT(h4(on[0:n, :]), h4(o[0:n, 0:512]), ssq[0:n, 0:4].unsqueeze(2).to_broadcast([n, 4, 128]), ALU.mult,
             [o, ssq], [on])
        hg = s.b1k.get()
        s.TT(hg[0:n, :], on[0:n, :], sgt[0:n, :], ALU.mult, [on, sgt], [hg])
        s.transposes(hg, n, 4, s.hgT, 0, eng="dve")
        Knew = s.Knew
        Vnew = T.sb([128, 3, 520], BF16, "Vnew")
        for g in range(3):
            wq = s.wload(s.wb_in, 0, 8, C_Q + g * 512, 512)
            for h in range(4):
                ps = s.proj_fm(wq, 8, h * 128, hT, n)
                s.ACT(s.Qfm[:, g * 4 + h, 0:n], ps[:, 0:n], AF.Identity, [ps, s.bcol], [s.Qfm],
                      bias=s.bcol[:, g * 4 + h:g * 4 + h + 1])
            wk = s.wload(s.wb_in, 0, 8, C_K + g * 512, 512, bias_c0=C_K + g * 512)
            wv = s.wload(s.wb_in, 0, 8, C_V + g * 512, 512, bias_c0=C_V + g * 512)
            for h in range(4):
                ps = s.proj_fm(wk, 8, h * 128, hT, n)
                s.ACT(Knew[:, (g * 4 + h) * 32:(g * 4 + h) * 32 + n], ps[:, 0:n], AF.Identity, [ps, s.bcol], [Knew],
                      bias=s.bcol[:, C_K // 128 + g * 4 + h:C_K // 128 + g * 4 + h + 1])
            pv = s.proj_tm(wv, 8, 512, hT, 0, n, True)
            v3 = Vnew[:, g, :].rearrange("p (h c) -> p h c", c=130)
            s.CP(v3[0:n, :, 0:128], pv[0:n, 0:512].rearrange("p (h c) -> p h c", c=128), [pv], [Vnew], eng="act")
            T.op("dve", lambda e: e.memset(v3[0:n, :, 128:130], 1.0), [], [Vnew])
            st = s.stage.get()
            s.CP(st[0:n, 512:1024], pv[0:n, 0:512], [pv], [st], eng="dve")
            pk = s.proj_tm(wk, 8, 512, hT, 0, n, True)
            s.CP(st[0:n, 0:512], pk[0:n, 0:512], [pk], [st], eng="act")
            for b in range(4):
                T.dma("pool", s.o_swin[g][b, W_G[g] - 8:W_G[g], :], st[8 * b:8 * b + 8, :], reads=[st],
                      writes=[s.o_swin[g]])
        wq = s.wload(s.wb_in, 0, 8, C_MQ, 512)
        for h in range(4):
            ps = s.proj_fm(wq, 8, h * 128, hT, n)
            s.ACT(s.MQfm[:, h, 0:n], ps[:, 0:n], AF.Identity, [ps, s.bcol], [s.MQfm],
                  bias=s.bcol[:, C_MQ // 128 + h:C_MQ // 128 + h + 1])
        Ppad = [[T.sb([128, 128], BF16, "Ppad") for _ in range(2)] for _ in range(4)]
        for b in range(4):
            for j in range(2):
                T.op("pool", lambda e: e.memset(Ppad[b][j][:], 0.0), [], [Ppad[b][j]])
        pcnt = [0, 0, 0, 0]
        U = [s.psA.get() for _ in range(4)]
        Mv = s.Mtab[:].rearrange("p (b h q) -> p b h q", h=4, q=128)
        for g in range(3):
            sp_ = s.psB.get()
            for h in range(4):
                s.MM(sp_[0:n, h * 32:h * 32 + n], Knew[:, (g * 4 + h) * 32:(g * 4 + h) * 32 + n],
                     s.Qfm[:, g * 4 + h, 0:n], True, True, [Knew, s.Qfm], [sp_])
            P = s.f2k.get()
            s.ACT(P[0:n, 0:128], sp_[0:n, 0:128], AF.Exp, [sp_], [P], scale=SCALE)
            Pn = s.b1k.get()
            s.TT(Pn[0:n, 0:128], P[0:n, 0:128], s.Mnew[0:n, g * 128:(g + 1) * 128], ALU.mult, [P, s.Mnew], [Pn])
            for h in range(4):
                s.MM(U[h][0:n, 0:129], Pn[0:n, h * 32:h * 32 + n], Vnew[0:n, g, h * 130:h * 130 + 129],
                     g == 0, False, [Pn, Vnew], [U[h]])
        todo = [(b, g, d) for b in range(4) for g in range(3) for d in range(1, NDEL[g])]
        pend = []

        def issue(ci):
            b, g, d = todo[ci]
            beta = W_G[g] // 128 - d
            ks, vs = s.cache_block(s.cwin[g][b, beta * 128:(beta + 1) * 128, :], s.cwin[g])
            sp_ = s.psB.get()
            for h in range(4):
                s.MM(sp_[:, h * 8:h * 8 + 8], ks[:, h * 128:(h + 1) * 128], s.Qfm[:, g * 4 + h, 8 * b:8 * b + 8],
                     True, True, [ks, s.Qfm], [sp_])
            pend.append((ci, b, g, d, sp_, vs))

        def retire():
            ci, b, g, d, sp_, vs = pend.pop(0)
            P = s.f2k.get()
            s.ACT(P[:, 0:32], sp_[:, 0:32], AF.Exp, [sp_], [P], scale=SCALE)
            pp = Ppad[b][pcnt[b] % 2]; pcnt[b] += 1
            s.TT(pp[:].rearrange("p (h q) -> p h q", q=32)[:, :, 8 * b:8 * b + 8],
                 P[:, 0:32].rearrange("p (h q) -> p h q", q=8), Mv[:, s.blk0[g] + d, :, 0:8], ALU.mult,
                 [P, s.Mtab], [pp])
            for h in range(4):
                s.MM(U[h][0:n, 0:129], pp[:, h * 32:(h + 1) * 32], vs[:, h * 130:h * 130 + 129],
                     False, ci == len(todo) - 1, [pp, vs], [U[h]])

        for ci in range(len(todo)):
            issue(ci)
            if len(pend) > 1:
                retire()
        while pend:
            retire()
        s.finalize_att(U, n, s.attT, 0)
        U = [s.psA.get() for _ in range(4)]
        for b in range(4):
            for blk in range(2):
                ks, vs = s.cache_block(s.cmem[b, blk * 128:(blk + 1) * 128, :], s.cmem)
                sp_ = s.psB.get()
                for h in range(4):
                    s.MM(sp_[:, h * 8:h * 8 + 8], ks[:, h * 128:(h + 1) * 128], s.MQfm[:, h, 8 * b:8 * b + 8],
                         True, True, [ks, s.MQfm], [sp_])
                pp = Ppad[b][pcnt[b] % 2]; pcnt[b] += 1
                s.ACT(pp[:].rearrange("p (h q) -> p h q", q=32)[:, :, 8 * b:8 * b + 8],
                      sp_[:, 0:32].rearrange("p (h q) -> p h q", q=8), AF.Exp, [sp_], [pp], scale=SCALE)
                for h in range(4):
                    s.MM(U[h][0:n, 0:129], pp[:, h * 32:(h + 1) * 32], vs[:, h * 130:h * 130 + 129],
                         b == 0 and blk == 0, b == 3 and blk == 1, [pp, vs], [U[h]])
        s.finalize_att(U, n, s.memT, 0)
        h2T = s.h2T.get()
        s.mixer_out_multi(hT, [(0, n, xt)], h2T)
        s.carry_s = T.sb([128, 32, 8], F32, "carry_s")
        for pc in range(4):
            sc_ = s.f4k.get()
            T.dma("sp", sc_[0:8, :], s.sconv_in[:, pc * 1024:(pc + 1) * 1024], writes=[sc_])
            ps = s.psB.get()
            for c in range(8):
                s.TR(ps[:, c * 8:(c + 1) * 8], sc_[0:8, c * 128:(c + 1) * 128], s.idf[0:8, 0:8], [sc_, s.idf], [ps])
            s.CP(s.carry_s[:, pc * 8:(pc + 1) * 8, :], ps[:, 0:64].rearrange("p (c j) -> p c j", j=8), [ps],
                 [s.carry_s], eng="dve")
        cs4 = s.carry_s[:].rearrange("p c (b j) -> p c b j", j=2)
        s.ffn_group(h2T, n, [xt], lambda ti: (s.o_ys, 0), lambda c: cs4[:, c, :, :], None, tok_view=(4, 8))
        s.a_rows(h2T, slice(0, n), n, s.o_sconv, [(2 * b, 8 * b + 6, 2) for b in range(4)])


NPRE, NHALO, NMAIN = 32, 16, 16
_CACHE = {}


def _get_program(npre, nhalo, nmain, with_sample=True):
    key = (npre, nhalo, nmain, with_sample)
    if key not in _CACHE:
        _CACHE[key] = KB(npre, nhalo, nmain, with_sample=with_sample)
    return _CACHE[key]


def _common_inputs(inp):
    f = lambda a: np.ascontiguousarray(np.asarray(a, dtype=np.float32))
    c = _static_consts()
    m = {
        "rel_bias": f(inp["rel_bias"]), "lb_logits": f(inp["hg_lb_logits"]),
        "g_mix_pre": f(inp["norm_mix_pre"][0]), "g_mix_post": f(inp["norm_mix_post"][0]),
        "g_ffn_pre": f(inp["norm_ffn_pre"][0]), "g_ffn_post": f(inp["norm_ffn_post"][0]),
        "g_mem": f(inp["mem_norm"][0]), "g_hg": f(inp["hg_norm"][0]),
        "w_in": f(inp["w_in"][0]), "b_in": f(inp["b_in"][0]), "w_memkv": f(inp["w_mem_kv"][0]),
        "w_br0": f(inp["w_br_att"][0]), "w_br1": f(inp["w_br_hg"][0]), "w_br2": f(inp["w_br_mem"][0]),
        "w_out": f(inp["w_out"][0]), "w_a": f(inp["w_ffn_a"][0]), "w_b": f(inp["w_ffn_b"][0]),
        "w_d": f(inp["w_ffn_d"][0]), "conv_w": f(inp["ffn_conv_w"][0]), "conv_b": f(inp["ffn_conv_b"][0]),
    }
    for nm in ("identf", "antiid", "tri_incl", "tri_up", "causal", "tri_incl_s", "tri_up_s", "causal_s",
               "blockdiag_s", "rowmask_s", "onehot"):
        m["c_" + nm] = c[nm]
    return m


def _core_inputs(inp, common, seq, p0, npre, nhalo, nmain, sq0):
    f = lambda a: np.ascontiguousarray(np.asarray(a, dtype=np.float32))
    nt = npre + nhalo + 1 + nmain
    start = p0 - 128 * (npre + nhalo + 1)
    x = np.zeros((nt * 128, D), np.float32)
    lo = max(start, 0)
    x[lo - start:] = inp["x_prompt"][seq, lo:p0 + nmain * 128]
    vm = (np.arange(start, p0 + nmain * 128) >= 0).astype(np.float32)[:, None]
    m = dict(common)
    m["x_ext"] = x
    m["vmask"] = np.ascontiguousarray(vm)
    m["flags"] = np.array([[1.0 if p0 > 0 else 0.0, 0.0]], np.float32)
    m["mem_in"] = f(inp["mem_prompt"][seq])
    m["xs_in"] = f(inp["x_sample"][sq0:sq0 + 4]).reshape(32, D)
    for g, nm in enumerate(("cache_win1_kv", "cache_win2_kv", "cache_win3_kv")):
        m[f"cwin{g}"] = f(inp[nm][0, sq0:sq0 + 4]).reshape(4, W_G[g], 1024)
    m["cmem"] = f(inp["cache_mem_kv"][0, sq0:sq0 + 4]).reshape(4, 256, 1024)
    m["shg_in"] = f(inp["state_hgrn"][0, sq0:sq0 + 4])
    m["sconv_in"] = f(inp["state_ffn_conv"][0, sq0:sq0 + 4]).reshape(8, DFF)
    return m


def kernel(**inp):
    kb = _get_program(NPRE, NHALO, NMAIN)
    common = _common_inputs(inp)
    in_maps = []
    for c in range(8):
        in_maps.append(_core_inputs(inp, common, c // 4, (c % 4) * 2048, NPRE, NHALO, NMAIN, 4 * c))
    res = run_bass_kernel_spmd(kb.nc, in_maps, core_ids=list(range(8))).results
    B, S = 2, 8192
    y_p = np.zeros((B, S, D), np.float32)
    y_s = np.zeros((32, 8, D), np.float32)
    p_win = [np.zeros((1, B, W_G[g], 2, 4, 128), np.float32) for g in range(3)]
    p_hg = np.zeros((1, B, 4, 128, 128), np.float32)
    p_conv = np.zeros((1, B, 2, DFF), np.float32)
    p_mem = np.zeros((1, B, 256, 2, 4, 128), np.float32)
    s_win = [np.zeros((1, 32, W_G[g], 2, 4, 128), np.float32) for g in range(3)]
    s_hg = np.zeros((1, 32, 4, 128, 128), np.float32)
    s_conv = np.zeros((1, 32, 2, DFF), np.float32)
    for c in range(8):
        r = res[c]
        b, j = c // 4, c % 4
        y_p[b, j * 2048:(j + 1) * 2048] = r["o_y"]
        y_s[4 * c:4 * c + 4] = r["o_ys"].reshape(4, 8, D)
        for g in range(3):
            s_win[g][0, 4 * c:4 * c + 4] = r[f"o_swin{g}"].reshape(4, W_G[g], 2, 4, 128)
        s_hg[0, 4 * c:4 * c + 4] = r["o_shg"]
        s_conv[0, 4 * c:4 * c + 4] = r["o_sconv"].reshape(4, 2, DFF)
        if j == 3:
            for g in range(3):
                p_win[g][0, b] = r[f"o_pwin{g}"].reshape(W_G[g], 2, 4, 128)
            p_hg[0, b] = r["o_phg"]
            p_conv[0, b] = r["o_pconv"]
        if j == 0:
            p_mem[0, b] = r["o_pmem"].reshape(256, 2, 4, 128)
    return (y_p, y_s, p_win[0], p_win[1], p_win[2], p_hg, p_conv, p_mem,
            s_win[0], s_win[1], s_win[2], s_hg, s_conv)
```

```python
import math
import numpy as np
import concourse.bass as bass
import concourse.mybir as mybir
from concourse.ap import AP
from concourse.bass_utils import run_bass_kernel_spmd

F32 = mybir.dt.float32
BF16 = mybir.dt.bfloat16
ALU = mybir.AluOpType
AF = mybir.ActivationFunctionType
AX = mybir.AxisListType

EPOCH = 3000


class Buf:
    __slots__ = ("t", "name", "w", "r", "excl")

    def __init__(self, t, name, excl=False):
        self.t = t
        self.name = name
        self.w = None
        self.r = []
        self.excl = excl

    def __getitem__(self, k):
        return self.t[k]


class Eng:
    def __init__(self, nc, name, obj, nep, ndma):
        self.name = name
        self.obj = obj
        self.sems = [nc.alloc_semaphore(f"s_{name}_{i}") for i in range(nep)]
        self.n = 0
        self.waited = {}
        self.dslots = [[nc.alloc_semaphore(f"d_{name}_{i}"), 0] for i in range(ndma)]
        self.dn = 0

    def tag_of(self, n):
        return (self.sems[(n - 1) // EPOCH], (n - 1) % EPOCH + 1, self.name)


class Trk:
    def __init__(self, nc):
        self.nc = nc
        self.E = {
            "pe": Eng(nc, "pe", nc.tensor, 14, 0),
            "act": Eng(nc, "act", nc.scalar, 8, 0),
            "dve": Eng(nc, "dve", nc.vector, 8, 0),
            "pool": Eng(nc, "pool", nc.gpsimd, 4, 16),
            "sp": Eng(nc, "sp", nc.sync, 1, 24),
        }
        self.nbuf = 0

    def sb(self, shape, dt, name):
        self.nbuf += 1
        return Buf(self.nc.alloc_sbuf_tensor(f"{name}_{self.nbuf}", list(shape), dt), name)

    def dram(self, shape, dt, name, kind="Internal"):
        return Buf(self.nc.dram_tensor(name, list(shape), dt, kind=kind), name)

    def _wait(self, E, deps):
        for d in deps:
            if d is None:
                continue
            sem, val, src = d
            if src == "pe" and E.name == "pe":
                continue
            key = id(sem)
            if E.waited.get(key, 0) < val:
                E.obj.wait_ge(sem, val)
                E.waited[key] = val

    def _deps(self, reads, writes):
        deps = []
        for b in reads:
            deps.append(b.w)
            if b.excl:
                deps.extend(b.r)
        for b in writes:
            deps.append(b.w)
            deps.extend(b.r)
        return deps

    def op(self, eng, fn, reads=(), writes=(), inc=True):
        E = self.E[eng]
        self._wait(E, self._deps(reads, writes))
        ins = fn(E.obj)
        if inc:
            E.n += 1
            tag = E.tag_of(E.n)
            ins.then_inc(tag[0], 1)
        else:
            assert eng == "pe"
            tag = E.tag_of(E.n + 1)
        for b in reads:
            b.r.append(tag)
        for b in writes:
            b.w = tag
            b.r = []
        return ins

    def dma(self, q, out, in_, reads=(), writes=(), **kw):
        E = self.E[q]
        self._wait(E, self._deps(reads, writes))
        slot = E.dslots[E.dn % len(E.dslots)]
        E.dn += 1
        if slot[1] > 0 and E.waited.get(id(slot[0]), 0) < slot[1]:
            E.obj.wait_ge(slot[0], slot[1])
            E.waited[id(slot[0])] = slot[1]
        ins = E.obj.dma_start(out=out, in_=in_, **kw)
        slot[1] += 16
        ins.then_inc(slot[0], 16)
        tag = (slot[0], slot[1], "dma")
        for b in reads:
            b.r.append(tag)
        for b in writes:
            b.w = tag
            b.r = []
        return ins

    def finish(self):
        for q in ("sp", "pool"):
            E = self.E[q]
            for sem, val in E.dslots:
                if val > 0:
                    E.obj.wait_ge(sem, val)
        sp = self.E["sp"]
        for nm in ("pe", "act", "dve", "pool"):
            E = self.E[nm]
            if E.n > 0:
                sem, val, _ = E.tag_of(E.n)
                sp.obj.wait_ge(sem, val)


class Pool:
    def __init__(self, bufs):
        self.bufs = bufs
        self.i = 0

    def get(self):
        b = self.bufs[self.i % len(self.bufs)]
        self.i += 1
        return b


D = 1024
KC = 8
IN_COLS = 10240
DFF = 4096
W_G = (128, 512, 2048)
DIL = (1, 4, 16)
NDEL = (2, 5, 17)
C_Q, C_K, C_V, C_HQ, C_HF, C_HI, C_HG, C_MQ, C_GA, C_GH, C_GM = (
    0, 1536, 3072, 4608, 5120, 5632, 6144, 6656, 7168, 8192, 9216)
EPS = 1e-6
TABL = 2304
SCALE = 1.0 / math.sqrt(128.0)


def _rel_bucket_np(dist):
    d = np.maximum(dist, 1).astype(np.float32)
    large = 16 + (np.log(d / np.float32(16)) / np.float32(math.log(128.0)) * np.float32(16)).astype(np.int32)
    large = np.minimum(large, 31)
    return np.where(dist < 16, dist, large)


def _static_consts():
    c = {}
    c["identf"] = np.eye(128, dtype=np.float32)
    c["antiid"] = np.eye(128, dtype=np.float32)[::-1].copy()
    s = np.arange(128)
    c["tri_incl"] = (s[:, None] <= s[None, :]).astype(np.float32)
    c["tri_up"] = (s[:, None] > s[None, :]).astype(np.float32)
    c["causal"] = (s[:, None] <= s[None, :]).astype(np.float32)
    s32 = np.arange(32)
    same = (s32[:, None] // 8) == (s32[None, :] // 8)
    t32 = np.zeros((128, 128), np.float32); t32[:32, :32] = same & (s32[:, None] <= s32[None, :])
    u32 = np.zeros((128, 128), np.float32); u32[:32, :32] = same & (s32[:, None] > s32[None, :])
    c["tri_incl_s"] = t32
    c["tri_up_s"] = u32
    c["causal_s"] = t32.copy()
    bd = np.zeros((128, 128), np.float32); bd[:32, :32] = same
    c["blockdiag_s"] = bd
    rm = np.zeros((128, 4), np.float32)
    for b in range(4):
        rm[8 * b:8 * b + 8, b] = 1.0
    c["rowmask_s"] = rm
    oh = np.zeros((3, 32, TABL), np.float32)
    for g in range(3):
        i = np.arange(TABL)
        dl = i - 127
        ok = (dl >= 0) & (dl <= W_G[g]) & (dl % DIL[g] == 0)
        bk = _rel_bucket_np(np.maximum(dl, 0).astype(np.int32))
        oh[g, bk[ok], i[ok]] = 1.0
    c["onehot"] = oh
    return c


class KB:
    def __init__(self, npre, nhalo, nmain, stn=2, with_sample=True, dbg=None):
        self.DBG = dbg
        self.npre, self.nhalo, self.nmain, self.stn = npre, nhalo, nmain, stn
        self.with_sample = with_sample
        self.NT = npre + nhalo + 1 + nmain
        self.T0 = npre + nhalo
        self.first_g = [self.T0 - min(nhalo, NDEL[g] - 1) for g in range(3)]
        self.ntile_g = [self.NT - self.first_g[g] for g in range(3)]
        nc = self.nc = bass.Bass("TRN2", target_bir_lowering=False)
        T = self.T = Trk(nc)
        self.declare_io()
        self.alloc()
        try:
            self.setup()
            self.stop_at("setup")
            self.prompt()
            self.stop_at("prompt")
            if with_sample:
                self.sample()
        except StopIteration:
            pass
        T.finish()

    def declare_io(s):
        T = s.T
        I = lambda n, sh: T.dram(sh, F32, n, kind="ExternalInput")
        O = lambda n, sh: T.dram(sh, F32, n, kind="ExternalOutput")
        NT = s.NT
        s.x_ext = I("x_ext", [NT * 128, D])
        s.vmask = I("vmask", [NT * 128, 1])
        s.flags = I("flags", [1, 2])
        s.mem_in = I("mem_in", [256, D])
        s.xs_in = I("xs_in", [32, D])
        s.cwin = [I(f"cwin{g}", [4, W_G[g], 1024]) for g in range(3)]
        s.cmem = I("cmem", [4, 256, 1024])
        s.shg_in = I("shg_in", [4, 4, 128, 128])
        s.sconv_in = I("sconv_in", [8, DFF])
        s.rel_bias = I("rel_bias", [32, 12])
        s.lb_logits = I("lb_logits", [2, 512])
        s.g_mix_pre = I("g_mix_pre", [D]); s.g_mix_post = I("g_mix_post", [D])
        s.g_ffn_pre = I("g_ffn_pre", [D]); s.g_ffn_post = I("g_ffn_post", [D])
        s.g_mem = I("g_mem", [D]); s.g_hg = I("g_hg", [128])
        s.w_in = I("w_in", [D, IN_COLS]); s.b_in = I("b_in", [IN_COLS])
        s.w_memkv = I("w_memkv", [D, 1024])
        s.w_br = [I(f"w_br{i}", [512, D]) for i in range(3)]
        s.w_out = I("w_out", [D, D])
        s.w_a = I("w_a", [D, DFF]); s.w_b = I("w_b", [D, DFF]); s.w_d = I("w_d", [DFF, D])
        s.conv_w = I("conv_w", [3, DFF]); s.conv_b = I("conv_b", [DFF])
        for nm in ("identf", "antiid", "tri_incl", "tri_up", "causal", "tri_incl_s", "tri_up_s",
                   "causal_s", "blockdiag_s"):
            setattr(s, "c_" + nm, I("c_" + nm, [128, 128]))
        s.c_rowmask_s = I("c_rowmask_s", [128, 4])
        s.c_onehot = I("c_onehot", [3, 32, TABL])
        s.o_y = O("o_y", [s.nmain * 128, D])
        s.o_ys = O("o_ys", [32, D])
        s.o_pwin = [O(f"o_pwin{g}", [min(W_G[g], s.nmain * 128), 1024]) for g in range(3)]
        s.o_phg = O("o_phg", [4, 128, 128])
        s.o_pconv = O("o_pconv", [2, DFF])
        s.o_pmem = O("o_pmem", [256, 1024])
        s.o_swin = [O(f"o_swin{g}", [4, W_G[g], 1024]) for g in range(3)]
        s.o_shg = O("o_shg", [4, 4, 128, 128])
        s.o_sconv = O("o_sconv", [8, DFF])
        Sc = lambda n, sh, dt=BF16: T.dram(sh, dt, n)
        s.wb_in = [Sc(f"wb_in{i}", [2, 128, 8, 512]) for i in range(10)]; s.bb_in = Sc("bb_in", [1, IN_COLS])
        s.wb_memkv = Sc("wb_memkv", [2, 128, 8, 512])
        s.wb_br = [Sc(f"wb_br{i}", [2, 128, 4, 512]) for i in range(3)]
        s.wb_out = Sc("wb_out", [2, 128, 8, 512])
        s.wb_a = Sc("wb_a", [8, 128, 8, 512]); s.wb_b = Sc("wb_b", [8, 128, 8, 512])
        s.wb_d = Sc("wb_d", [8, 128, 8, 512])
        s.vtab = Sc("vtab", [12, TABL], F32)
        s.Kd = [Sc(f"Kd{g}", [s.ntile_g[g], 128, 512]) for g in range(3)]
        s.Vd = [Sc(f"Vd{g}", [s.ntile_g[g], 128, 520]) for g in range(3)]

    DBG = None

    def stop_at(s, name):
        if s.DBG and s.DBG.get("stop") == name:
            raise StopIteration

    def dbg(s, name, ap, bufs, shape, dt=F32):
        if not s.DBG or name in s.DBG["done"] or name not in s.DBG["want"]:
            return
        s.DBG["done"].add(name)
        d = s.T.dram(list(shape), dt, "dbg_" + name, kind="ExternalOutput")
        s.T.dma("sp", d[:], ap, reads=bufs, writes=[d])

    def ACT(s, out, in_, func, R, W, **kw):
        return s.T.op("act", lambda e: e.activation(out=out, in_=in_, func=func, **kw), R, W)

    def TT(s, out, a, b, op, R, W, eng="dve"):
        return s.T.op(eng, lambda e: e.tensor_tensor(out=out, in0=a, in1=b, op=op), R, W)

    def TS(s, out, a, s1, s2, op0, op1, R, W, eng="dve"):
        if s2 is None:
            return s.T.op(eng, lambda e: e.tensor_scalar(out=out, in0=a, scalar1=s1, scalar2=None, op0=op0), R, W)
        return s.T.op(eng, lambda e: e.tensor_scalar(out=out, in0=a, scalar1=s1, scalar2=s2, op0=op0, op1=op1), R, W)

    def STT(s, out, a, sc, b, op0, op1, R, W, eng="dve"):
        return s.T.op(eng, lambda e: e.scalar_tensor_tensor(out=out, in0=a, scalar=sc, in1=b, op0=op0, op1=op1), R, W)

    def CP(s, out, in_, R, W, eng="dve"):
        if eng == "act":
            return s.T.op("act", lambda e: e.activation(out=out, in_=in_, func=AF.Copy), R, W)
        return s.T.op(eng, lambda e: e.tensor_copy(out=out, in_=in_), R, W)

    def MM(s, out, lhsT, rhs, start, stop, R, W, inc=True):
        return s.T.op("pe", lambda e: e.matmul(out, lhsT=lhsT, rhs=rhs, start=start, stop=stop), R, W, inc=inc)

    def TR(s, out, in_, ident, R, W):
        return s.T.op("pe", lambda e: e.transpose(out=out, in_=in_, identity=ident), R, W)

    def alloc(s):
        T, nc = s.T, s.nc
        NTOK = s.stn * 128
        s.NTOK = NTOK
        banks = [Buf(nc.alloc_psum_tensor(f"psb{i}", [128, 512], F32), f"psb{i}", excl=True) for i in range(8)]
        s.psA = Pool(banks[:4])
        s.psB = Pool(banks[4:])
        P = lambda name, shape, dt, n: Pool([T.sb(shape, dt, name) for _ in range(n)])
        s.idf = T.sb([128, 128], F32, "idf"); s.idb = T.sb([128, 128], BF16, "idb")
        s.tri_incl = T.sb([128, 128], F32, "tri_incl"); s.tri_up = T.sb([128, 128], F32, "tri_up")
        s.causal = s.tri_incl
        s.tri_incl_s = T.sb([128, 128], F32, "tri_incl_s"); s.tri_up_s = T.sb([128, 128], F32, "tri_up_s")
        s.causal_s = s.tri_incl_s; s.bd_s = T.sb([128, 128], F32, "bd_s")
        s.rowmask = T.sb([128, 4], F32, "rowmask")
        s.ones_bf = T.sb([1, 128], BF16, "ones_bf"); s.ones_col = T.sb([128, 1], F32, "ones_col")
        s.gcols = T.sb([128, 24], F32, "gcols")
        s.bcol = T.sb([128, 80], F32, "bcol")
        s.ccol = T.sb([128, 128], F32, "ccol")
        s.lbcol = T.sb([128, 12], F32, "lbcol")
        s.oml_tm = T.sb([128, 512], F32, "oml_tm")
        s.noml_tm = T.sb([128, 512], F32, "noml_tm")
        s.gpost_mix = T.sb([128, D], F32, "gpost_mix"); s.gpost_ffn = T.sb([128, D], F32, "gpost_ffn")
        s.ghg4 = T.sb([128, 512], F32, "ghg4")
        s.flg = T.sb([128, 2], F32, "flg")
        s.Mtab = T.sb([128, 24 * 4 * 128], BF16, "Mtab")
        s.Mnew = T.sb([128, 12 * 32], BF16, "Mnew")
        s.wpool = P("wslot", [128, 9, 512], BF16, 4)
        s.xpool = P("xt", [128, D], F32, s.stn + 1)
        s.vmpool = P("vm", [128, 1], F32, s.stn + 1)
        s.hT = P("hT", [128, KC, NTOK], BF16, 1)
        s.h2T = P("h2T", [128, KC, NTOK], BF16, 1)
        s.Qfm = T.sb([128, 12, NTOK], BF16, "Qfm")
        s.MQfm = T.sb([128, 4, NTOK], BF16, "MQfm")
        s.qfm = T.sb([128, 4, NTOK], F32, "qfm")
        s.kTfm = T.sb([128, 4, NTOK], F32, "kTfm")
        s.attT = T.sb([128, 4, NTOK], BF16, "attT"); s.hgT = T.sb([128, 4, NTOK], BF16, "hgT")
        s.memT = T.sb([128, 4, NTOK], BF16, "memT")
        s.kslot = P("kslot", [128, 512], BF16, 5)
        s.vslot = P("vslot", [128, 520], BF16, 5)
        s.memK = T.sb([128, 4, 256], BF16, "memK"); s.memV = T.sb([128, 2, 520], BF16, "memV")
        s.f2k = P("f2k", [128, 512], F32, 7)
        s.b1k = P("b1k", [128, 512], BF16, 6)
        s.f4k = P("f4k", [128, D], F32, 4)
        s.b2k = P("b2k", [128, D], BF16, 3)
        s.sm = P("sm", [128, 16], F32, 12)
        s.S = T.sb([128, 512], F32, "S"); s.Sbf = T.sb([128, 512], BF16, "Sbf")
        s.hv = T.sb([128, 512], BF16, "hv"); s.hqhat = T.sb([128, 512], BF16, "hqhat")
        s.hqtil = T.sb([128, 512], BF16, "hqtil")
        s.ebias = T.sb([32, 12], F32, "ebias")
        s.Knew = T.sb([128, 384], BF16, "Knew")
        s.Gext = P("Gext", [128, 4, 129], F32, 1)
        s.carry = T.sb([128, 32, 2], F32, "carry")
        s.aext = P("aext", [128, 264], F32, 2)
        s.stage = s.f4k

    def colstage(s, rows_spec, dst_buf, ncols):
        T = s.T
        st = s.f2k.get()
        r0 = 0
        for ap, r in rows_spec:
            T.dma("sp", st[r0:r0 + r, 0:128], ap, writes=[st])
            r0 += r
        assert r0 == ncols
        ps = s.psB.get()
        s.TR(ps[:, 0:ncols], st[0:ncols, 0:128], s.idf[0:ncols, 0:ncols], [st, s.idf], [ps])
        s.CP(dst_buf[:, 0:ncols], ps[:, 0:ncols], [ps], [dst_buf])
        return st

    def setup(s):
        T = s.T
        ld = lambda dst, src: T.dma("sp", dst[:], src[:], writes=[dst])
        s.J = s.f4k.get()
        ld(s.idf, s.c_identf); T.dma("sp", s.J[:, 0:128], s.c_antiid[:], writes=[s.J])
        ld(s.tri_incl, s.c_tri_incl); ld(s.tri_up, s.c_tri_up)
        ld(s.tri_incl_s, s.c_tri_incl_s); ld(s.tri_up_s, s.c_tri_up_s)
        ld(s.bd_s, s.c_blockdiag_s); ld(s.rowmask, s.c_rowmask_s)
        s.CP(s.idb[:], s.idf[:], [s.idf], [s.idb], eng="pool")
        T.op("pool", lambda e: e.memset(s.ones_bf[:], 1.0), [], [s.ones_bf])
        T.op("pool", lambda e: e.memset(s.ones_col[:], 1.0), [], [s.ones_col])
        T.op("pool", lambda e: e.memset(s.carry[:], 0.0), [], [s.carry])
        T.dma("sp", s.flg[:], AP(s.flags.t, 0, [[0, 128], [1, 2]]), writes=[s.flg])
        v8 = lambda b: b[:].rearrange("(r c) -> r c", c=128)
        s.colstage([(v8(s.g_mix_pre), 8), (v8(s.g_ffn_pre), 8), (v8(s.g_mem), 8)], s.gcols, 24)
        st = s.colstage([(v8(s.b_in), 80)], s.bcol, 80)
        bb = s.b1k.get()
        s.CP(bb[0:80, 0:128], st[0:80, 0:128], [st], [bb])
        T.dma("pool", s.bb_in[:].rearrange("o (r c) -> (o r) c", c=128), bb[0:80, 0:128], reads=[bb], writes=[s.bb_in])
        s.colstage([(s.conv_w[:].rearrange("j (r c) -> (j r) c", c=128), 96), (v8(s.conv_b), 32)], s.ccol, 128)
        tmp = T.sb([128, 8], F32, "lbtmp")
        s.colstage([(s.lb_logits[:].rearrange("l (r c) -> (l r) c", c=128), 8)], tmp, 8)
        s.TT(s.lbcol[:, 0:4], tmp[:, 0:4], tmp[:, 4:8], ALU.subtract, [tmp], [s.lbcol])
        s.ACT(s.lbcol[:, 0:4], s.lbcol[:, 0:4], AF.Sigmoid, [s.lbcol], [s.lbcol])
        s.TS(s.lbcol[:, 4:8], s.lbcol[:, 0:4], -1.0, 1.0, ALU.mult, ALU.add, [s.lbcol], [s.lbcol])
        s.TS(s.lbcol[:, 8:12], s.lbcol[:, 4:8], -1.0, None, ALU.mult, None, [s.lbcol], [s.lbcol])
        l1 = s.f2k.get()
        s.lb_tm = s.f2k.get()
        T.dma("sp", s.lb_tm[:], s.lb_logits[0, :].partition_broadcast(128), writes=[s.lb_tm])
        T.dma("sp", l1[:], s.lb_logits[1, :].partition_broadcast(128), writes=[l1])
        s.TT(s.lb_tm[:], s.lb_tm[:], l1[:], ALU.subtract, [s.lb_tm, l1], [s.lb_tm])
        s.ACT(s.lb_tm[:], s.lb_tm[:], AF.Sigmoid, [s.lb_tm], [s.lb_tm])
        s.TS(s.oml_tm[:], s.lb_tm[:], -1.0, 1.0, ALU.mult, ALU.add, [s.lb_tm], [s.oml_tm])
        s.TS(s.noml_tm[:], s.oml_tm[:], -1.0, None, ALU.mult, None, [s.oml_tm], [s.noml_tm])
        T.dma("sp", s.gpost_mix[:], s.g_mix_post[:].partition_broadcast(128), writes=[s.gpost_mix])
        T.dma("sp", s.gpost_ffn[:], s.g_ffn_post[:].partition_broadcast(128), writes=[s.gpost_ffn])
        for h in range(4):
            T.dma("sp", s.ghg4[:, h * 128:(h + 1) * 128], s.g_hg[:].partition_broadcast(128), writes=[s.ghg4])
        eb = s.ebias
        T.dma("sp", eb[0:32, 0:12], s.rel_bias[:], writes=[eb])
        s.ACT(eb[0:32, 0:12], eb[0:32, 0:12], AF.Exp, [eb], [eb])
        for g in range(3):
            for c0 in range(0, TABL, 512):
                n = min(512, TABL - c0)
                oh = s.f2k.get()
                T.dma("sp", oh[0:32, 0:n], s.c_onehot[g, :, c0:c0 + n], writes=[oh])
                ps = s.psB.get()
                s.MM(ps[0:4, 0:n], eb[0:32, g * 4:(g + 1) * 4], oh[0:32, 0:n], True, True, [eb, oh], [ps])
                vt = s.f2k.get()
                s.CP(vt[0:4, 0:n], ps[0:4, 0:n], [ps], [vt])
                T.dma("pool", s.vtab[g * 4:(g + 1) * 4, c0:c0 + n], vt[0:4, 0:n], reads=[vt], writes=[s.vtab])
        Mv = s.Mtab[:].rearrange("p (b h q) -> p b h q", h=4, q=128)
        blk0 = 0
        s.blk0 = []
        for g in range(3):
            s.blk0.append(blk0)
            for h in range(4):
                for d0 in range(0, NDEL[g], 4):
                    nd = min(4, NDEL[g] - d0)
                    hk = s.f2k.get()
                    T.dma("sp", hk[:, 0:nd * 128],
                          AP(s.vtab.t, (g * 4 + h) * TABL + d0 * 128, [[1, 128], [1, nd * 128]]),
                          reads=[s.vtab], writes=[hk])
                    ps = s.psB.get()
                    s.MM(ps[:, 0:nd * 128], s.J[:, 0:128], hk[:, 0:nd * 128], True, True, [s.J, hk], [ps])
                    s.CP(Mv[:, blk0 + d0:blk0 + d0 + nd, h, :],
                         ps[:, 0:nd * 128].rearrange("p (b q) -> p b q", q=128), [ps], [s.Mtab],
                         eng=("act" if h % 2 else "dve"))
            blk0 += NDEL[g]
        Mn = s.Mnew[:].rearrange("p (a q) -> p a q", q=32)
        for g in range(3):
            for h in range(4):
                s.TT(Mn[0:32, g * 4 + h, :], Mv[0:32, s.blk0[g], h, 0:32], s.bd_s[0:32, 0:32], ALU.mult,
                     [s.Mtab, s.bd_s], [s.Mnew])
        s.castg = s.cast_gen()
        s.run_casts(8)

    def cast_gen(s):
        T = s.T
        engs = ["dve", "act"]
        k = [0]

        def cast(src, dst, R, C, gcol0, sc0=0, wd=False):
            for r in range(R // 128):
                for c0 in range(0, C, 1024):
                    a = s.f4k.get(); b = s.b2k.get()
                    T.dma("sp", a[:], src[r * 128:(r + 1) * 128, sc0 + c0:sc0 + c0 + 1024], writes=[a])
                    eng = engs[k[0] % 2]; k[0] += 1
                    if gcol0 is None:
                        s.CP(b[:], a[:], [a], [b], eng=eng)
                    elif eng == "act":
                        s.ACT(b[:], a[:], AF.Copy, [a, s.gcols], [b], scale=s.gcols[:, gcol0 + r:gcol0 + r + 1])
                    else:
                        s.TS(b[:], a[:], s.gcols[:, gcol0 + r:gcol0 + r + 1], None, ALU.mult, None,
                             [a, s.gcols], [b], eng=eng)
                    if wd:
                        dap = dst[r // 4].rearrange("p (h k) c -> p h k c", h=2)[:, :, r % 4, :]
                    else:
                        dap = dst[c0 // 512:c0 // 512 + 2, :, r, :].rearrange("j p c -> p j c")
                    T.dma("pool", dap, b[:].rearrange("p (j c) -> p j c", c=512), reads=[b], writes=[dst])
                    yield

        for blk in (5, 1, 2, 3, 4, 0, 6, 7, 8, 9):
            yield from cast(s.w_in, s.wb_in[blk], D, 1024, 0, sc0=blk * 1024)
        yield from cast(s.w_memkv, s.wb_memkv, D, 1024, 16)
        for i in range(3):
            yield from cast(s.w_br[i], s.wb_br[i], 512, D, None)
        yield from cast(s.w_out, s.wb_out, D, D, None)
        yield from cast(s.w_a, s.wb_a, D, DFF, 8)
        yield from cast(s.w_b, s.wb_b, D, DFF, 8)
        yield from cast(s.w_d, s.wb_d, DFF, D, None, wd=True)

    def run_casts(s, n=None):
        i = 0
        for _ in s.castg:
            i += 1
            if n is not None and i >= n:
                break

    def wload(s, src, r0, nk, c0, ncols, bias_c0=None):
        T = s.T
        if isinstance(src, list):
            src = src[c0 // 1024]
            c0 = c0 % 1024
        w = s.wpool.get()
        assert ncols == 512 and c0 % 512 == 0 and r0 == 0
        T.dma("sp", w[:, 0:nk, 0:512], src[c0 // 512], reads=[src], writes=[w])
        if bias_c0 is not None:
            T.dma("sp", w[0:1, 8, 0:ncols], s.bb_in[0:1, bias_c0:bias_c0 + ncols], reads=[s.bb_in], writes=[w])
        return w

    def rstd_of(s, src, srcbufs, n, Dn):
        junk = s.b2k.get(); ss = s.sm.get()
        s.ACT(junk[0:n, 0:Dn], src, AF.Square, srcbufs, [junk, ss], accum_out=ss[0:n, 0:1])
        s.TS(ss[0:n, 0:1], ss[0:n, 0:1], 1.0 / Dn, EPS, ALU.mult, ALU.add, [ss], [ss])
        s.ACT(ss[0:n, 0:1], ss[0:n, 0:1], AF.Sqrt, [ss], [ss])
        s.T.op("dve", lambda e: e.reciprocal(out=ss[0:n, 0:1], in_=ss[0:n, 0:1]), [ss], [ss])
        return ss

    def transposes(s, src, n, nch, dstT, col0, eng="act"):
        ps = s.psB.get()
        pb = ps.t[:].bitcast(BF16)
        for c in range(nch):
            s.TR(pb[:, c * 128:c * 128 + n], src[0:n, c * 128:(c + 1) * 128], s.idb[0:n, 0:n], [src, s.idb], [ps])
        s.CP(dstT[:, 0:nch, col0:col0 + n],
             pb[:, 0:nch * 128].rearrange("p (c t) -> p c t", t=128)[:, :, 0:n], [ps], [dstT], eng=eng)

    def front(s, xsrc, row0, n, hT, col0, xt=None):
        T = s.T
        if xt is None:
            xt = s.xpool.get()
        T.dma("sp", xt[0:n, :], xsrc[row0:row0 + n, :], writes=[xt])
        ss = s.rstd_of(xt[0:n, :], [xt], n, D)
        xn = s.b2k.get()
        s.TS(xn[0:n, :], xt[0:n, :], ss[0:n, 0:1], None, ALU.mult, None, [xt, ss], [xn])
        s.transposes(xn, n, KC, hT, col0)
        return xt

    def proj_fm(s, w, nk, wc0, actT, ntok):
        ps = s.psA.get()
        for kc in range(nk):
            s.MM(ps[:, 0:ntok], w[:, kc, wc0:wc0 + 128], actT[:, kc, 0:ntok], kc == 0, kc == nk - 1, [w, actT], [ps],
                 inc=(kc == nk - 1))
        return ps

    def proj_tm(s, w, nk, ncols, actT, col0, n, bias, ps=None, pc0=0, wc0=0):
        if ps is None:
            ps = s.psA.get()
        for kc in range(nk):
            s.MM(ps[0:n, pc0:pc0 + ncols], actT[:, kc, col0:col0 + n], w[:, kc, wc0:wc0 + ncols],
                 kc == 0, (kc == nk - 1) and not bias, [w, actT], [ps], inc=((kc == nk - 1) and not bias))
        if bias:
            s.MM(ps[0:n, pc0:pc0 + ncols], s.ones_bf[0:1, 0:n], w[0:1, 8, wc0:wc0 + ncols], False, True,
                 [w, s.ones_bf], [ps])
        return ps

    def hgrn_tile(s, col0, n, vm, hf_ps, hi_ps, hg_ps, full, need_bf):
        h4 = lambda ap: ap.rearrange("p (h t) -> p h t", h=4)
        sg = s.f2k.get()
        s.ACT(sg[0:n, :], hf_ps[0:n, 0:512], AF.Sigmoid, [hf_ps], [sg])
        k = s.f2k.get()
        s.TT(k[0:n, :], sg[0:n, :], s.noml_tm[0:n, :], ALU.mult, [sg, s.noml_tm], [k])
        s.TT(k[0:n, :], k[0:n, :], s.oml_tm[0:n, :], ALU.add, [k, s.oml_tm], [k])
        g = s.f2k.get()
        s.ACT(g[0:n, :], k[0:n, :], AF.Ln, [k], [g], scale=-1.0, bias=1.0)
        R = s.psB.get()
        s.MM(R[0:n, 0:512], s.tri_up[0:n, 0:n], g[0:n, :], True, True, [s.tri_up, g], [R])
        eR = s.f2k.get()
        s.ACT(eR[0:n, :], R[0:n, 0:512], AF.Exp, [R], [eR])
        kend = s.b1k.get()
        s.TT(kend[0:n, :], k[0:n, :], eR[0:n, :], ALU.mult, [k, eR], [kend])
        v = s.hv
        s.TS(v[0:n, :], hi_ps[0:n, 0:512], vm[0:n, 0:1], None, ALU.mult, None, [hi_ps, vm], [v])
        ge = s.psB.get()
        for h in range(4):
            s.MM(ge[:, h:h + 1], g[0:n, h * 128:(h + 1) * 128], s.ones_col[0:n, 0:1], True, True,
                 [g, s.ones_col], [ge])
        eg = s.sm.get()
        s.ACT(eg[:, 0:4], ge[:, 0:4], AF.Exp, [ge], [eg])
        st = s.psB.get()
        for h in range(4):
            hs = slice(h * 128, (h + 1) * 128)
            s.MM(st[:, hs], kend[0:n, hs], v[0:n, hs], True, True, [kend, v], [st])
        for h in range(4):
            hs = slice(h * 128, (h + 1) * 128)
            s.STT(s.S[:, hs], s.S[:, hs], eg[:, h:h + 1], st[:, hs], ALU.mult, ALU.add, [s.S, eg, st], [s.S])
        if full:
            assert n == 128
            GT = s.psB.get()
            for h in range(4):
                s.MM(GT[:, h * 128:(h + 1) * 128], g[:, h * 128:(h + 1) * 128], s.tri_incl[:], True, True,
                     [g, s.tri_incl], [GT])
            Gx = s.Gext.get()
            s.CP(Gx[:, :, 1:129], h4(GT[:, 0:512]), [GT], [Gx], eng="act")
            eG = s.f2k.get()
            s.ACT(h4(eG[:]), Gx[:, :, 1:129], AF.Exp, [Gx], [eG])
            qhat = s.hqhat
            s.TT(h4(qhat[:]), s.qfm[:, :, col0:col0 + 128], h4(eG[:]), ALU.mult, [s.qfm, eG], [qhat])
            dq = s.f2k.get()
            v5 = lambda ap: ap.rearrange("p h (m j) -> p h m j", j=16)
            s.TT(v5(h4(dq[:])), v5(Gx[:, :, 1:129]),
                 v5(Gx[:, :, 0:128])[:, :, :, 0:1].to_broadcast([128, 4, 8, 16]), ALU.subtract, [Gx], [dq])
            s.ACT(dq[:], dq[:], AF.Exp, [dq], [dq])
            qtil = s.hqtil
            s.TT(h4(qtil[:]), s.qfm[:, :, col0:col0 + 128], h4(dq[:]), ALU.mult, [s.qfm, dq], [qtil])
            sc = s.psB.get()
            for m in range(8):
                dk = s.f2k.get()
                s.STT(h4(dk[:]), Gx[:, :, 1:129], -1.0,
                      Gx[:, :, 16 * m:16 * m + 1].to_broadcast([128, 4, 128]), ALU.mult, ALU.add, [Gx], [dk])
                s.ACT(dk[:], dk[:], AF.Exp, [dk], [dk])
                kt = s.b1k.get()
                s.STT(h4(kt[:]), h4(dk[:]), 1e30, s.kTfm[:, :, col0:col0 + 128], ALU.min, ALU.mult,
                      [dk, s.kTfm], [kt])
                for h in range(4):
                    c = h * 128 + 16 * m
                    s.MM(sc[:, c:c + 16], kt[:, h * 128:(h + 1) * 128], qtil[:, c:c + 16], True, True,
                         [kt, qtil], [sc])
            scT = s.b1k.get()
            s.TT(h4(scT[:]), h4(sc[:, 0:512]), s.causal[:].unsqueeze(1).to_broadcast([128, 4, 128]), ALU.mult,
                 [sc, s.causal], [scT])
            s.dbg("Gx", Gx[:].rearrange("p h t -> p (h t)"), [Gx], [128, 516])
            s.dbg("qhat", qhat[:], [qhat], [128, 512], BF16)
            s.dbg("qtil", qtil[:], [qtil], [128, 512], BF16)
            s.dbg("scT", scT[:], [scT], [128, 512], BF16)
            s.dbg("Sbf", s.Sbf[:], [s.Sbf], [128, 512], BF16)
            s.dbg("S", s.S[:], [s.S], [128, 512])
            s.dbg("v", v[:], [v], [128, 512], BF16)
            s.dbg("kTfm", s.kTfm[:, :, col0:col0 + 128], [s.kTfm], [128, 4, 128])
            s.dbg("qfm", s.qfm[:, :, col0:col0 + 128], [s.qfm], [128, 4, 128])
            if s.DBG and s.DBG.get("stop") == "hgrn":
                raise StopIteration
            o = s.psA.get()
            for h in range(4):
                hs = slice(h * 128, (h + 1) * 128)
                s.MM(o[:, hs], scT[:, hs], v[:, hs], True, False, [scT, v], [o], inc=True)
                s.MM(o[:, hs], qhat[:, hs], s.Sbf[:, hs], False, True, [qhat, s.Sbf], [o])
            o2 = s.f2k.get()
            s.ACT(o2[:], o[:, 0:512], AF.Square, [o], [o2])
            ssq = s.sm.get()
            s.T.op("dve", lambda e: e.tensor_reduce(out=ssq[:, 0:4], in_=h4(o2[:]), axis=AX.X, op=ALU.add),
                   [o2], [ssq])
            s.TS(ssq[:, 0:4], ssq[:, 0:4], 1.0 / 128, EPS, ALU.mult, ALU.add, [ssq], [ssq])
            s.ACT(ssq[:, 0:4], ssq[:, 0:4], AF.Sqrt, [ssq], [ssq])
            s.T.op("dve", lambda e: e.reciprocal(out=ssq[:, 0:4], in_=ssq[:, 0:4]), [ssq], [ssq])
            sgt = s.f2k.get()
            s.ACT(sgt[:], hg_ps[:, 0:512], AF.Sigmoid, [hg_ps], [sgt])
            s.TT(sgt[:], sgt[:], s.ghg4[:], ALU.mult, [sgt, s.ghg4], [sgt])
            on = s.f2k.get()
            s.TT(h4(on[:]), h4(o[:, 0:512]), ssq[:, 0:4].unsqueeze(2).to_broadcast([128, 4, 128]), ALU.mult,
                 [o, ssq], [on])
            hg = s.b1k.get()
            s.TT(hg[:], on[:], sgt[:], ALU.mult, [on, sgt], [hg])
            s.transposes(hg, 128, 4, s.hgT, col0, eng="dve")
        if need_bf:
            s.CP(s.Sbf[:], s.S[:], [s.S], [s.Sbf], eng="act")

    def hgrn_state_multi(s, nt_, vms, hf_ps, hi_ps, need_bf):
        Wd = 512 * nt_
        sg = s.f4k.get(); k = s.f4k.get(); g = s.f4k.get()
        for i in range(nt_):
            s.ACT(sg[:, i * 512:(i + 1) * 512], hf_ps[i][:, 0:512], AF.Sigmoid, [hf_ps[i]], [sg])
        v3 = lambda ap: ap.rearrange("p (i c) -> p i c", c=512)
        bc = lambda b: b[:].unsqueeze(1).to_broadcast([128, nt_, 512])
        s.TT(v3(k[:, 0:Wd]), v3(sg[:, 0:Wd]), bc(s.noml_tm), ALU.mult, [sg, s.noml_tm], [k])
        s.TT(v3(k[:, 0:Wd]), v3(k[:, 0:Wd]), bc(s.oml_tm), ALU.add, [k, s.oml_tm], [k])
        s.ACT(g[:, 0:Wd], k[:, 0:Wd], AF.Ln, [k], [g], scale=-1.0, bias=1.0)
        for i in range(nt_):
            R = s.psB.get()
            s.MM(R[:, 0:512], s.tri_up[:], g[:, i * 512:(i + 1) * 512], True, True, [s.tri_up, g], [R])
            s.ACT(sg[:, i * 512:(i + 1) * 512], R[:, 0:512], AF.Exp, [R], [sg])
        kend = s.b2k.get(); v = s.b2k.get()
        s.TT(kend[:, 0:Wd], k[:, 0:Wd], sg[:, 0:Wd], ALU.mult, [k, sg], [kend])
        for i in range(nt_):
            s.TS(v[:, i * 512:(i + 1) * 512], hi_ps[i][:, 0:512], vms[i][:, 0:1], None, ALU.mult, None,
                 [hi_ps[i], vms[i]], [v])
        ge = s.psB.get()
        for i in range(nt_):
            for h in range(4):
                c = i * 512 + h * 128
                s.MM(ge[:, i * 4 + h:i * 4 + h + 1], g[:, c:c + 128], s.ones_col[:, 0:1], True, True,
                     [g, s.ones_col], [ge])
        eg = s.sm.get()
        s.ACT(eg[:, 0:4 * nt_], ge[:, 0:4 * nt_], AF.Exp, [ge], [eg])
        sts = []
        for i in range(nt_):
            st = s.psB.get()
            for h in range(4):
                c = i * 512 + h * 128
                s.MM(st[:, h * 128:(h + 1) * 128], kend[:, c:c + 128], v[:, c:c + 128], True, True, [kend, v], [st])
            sts.append(st)
        for i in range(nt_):
            for h in range(4):
                hs = slice(h * 128, (h + 1) * 128)
                s.STT(s.S[:, hs], s.S[:, hs], eg[:, i * 4 + h:i * 4 + h + 1], sts[i][:, hs], ALU.mult, ALU.add,
                      [s.S, eg, sts[i]], [s.S])
        if need_bf:
            s.CP(s.Sbf[:], s.S[:], [s.S], [s.Sbf], eng="act")

    def finalize_att(s, U, n, dstT, col0):
        ao = s.b1k.get()
        for h in range(4):
            rc = s.sm.get()
            s.TS(rc[0:n, 0:1], U[h][0:n, 128:129], 1e-30, None, ALU.add, None, [U[h]], [rc])
            s.T.op("dve", lambda e: e.reciprocal(out=rc[0:n, 0:1], in_=rc[0:n, 0:1]), [rc], [rc])
            s.TS(ao[0:n, h * 128:(h + 1) * 128], U[h][0:n, 0:128], rc[0:n, 0:1], None, ALU.mult, None,
                 [U[h], rc], [ao], eng=("dve" if h % 2 else "act") if False else "dve")
        s.transposes(ao, n, 4, dstT, col0)

    def attention_tile(s, t, col0, depth=3):
        T = s.T
        U = [s.psA.get() for _ in range(4)]
        blocks = [(g, d) for g in range(3) for d in range(NDEL[g]) if t - d >= s.first_g[g]]
        nb = len(blocks)
        pend = []

        def issue(bi):
            g, d = blocks[bi]
            ks = s.kslot.get(); vs = s.vslot.get()
            bidx = t - d - s.first_g[g]
            T.dma("sp", ks[:], s.Kd[g][bidx], reads=[s.Kd[g]], writes=[ks])
            T.dma("sp", vs[:], s.Vd[g][bidx], reads=[s.Vd[g]], writes=[vs])
            sp_ = s.psB.get()
            for h in range(4):
                hs = slice(h * 128, (h + 1) * 128)
                s.MM(sp_[:, hs], ks[:, hs], s.Qfm[:, g * 4 + h, col0:col0 + 128], True, True, [ks, s.Qfm], [sp_])
            pend.append((bi, g, d, sp_, vs))

        def retire():
            bi, g, d, sp_, vs = pend.pop(0)
            P = s.f2k.get()
            s.ACT(P[:], sp_[:, 0:512], AF.Exp, [sp_], [P], scale=SCALE)
            Pm = s.b1k.get()
            mo = (s.blk0[g] + d) * 512
            s.TT(Pm[:], P[:], s.Mtab[:, mo:mo + 512], ALU.mult, [P, s.Mtab], [Pm])
            for h in range(4):
                s.MM(U[h][:, 0:129], Pm[:, h * 128:(h + 1) * 128], vs[:, h * 130:h * 130 + 129],
                     bi == 0, bi == nb - 1, [Pm, vs], [U[h]])

        for bi in range(nb):
            issue(bi)
            if len(pend) > depth:
                retire()
        while pend:
            retire()
        s.finalize_att(U, 128, s.attT, col0)

    def mem_att_tile(s, col0, n=128):
        U = [s.psA.get() for _ in range(4)]
        for hp in range(2):
            sp_ = s.psB.get()
            for hh in range(2):
                h = 2 * hp + hh
                for blk in range(2):
                    c = (hh * 2 + blk) * 128
                    s.MM(sp_[:, c:c + n], s.memK[:, h, blk * 128:(blk + 1) * 128], s.MQfm[:, h, col0:col0 + n],
                         True, True, [s.memK, s.MQfm], [sp_])
            Pm = s.b1k.get()
            s.ACT(Pm[:].rearrange("p (a q) -> p a q", q=128)[:, :, 0:n],
                  sp_[:, 0:512].rearrange("p (a q) -> p a q", q=128)[:, :, 0:n], AF.Exp, [sp_], [Pm], scale=SCALE)
            for hh in range(2):
                h = 2 * hp + hh
                for blk in range(2):
                    c = (hh * 2 + blk) * 128
                    s.MM(U[h][0:n, 0:129], Pm[:, c:c + n], s.memV[:, blk, h * 130:h * 130 + 129],
                         blk == 0, blk == 1, [Pm, s.memV], [U[h]])
        s.finalize_att(U, n, s.memT, col0)

    def mem_kv(s):
        T = s.T
        hT = s.hT.get()
        for i in range(2):
            s.front(s.mem_in, i * 128, 128, hT, i * 128)
        s.stop_at("mk_front")
        wk = s.wload(s.wb_memkv, 0, 8, 0, 512)
        wv = s.wload(s.wb_memkv, 0, 8, 512, 512)
        for h in range(4):
            ps = s.proj_fm(wk, 8, h * 128, hT, 256)
            s.CP(s.memK[:, h, :], ps[:, 0:256], [ps], [s.memK], eng="act")
        s.stop_at("mk_fm")
        for i in range(2):
            st = s.stage.get()
            pk = s.proj_tm(wk, 8, 512, hT, i * 128, 128, False)
            s.CP(st[:, 0:512], pk[:, 0:512], [pk], [st], eng="act")
            s.stop_at("mk_a")
            pv = s.proj_tm(wv, 8, 512, hT, i * 128, 128, False)
            s.CP(st[:, 512:1024], pv[:, 0:512], [pv], [st], eng="dve")
            mv = s.memV[:, i, :].rearrange("p (h c) -> p h c", c=130)
            s.CP(mv[:, :, 0:128], pv[:, 0:512].rearrange("p (h c) -> p h c", c=128), [pv], [s.memV], eng="act")
            s.stop_at("mk_b")
            T.op("dve", lambda e: e.memset(mv[:, :, 128:130], 1.0), [], [s.memV])
            s.stop_at("mk_c")
            T.dma("pool", s.o_pmem[i * 128:(i + 1) * 128, :], st[:], reads=[st], writes=[s.o_pmem])

    def mixer_out_multi(s, hT, tl, h2T):
        nt_ = len(tl)
        merged = [s.f4k.get() for _ in tl]
        branches = [(s.attT, C_GA, 0), (s.hgT, C_GH, 1), (s.memT, C_GM, 2)]
        for bi, (bT, cg, wi) in enumerate(branches):
            for half in range(2):
                hs = slice(half * 512, (half + 1) * 512)
                wg = s.wload(s.wb_in, 0, 8, cg + half * 512, 512, bias_c0=cg + half * 512)
                wb = s.wload(s.wb_br[wi], 0, 4, half * 512, 512)
                gp = [s.proj_tm(wg, 8, 512, hT, c0, n, True) for (c0, n, _) in tl]
                bp = [s.proj_tm(wb, 4, 512, bT, c0, n, False) for (c0, n, _) in tl]
                for i, (c0, n, _) in enumerate(tl):
                    sg = s.f2k.get()
                    s.ACT(sg[0:n, :], gp[i][0:n, 0:512], AF.Sigmoid, [gp[i]], [sg])
                    if bi == 0:
                        s.TT(merged[i][0:n, hs], sg[0:n, :], bp[i][0:n, 0:512], ALU.mult, [sg, bp[i]], [merged[i]])
                    else:
                        s.TT(sg[0:n, :], sg[0:n, :], bp[i][0:n, 0:512], ALU.mult, [sg, bp[i]], [sg])
                        s.TT(merged[i][0:n, hs], merged[i][0:n, hs], sg[0:n, :], ALU.add, [merged[i], sg], [merged[i]])
        wo = [s.wload(s.wb_out, 0, 8, half * 512, 512) for half in range(2)]
        for i, (col0, n, xt) in enumerate(tl):
            mb = s.b2k.get()
            s.CP(mb[0:n, :], merged[i][0:n, :], [merged[i]], [mb], eng="act")
            mT = s.b2k.get()
            mT3 = mT[:].rearrange("p (c t) -> p c t", t=128)
            ps = s.psB.get()
            pb = ps.t[:].bitcast(BF16)
            for c in range(KC):
                s.TR(pb[:, c * 128:c * 128 + n], mb[0:n, c * 128:(c + 1) * 128], s.idb[0:n, 0:n], [mb, s.idb], [ps])
            s.CP(mT3[:, :, 0:n], pb[:, 0:1024].rearrange("p (c t) -> p c t", t=128)[:, :, 0:n], [ps], [mT], eng="dve")
            mix = [None, None]
            for half in range(2):
                ps2 = s.psA.get()
                for kc in range(KC):
                    s.MM(ps2[0:n, 0:512], mT3[:, kc, 0:n], wo[half][:, kc, 0:512], kc == 0, kc == KC - 1,
                         [mT, wo[half]], [ps2], inc=(kc == KC - 1))
                mix[half] = ps2
            mx = merged[i]
            s.CP(mx[0:n, 0:512], mix[0][0:n, 0:512], [mix[0]], [mx], eng="act")
            s.CP(mx[0:n, 512:1024], mix[1][0:n, 0:512], [mix[1]], [mx], eng="dve")
            ss = s.rstd_of(mx[0:n, :], [mx], n, D)
            s.STT(mx[0:n, :], mx[0:n, :], ss[0:n, 0:1], s.gpost_mix[0:n, :], ALU.mult, ALU.mult,
                  [mx, ss, s.gpost_mix], [mx])
            s.TT(xt[0:n, :], xt[0:n, :], mx[0:n, :], ALU.add, [xt, mx], [xt])
            ss2 = s.rstd_of(xt[0:n, :], [xt], n, D)
            h2 = s.b2k.get()
            s.TS(h2[0:n, :], xt[0:n, :], ss2[0:n, 0:1], None, ALU.mult, None, [xt, ss2], [h2])
            s.transposes(h2, n, KC, h2T, col0)

    def ffn_group(s, h2T, ntok, xts, out_rows, carry_in, carry_out, tok_view=None):
        T = s.T
        nt = (ntok + 127) // 128
        Y = [s.psA.get() for _ in range(2 * nt)]
        pend = []

        def down(c, cc, pab, wd):
            pa = pab;
            ae = s.aext.get()
            w0 = s.ccol[:, c:c + 1]; w1 = s.ccol[:, 32 + c:33 + c]; w2 = s.ccol[:, 64 + c:65 + c]
            cb = s.ccol[:, 96 + c:97 + c]
            cv = s.f2k.get()
            if tok_view is None:
                s.CP(ae[:, 0:2], carry_in(c), [s.carry], [ae], eng="dve")
                s.CP(ae[:, 2:2 + ntok], pa[:, 0:ntok], [pa], [ae], eng="act")
                s.CP(carry_out(c), ae[:, ntok:ntok + 2], [ae], [s.carry], eng="dve")
                a0 = ae[:, 0:ntok]; a1 = ae[:, 1:1 + ntok]; a2 = ae[:, 2:2 + ntok]; co = cv[:, 0:ntok]
            else:
                nsq, TT_ = tok_view
                av = ae[:, 0:nsq * (TT_ + 2)].rearrange("p (b j) -> p b j", j=TT_ + 2)
                s.CP(av[:, :, 0:2], carry_in(c), [s.carry_s], [ae], eng="dve")
                s.CP(av[:, :, 2:2 + TT_], pa[:, 0:ntok].rearrange("p (b j) -> p b j", j=TT_), [pa], [ae], eng="act")
                a0 = av[:, :, 0:TT_]; a1 = av[:, :, 1:1 + TT_]; a2 = av[:, :, 2:2 + TT_]
                co = cv[:, 0:ntok].rearrange("p (b j) -> p b j", j=TT_)
            s.TS(co, a2, w2, cb, ALU.mult, ALU.add, [ae, s.ccol], [cv])
            s.STT(co, a1, w1, co, ALU.mult, ALU.add, [ae, s.ccol, cv], [cv])
            s.STT(co, a0, w0, co, ALU.mult, ALU.add, [ae, s.ccol, cv], [cv])
            sl = s.f2k.get()
            s.ACT(sl[:, 0:ntok], cv[:, 0:ntok], AF.Silu, [cv], [sl])
            gt = s.b1k.get()
            s.TT(gt[:, 0:ntok], sl[:, 0:ntok], pab[:, 256:256 + ntok], ALU.mult, [sl, pab], [gt])
            for ti in range(nt):
                n = min(128, ntok - ti * 128)
                for half in range(2):
                    s.MM(Y[2 * ti + half][0:n, 0:512], gt[:, ti * 128:ti * 128 + n], wd[:, half * 4 + cc, :],
                         c == 0, c == 31, [gt, wd], [Y[2 * ti + half]])

        for c4 in range(8):
            wa = s.wload(s.wb_a, 0, 8, c4 * 512, 512)
            wb = s.wload(s.wb_b, 0, 8, c4 * 512, 512)
            wd = s.wpool.get()
            T.dma("sp", wd[:, 0:8, :], s.wb_d[c4], reads=[s.wb_d], writes=[wd])
            for cc in range(4):
                c = c4 * 4 + cc
                pab = s.psB.get()
                for kc in range(KC):
                    s.MM(pab[:, 0:ntok], wa[:, kc, cc * 128:(cc + 1) * 128], h2T[:, kc, 0:ntok], kc == 0, kc == KC - 1,
                         [wa, h2T], [pab], inc=(kc == KC - 1))
                for kc in range(KC):
                    s.MM(pab[:, 256:256 + ntok], wb[:, kc, cc * 128:(cc + 1) * 128], h2T[:, kc, 0:ntok], kc == 0,
                         kc == KC - 1, [wb, h2T], [pab], inc=(kc == KC - 1))
                pend.append((c, cc, pab, wd))
                if len(pend) > 2:
                    down(*pend.pop(0))
        while pend:
            down(*pend.pop(0))
        for ti in range(nt):
            n = min(128, ntok - ti * 128)
            yt = s.f4k.get()
            s.CP(yt[0:n, 0:512], Y[2 * ti][0:n, 0:512], [Y[2 * ti]], [yt], eng="act")
            s.CP(yt[0:n, 512:1024], Y[2 * ti + 1][0:n, 0:512], [Y[2 * ti + 1]], [yt], eng="dve")
            ss = s.rstd_of(yt[0:n, :], [yt], n, D)
            s.STT(yt[0:n, :], yt[0:n, :], ss[0:n, 0:1], s.gpost_ffn[0:n, :], ALU.mult, ALU.mult,
                  [yt, ss, s.gpost_ffn], [yt])
            xt = xts[ti]
            s.TT(yt[0:n, :], yt[0:n, :], xt[0:n, :], ALU.add, [yt, xt], [yt])
            dst, r0 = out_rows(ti)
            if dst is not None:
                T.dma("pool", dst[r0:r0 + n, :], yt[0:n, :], reads=[yt], writes=[dst])

    def a_rows(s, h2T, cols, n, dstbuf, rowsel):
        T = s.T
        for c4 in range(8):
            wa = s.wload(s.wb_a, 0, 8, c4 * 512, 512)
            ps = s.psA.get()
            for kc in range(KC):
                s.MM(ps[0:n, 0:512], h2T[:, kc, cols], wa[:, kc, 0:512], kc == 0, kc == KC - 1, [h2T, wa], [ps],
                     inc=(kc == KC - 1))
            st = s.f2k.get()
            s.CP(st[0:n, :], ps[0:n, 0:512], [ps], [st], eng="act")
            for (d0, s0, nr) in rowsel:
                T.dma("pool", dstbuf[d0:d0 + nr, c4 * 512:(c4 + 1) * 512], st[s0:s0 + nr, :], reads=[st], writes=[dstbuf])

    def a_carry(s, h2T, c0, scale_col):
        for c4 in range(8):
            wa = s.wload(s.wb_a, 0, 8, c4 * 512, 512)
            ps = s.psB.get()
            for cc in range(4):
                for kc in range(KC):
                    s.MM(ps[:, 2 * cc:2 * cc + 2], wa[:, kc, cc * 128:(cc + 1) * 128], h2T[:, kc, c0:c0 + 2],
                         kc == 0, kc == KC - 1, [wa, h2T], [ps], inc=(kc == KC - 1))
            s.TS(s.carry[:, c4 * 4:(c4 + 1) * 4, :], ps[:, 0:8].rearrange("p (c j) -> p c j", j=2), scale_col, None,
                 ALU.mult, None, [ps, s.flg], [s.carry])

    def prompt(s):
        T = s.T
        T.op("pool", lambda e: e.memset(s.S[:], 0.0), [], [s.S])
        T.op("pool", lambda e: e.memset(s.Sbf[:], 0.0), [], [s.Sbf])
        for gx in s.Gext.bufs:
            T.op("pool", lambda e: e.memset(gx[:], 0.0), [], [gx])
        kinds = ["pre"] * s.npre + ["halo"] * s.nhalo + ["ovl"] + ["main"] * s.nmain
        sts = []
        t = 0
        while t < s.NT:
            k = kinds[t]
            n = 1 if k == "ovl" else s.stn
            grp = [u for u in range(t, min(t + n, s.NT)) if kinds[u] == k]
            sts.append((grp, k))
            t += len(grp)
        s.copyg = s.copy_gen() if s.with_sample else iter(())
        nlight = sum(1 for _, k in sts if k in ("pre", "halo"))
        per = (204 + max(nlight, 1) - 1) // max(nlight, 1) + 1
        done_mem = False
        for si, (grp, k) in enumerate(sts):
            if k in ("pre", "halo"):
                s.run_casts(per)
                if si % 2 == 1:
                    next(s.copyg, None)
            elif not done_mem:
                s.run_casts(None)
                s.mem_kv()
                done_mem = True
            s.do_st(grp, k)
            s.stop_at(f"st{si}")
        T.dma("pool", s.o_phg[:].rearrange("h k v -> k h v"), s.S[:].rearrange("p (h v) -> p h v", h=4),
              reads=[s.S], writes=[s.o_phg])

    def do_st(s, tiles, kind):
        T = s.T
        nt_ = len(tiles)
        NTOK = nt_ * 128
        mainlike = kind in ("ovl", "main")
        hT = s.hT.get()
        xts, vms = [], []
        for i, t in enumerate(tiles):
            vm = s.vmpool.get()
            T.dma("sp", vm[:, 0:1], s.vmask[t * 128:(t + 1) * 128, :], writes=[vm])
            vms.append(vm)
            xts.append(s.front(s.x_ext, t * 128, 128, hT, i * 128))
        if mainlike:
            w = s.wload(s.wb_in, 0, 8, C_HQ, 512)
            for c in range(4):
                ps = s.proj_fm(w, 8, c * 128, hT, NTOK)
                s.ACT(s.qfm[:, c, 0:NTOK], ps[:, 0:NTOK], AF.Identity, [ps, s.bcol], [s.qfm],
                      bias=s.bcol[:, C_HQ // 128 + c:C_HQ // 128 + c + 1])
            w = s.wload(s.wb_in, 0, 8, C_HF, 512)
            for c in range(4):
                ps = s.proj_fm(w, 8, c * 128, hT, NTOK)
                sg = s.f2k.get()
                s.ACT(sg[:, 0:NTOK], ps[:, 0:NTOK], AF.Sigmoid, [ps, s.bcol], [sg],
                      bias=s.bcol[:, C_HF // 128 + c:C_HF // 128 + c + 1])
                s.TS(s.kTfm[:, c, 0:NTOK], sg[:, 0:NTOK], s.lbcol[:, 8 + c:9 + c], s.lbcol[:, 4 + c:5 + c],
                     ALU.mult, ALU.add, [sg, s.lbcol], [s.kTfm])
        whf = s.wload(s.wb_in, 0, 8, C_HF, 512, bias_c0=C_HF)
        whi = s.wload(s.wb_in, 0, 8, C_HI, 512, bias_c0=C_HI)
        whg = s.wload(s.wb_in, 0, 8, C_HG, 512, bias_c0=C_HG) if mainlike else None
        if not mainlike:
            hfs = [s.proj_tm(whf, 8, 512, hT, i * 128, 128, True) for i in range(nt_)]
            his = [s.proj_tm(whi, 8, 512, hT, i * 128, 128, True) for i in range(nt_)]
            s.hgrn_state_multi(nt_, vms, hfs, his, need_bf=(tiles[-1] >= s.T0 - 1))
        for i, t in enumerate(tiles):
            if not mainlike:
                break
            hf_ps = s.proj_tm(whf, 8, 512, hT, i * 128, 128, True)
            hi_ps = s.proj_tm(whi, 8, 512, hT, i * 128, 128, True)
            hg_ps = s.proj_tm(whg, 8, 512, hT, i * 128, 128, True) if mainlike else None
            s.hgrn_tile(i * 128, 128, vms[i], hf_ps, hi_ps, hg_ps, mainlike, need_bf=(t >= s.T0 - 1))
        for g in range(3):
            need = [t >= s.first_g[g] for t in tiles]
            if not any(need):
                continue
            nout = min(W_G[g] // 128, s.nmain)
            wk = s.wload(s.wb_in, 0, 8, C_K + g * 512, 512, bias_c0=C_K + g * 512)
            wv = s.wload(s.wb_in, 0, 8, C_V + g * 512, 512, bias_c0=C_V + g * 512)
            kb = [s.b1k.get() for _ in tiles]
            for h in range(4):
                ps = s.proj_fm(wk, 8, h * 128, hT, NTOK)
                for i, t in enumerate(tiles):
                    if need[i]:
                        s.ACT(kb[i][:, h * 128:(h + 1) * 128], ps[:, i * 128:(i + 1) * 128], AF.Identity,
                              [ps, s.bcol], [kb[i]],
                              bias=s.bcol[:, C_K // 128 + g * 4 + h:C_K // 128 + g * 4 + h + 1])
            for i, t in enumerate(tiles):
                if not need[i]:
                    continue
                bidx = t - s.first_g[g]
                T.dma("pool", s.Kd[g][bidx], kb[i][:], reads=[kb[i]], writes=[s.Kd[g]])
                pv = s.proj_tm(wv, 8, 512, hT, i * 128, 128, True)
                vb = s.vslot.get()
                v3 = vb[:].rearrange("p (h c) -> p h c", c=130)
                s.TS(v3[:, :, 0:128], pv[:, 0:512].rearrange("p (h c) -> p h c", c=128), vms[i][:, 0:1], None,
                     ALU.mult, None, [pv, vms[i]], [vb])
                s.CP(v3[:, :, 128:130], vms[i][:, 0:1].unsqueeze(1).to_broadcast([128, 4, 2]), [vms[i]], [vb],
                     eng="dve")
                T.dma("pool", s.Vd[g][bidx], vb[:], reads=[vb], writes=[s.Vd[g]])
                mi = t - s.T0 - 1
                if kind == "main" and mi >= s.nmain - nout:
                    st = s.stage.get()
                    s.CP(st[:, 512:1024], pv[:, 0:512], [pv], [st], eng="act")
                    pk = s.proj_tm(wk, 8, 512, hT, i * 128, 128, True)
                    s.CP(st[:, 0:512], pk[:, 0:512], [pk], [st], eng="act")
                    r0 = (mi - (s.nmain - nout)) * 128
                    T.dma("pool", s.o_pwin[g][r0:r0 + 128, :], st[:], reads=[st], writes=[s.o_pwin[g]])
        if not mainlike:
            return
        for g in range(3):
            wq = s.wload(s.wb_in, 0, 8, C_Q + g * 512, 512)
            for h in range(4):
                ps = s.proj_fm(wq, 8, h * 128, hT, NTOK)
                s.ACT(s.Qfm[:, g * 4 + h, 0:NTOK], ps[:, 0:NTOK], AF.Identity, [ps, s.bcol], [s.Qfm],
                      bias=s.bcol[:, g * 4 + h:g * 4 + h + 1])
        wq = s.wload(s.wb_in, 0, 8, C_MQ, 512)
        for h in range(4):
            ps = s.proj_fm(wq, 8, h * 128, hT, NTOK)
            s.ACT(s.MQfm[:, h, 0:NTOK], ps[:, 0:NTOK], AF.Identity, [ps, s.bcol], [s.MQfm],
                  bias=s.bcol[:, C_MQ // 128 + h:C_MQ // 128 + h + 1])
        h2T = s.h2T.get()
        for i, t in enumerate(tiles):
            s.attention_tile(t, i * 128)
            s.mem_att_tile(i * 128)
        s.mixer_out_multi(hT, [(i * 128, 128, xts[i]) for i in range(nt_)], h2T)
        if kind == "ovl":
            s.a_carry(h2T, 126, s.flg[:, 0:1])
            return
        m0 = tiles[0] - s.T0 - 1

        def out_rows(ti):
            return s.o_y, (m0 + ti) * 128

        s.ffn_group(h2T, NTOK, xts, out_rows, lambda c: s.carry[:, c, :], lambda c: s.carry[:, c, :])
        if tiles[-1] == s.NT - 1:
            s.a_rows(h2T, slice(NTOK - 2, NTOK), 2, s.o_pconv, [(0, 0, 2)])

    def cache_block(s, src_ap, srcbuf):
        T = s.T
        ct = s.stage.get()
        T.dma("sp", ct[:], src_ap, reads=[srcbuf], writes=[ct])
        kbf = s.b1k.get()
        s.CP(kbf[:], ct[:, 0:512], [ct], [kbf], eng="dve")
        ps = s.psB.get()
        pb = ps.t[:].bitcast(BF16)
        for h in range(4):
            s.TR(pb[:, h * 128:(h + 1) * 128], kbf[:, h * 128:(h + 1) * 128], s.idb[:], [kbf, s.idb], [ps])
        ks = s.kslot.get()
        s.CP(ks[:], pb[:, 0:512], [ps], [ks], eng="act")
        vs = s.vslot.get()
        v3 = vs[:].rearrange("p (h c) -> p h c", c=130)
        s.CP(v3[:, :, 0:128], ct[:, 512:1024].rearrange("p (h c) -> p h c", c=128), [ct], [vs], eng="dve")
        T.op("dve", lambda e: e.memset(v3[:, :, 128:130], 1.0), [], [vs])
        return ks, vs

    def copy_gen(s):
        for g in (2, 1, 0):
            for b in range(4):
                s.T.dma("sp", s.o_swin[g][b, 0:W_G[g] - 8, :], s.cwin[g][b, 8:W_G[g], :], reads=[s.cwin[g]],
                        writes=[s.o_swin[g]])
                yield

    def sample(s):
        T = s.T
        n = 32
        h4 = lambda ap: ap.rearrange("p (h t) -> p h t", h=4)
        hT = s.hT.get()
        xt = s.front(s.xs_in, 0, n, hT, 0)
        for _ in s.copyg:
            pass
        w = s.wload(s.wb_in, 0, 8, C_HQ, 512)
        for c in range(4):
            ps = s.proj_fm(w, 8, c * 128, hT, n)
            s.ACT(s.qfm[:, c, 0:n], ps[:, 0:n], AF.Identity, [ps, s.bcol], [s.qfm],
                  bias=s.bcol[:, C_HQ // 128 + c:C_HQ // 128 + c + 1])
        w = s.wload(s.wb_in, 0, 8, C_HF, 512)
        for c in range(4):
            ps = s.proj_fm(w, 8, c * 128, hT, n)
            sg = s.f2k.get()
            s.ACT(sg[:, 0:n], ps[:, 0:n], AF.Sigmoid, [ps, s.bcol], [sg],
                  bias=s.bcol[:, C_HF // 128 + c:C_HF // 128 + c + 1])
            s.TS(s.kTfm[:, c, 0:n], sg[:, 0:n], s.lbcol[:, 8 + c:9 + c], s.lbcol[:, 4 + c:5 + c],
                 ALU.mult, ALU.add, [sg, s.lbcol], [s.kTfm])
        whf = s.wload(s.wb_in, 0, 8, C_HF, 512, bias_c0=C_HF)
        whi = s.wload(s.wb_in, 0, 8, C_HI, 512, bias_c0=C_HI)
        whg = s.wload(s.wb_in, 0, 8, C_HG, 512, bias_c0=C_HG)
        hf_ps = s.proj_tm(whf, 8, 512, hT, 0, n, True)
        hi_ps = s.proj_tm(whi, 8, 512, hT, 0, n, True)
        hg_ps = s.proj_tm(whg, 8, 512, hT, 0, n, True)
        sS = T.sb([128, 4, 512], F32, "sS"); sSb = T.sb([128, 4, 512], BF16, "sSb")
        for b in range(4):
            T.dma("sp", sS[:, b, :].rearrange("p (h v) -> p h v", h=4), s.shg_in[b].rearrange("h k v -> k h v"),
                  writes=[sS])
        s.CP(sSb[:], sS[:], [sS], [sSb], eng="act")
        sg = s.f2k.get()
        s.ACT(sg[0:n, :], hf_ps[0:n, 0:512], AF.Sigmoid, [hf_ps], [sg])
        k = s.f2k.get()
        s.TT(k[0:n, :], sg[0:n, :], s.noml_tm[0:n, :], ALU.mult, [sg, s.noml_tm], [k])
        s.TT(k[0:n, :], k[0:n, :], s.oml_tm[0:n, :], ALU.add, [k, s.oml_tm], [k])
        g_ = s.f2k.get()
        s.ACT(g_[0:n, :], k[0:n, :], AF.Ln, [k], [g_], scale=-1.0, bias=1.0)
        R = s.psB.get()
        s.MM(R[0:n, 0:512], s.tri_up_s[0:n, 0:n], g_[0:n, :], True, True, [s.tri_up_s, g_], [R])
        eR = s.f2k.get()
        s.ACT(eR[0:n, :], R[0:n, 0:512], AF.Exp, [R], [eR])
        kend = s.f2k.get()
        s.TT(kend[0:n, :], k[0:n, :], eR[0:n, :], ALU.mult, [k, eR], [kend])
        v = s.hv
        s.CP(v[0:n, :], hi_ps[0:n, 0:512], [hi_ps], [v], eng="act")
        ge = s.psB.get()
        for b in range(4):
            for h in range(4):
                s.MM(ge[:, b * 4 + h:b * 4 + h + 1], g_[0:n, h * 128:(h + 1) * 128], s.rowmask[0:n, b:b + 1],
                     True, True, [g_, s.rowmask], [ge])
        eg = s.sm.get()
        s.ACT(eg[:, 0:16], ge[:, 0:16], AF.Exp, [ge], [eg])
        for b in range(4):
            kb_ = s.b1k.get()
            s.TS(kb_[0:n, :], kend[0:n, :], s.rowmask[0:n, b:b + 1], None, ALU.mult, None, [kend, s.rowmask], [kb_])
            st = s.psB.get()
            for h in range(4):
                hs = slice(h * 128, (h + 1) * 128)
                s.MM(st[:, hs], kb_[0:n, hs], v[0:n, hs], True, True, [kb_, v], [st])
            for h in range(4):
                hs = slice(h * 128, (h + 1) * 128)
                s.STT(sS[:, b, hs], sS[:, b, hs], eg[:, b * 4 + h:b * 4 + h + 1], st[:, hs], ALU.mult, ALU.add,
                      [sS, eg, st], [sS])
            T.dma("pool", s.o_shg[b].rearrange("h k v -> k h v"), sS[:, b, :].rearrange("p (h v) -> p h v", h=4),
                  reads=[sS], writes=[s.o_shg])
        GT = s.psB.get()
        for h in range(4):
            s.MM(GT[:, h * 128:h * 128 + n], g_[0:n, h * 128:(h + 1) * 128], s.tri_incl_s[0:n, 0:n], True, True,
                 [g_, s.tri_incl_s], [GT])
        Gs = s.f2k.get()
        s.CP(h4(Gs[:])[:, :, 0:n], h4(GT[:, 0:512])[:, :, 0:n], [GT], [Gs], eng="act")
        eG = s.f2k.get()
        s.ACT(h4(eG[:])[:, :, 0:n], h4(Gs[:])[:, :, 0:n], AF.Exp, [Gs], [eG])
        qhat = s.hqhat
        s.TT(h4(qhat[:])[:, :, 0:n], s.qfm[:, :, 0:n], h4(eG[:])[:, :, 0:n], ALU.mult, [s.qfm, eG], [qhat])
        enG = s.f2k.get()
        s.ACT(h4(enG[:])[:, :, 0:n], h4(Gs[:])[:, :, 0:n], AF.Exp, [Gs], [enG], scale=-1.0)
        kt = s.hqtil
        s.TT(h4(kt[:])[:, :, 0:n], s.kTfm[:, :, 0:n], h4(enG[:])[:, :, 0:n], ALU.mult, [s.kTfm, enG], [kt])
        sc = s.psB.get()
        for h in range(4):
            s.MM(sc[0:n, h * 128:h * 128 + n], kt[:, h * 128:h * 128 + n], qhat[:, h * 128:h * 128 + n], True, True,
                 [kt, qhat], [sc])
        scT = s.b1k.get()
        s.TT(h4(scT[0:n, :])[:, :, 0:n], h4(sc[0:n, 0:512])[:, :, 0:n],
             s.causal_s[0:n, 0:n].unsqueeze(1).to_broadcast([n, 4, n]), ALU.mult, [sc, s.causal_s], [scT])
        qb = s.b1k.get()
        T.op("pool", lambda e: e.memset(qb[:], 0.0), [], [qb])
        qb4 = qb[:].rearrange("p (b h t) -> p b h t", b=4, h=4)
        for b in range(4):
            s.CP(qb4[:, b, :, 8 * b:8 * b + 8], h4(qhat[:])[:, :, 8 * b:8 * b + 8], [qhat], [qb], eng="dve")
        o = s.psA.get()
        for h in range(4):
            hs = slice(h * 128, (h + 1) * 128)
            s.MM(o[0:n, hs], scT[0:n, h * 128:h * 128 + n], v[0:n, hs], True, False, [scT, v], [o])
            for b in range(4):
                s.MM(o[0:n, hs], qb4[:, b, h, :], sSb[:, b, hs], False, b == 3, [qb, sSb], [o])
        o2 = s.f2k.get()
        s.ACT(o2[0:n, :], o[0:n, 0:512], AF.Square, [o], [o2])
        ssq = s.sm.get()
        T.op("dve", lambda e: e.tensor_reduce(out=ssq[0:n, 0:4], in_=h4(o2[0:n, :]), axis=AX.X, op=ALU.add),
             [o2], [ssq])
        s.TS(ssq[0:n, 0:4], ssq[0:n, 0:4], 1.0 / 128, EPS, ALU.mult, ALU.add, [ssq], [ssq])
        s.ACT(ssq[0:n, 0:4], ssq[0:n, 0:4], AF.Sqrt, [ssq], [ssq])
        T.op("dve", lambda e: e.reciprocal(out=ssq[0:n, 0:4], in_=ssq[0:n, 0:4]), [ssq], [ssq])
        sgt = s.f2k.get()
        s.ACT(sgt[0:n, :], hg_ps[0:n, 0:512], AF.Sigmoid, [hg_ps], [sgt])
        s.TT(sgt[0:n, :], sgt[0:n, :], s.ghg4[0:n, :], ALU.mult, [sgt, s.ghg4], [sgt])
        on = s.f2k.get()
        s.TT(h4(on[0:n, :]), h4(o[0:n, 0:512]), ssq[0:n, 0:4].unsqueeze(2).to_broadcast([n, 4, 128]), ALU.mult,
             [o, ssq], [on])
        hg = s.b1k.get()
        s.TT(hg[0:n, :], on[0:n, :], sgt[0:n, :], ALU.mult, [on, sgt], [hg])
        s.transposes(hg, n, 4, s.hgT, 0, eng="dve")
        Knew = s.Knew
        Vnew = T.sb([128, 3, 520], BF16, "Vnew")
        for g in range(3):
            wq = s.wload(s.wb_in, 0, 8, C_Q + g * 512, 512)
            for h in range(4):
                ps = s.proj_fm(wq, 8, h * 128, hT, n)
                s.ACT(s.Qfm[:, g * 4 + h, 0:n], ps[:, 0:n], AF.Identity, [ps, s.bcol], [s.Qfm],
                      bias=s.bcol[:, g * 4 + h:g * 4 + h + 1])
            wk = s.wload(s.wb_in, 0, 8, C_K + g * 512, 512, bias_c0=C_K + g * 512)
            wv = s.wload(s.wb_in, 0, 8, C_V + g * 512, 512, bias_c0=C_V + g * 512)
            for h in range(4):
                ps = s.proj_fm(wk, 8, h * 128, hT, n)
                s.ACT(Knew[:, (g * 4 + h) * 32:(g * 4 + h) * 32 + n], ps[:, 0:n], AF.Identity, [ps, s.bcol], [Knew],
                      bias=s.bcol[:, C_K // 128 + g * 4 + h:C_K // 128 + g * 4 + h + 1])
            pv = s.proj_tm(wv, 8, 512, hT, 0, n, True)
            v3 = Vnew[:, g, :].rearrange("p (h c) -> p h c", c=130)
            s.CP(v3[0:n, :, 0:128], pv[0:n, 0:512].rearrange("p (h c) -> p h c", c=128), [pv], [Vnew], eng="act")
            T.op("dve", lambda e: e.memset(v3[0:n, :, 128:130], 1.0), [], [Vnew])
            st = s.stage.get()
            s.CP(st[0:n, 512:1024], pv[0:n, 0:512], [pv], [st], eng="dve")
            pk = s.proj_tm(wk, 8, 512, hT, 0, n, True)
            s.CP(st[0:n, 0:512], pk[0:n, 0:512], [pk], [st], eng="act")
            for b in range(4):
                T.dma("pool", s.o_swin[g][b, W_G[g] - 8:W_G[g], :], st[8 * b:8 * b + 8, :], reads=[st],
                      writes=[s.o_swin[g]])
        wq = s.wload(s.wb_in, 0, 8, C_MQ, 512)
        for h in range(4):
            ps = s.proj_fm(wq, 8, h * 128, hT, n)
            s.ACT(s.MQfm[:, h, 0:n], ps[:, 0:n], AF.Identity, [ps, s.bcol], [s.MQfm],
                  bias=s.bcol[:, C_MQ // 128 + h:C_MQ // 128 + h + 1])
        Ppad = [[T.sb([128, 128], BF16, "Ppad") for _ in range(2)] for _ in range(4)]
        for b in range(4):
            for j in range(2):
                T.op("pool", lambda e: e.memset(Ppad[b][j][:], 0.0), [], [Ppad[b][j]])
        pcnt = [0, 0, 0, 0]
        U = [s.psA.get() for _ in range(4)]
        Mv = s.Mtab[:].rearrange("p (b h q) -> p b h q", h=4, q=128)
        for g in range(3):
            sp_ = s.psB.get()
            for h in range(4):
                s.MM(sp_[0:n, h * 32:h * 32 + n], Knew[:, (g * 4 + h) * 32:(g * 4 + h) * 32 + n],
                     s.Qfm[:, g * 4 + h, 0:n], True, True, [Knew, s.Qfm], [sp_])
            P = s.f2k.get()
            s.ACT(P[0:n, 0:128], sp_[0:n, 0:128], AF.Exp, [sp_], [P], scale=SCALE)
            Pn = s.b1k.get()
            s.TT(Pn[0:n, 0:128], P[0:n, 0:128], s.Mnew[0:n, g * 128:(g + 1) * 128], ALU.mult, [P, s.Mnew], [Pn])
            for h in range(4):
                s.MM(U[h][0:n, 0:129], Pn[0:n, h * 32:h * 32 + n], Vnew[0:n, g, h * 130:h * 130 + 129],
                     g == 0, False, [Pn, Vnew], [U[h]])
        todo = [(b, g, d) for b in range(4) for g in range(3) for d in range(1, NDEL[g])]
        pend = []

        def issue(ci):
            b, g, d = todo[ci]
            beta = W_G[g] // 128 - d
            ks, vs = s.cache_block(s.cwin[g][b, beta * 128:(beta + 1) * 128, :], s.cwin[g])
            sp_ = s.psB.get()
            for h in range(4):
                s.MM(sp_[:, h * 8:h * 8 + 8], ks[:, h * 128:(h + 1) * 128], s.Qfm[:, g * 4 + h, 8 * b:8 * b + 8],
                     True, True, [ks, s.Qfm], [sp_])
            pend.append((ci, b, g, d, sp_, vs))

        def retire():
            ci, b, g, d, sp_, vs = pend.pop(0)
            P = s.f2k.get()
            s.ACT(P[:, 0:32], sp_[:, 0:32], AF.Exp, [sp_], [P], scale=SCALE)
            pp = Ppad[b][pcnt[b] % 2]; pcnt[b] += 1
            s.TT(pp[:].rearrange("p (h q) -> p h q", q=32)[:, :, 8 * b:8 * b + 8],
                 P[:, 0:32].rearrange("p (h q) -> p h q", q=8), Mv[:, s.blk0[g] + d, :, 0:8], ALU.mult,
                 [P, s.Mtab], [pp])
            for h in range(4):
                s.MM(U[h][0:n, 0:129], pp[:, h * 32:(h + 1) * 32], vs[:, h * 130:h * 130 + 129],
                     False, ci == len(todo) - 1, [pp, vs], [U[h]])

        for ci in range(len(todo)):
            issue(ci)
            if len(pend) > 1:
                retire()
        while pend:
            retire()
        s.finalize_att(U, n, s.attT, 0)
        U = [s.psA.get() for _ in range(4)]
        for b in range(4):
            for blk in range(2):
                ks, vs = s.cache_block(s.cmem[b, blk * 128:(blk + 1) * 128, :], s.cmem)
                sp_ = s.psB.get()
                for h in range(4):
                    s.MM(sp_[:, h * 8:h * 8 + 8], ks[:, h * 128:(h + 1) * 128], s.MQfm[:, h, 8 * b:8 * b + 8],
                         True, True, [ks, s.MQfm], [sp_])
                pp = Ppad[b][pcnt[b] % 2]; pcnt[b] += 1
                s.ACT(pp[:].rearrange("p (h q) -> p h q", q=32)[:, :, 8 * b:8 * b + 8],
                      sp_[:, 0:32].rearrange("p (h q) -> p h q", q=8), AF.Exp, [sp_], [pp], scale=SCALE)
                for h in range(4):
                    s.MM(U[h][0:n, 0:129], pp[:, h * 32:(h + 1) * 32], vs[:, h * 130:h * 130 + 129],
                         b == 0 and blk == 0, b == 3 and blk == 1, [pp, vs], [U[h]])
        s.finalize_att(U, n, s.memT, 0)
        h2T = s.h2T.get()
        s.mixer_out_multi(hT, [(0, n, xt)], h2T)
        s.carry_s = T.sb([128, 32, 8], F32, "carry_s")
        for pc in range(4):
            sc_ = s.f4k.get()
            T.dma("sp", sc_[0:8, :], s.sconv_in[:, pc * 1024:(pc + 1) * 1024], writes=[sc_])
            ps = s.psB.get()
            for c in range(8):
                s.TR(ps[:, c * 8:(c + 1) * 8], sc_[0:8, c * 128:(c + 1) * 128], s.idf[0:8, 0:8], [sc_, s.idf], [ps])
            s.CP(s.carry_s[:, pc * 8:(pc + 1) * 8, :], ps[:, 0:64].rearrange("p (c j) -> p c j", j=8), [ps],
                 [s.carry_s], eng="dve")
        cs4 = s.carry_s[:].rearrange("p c (b j) -> p c b j", j=2)
        s.ffn_group(h2T, n, [xt], lambda ti: (s.o_ys, 0), lambda c: cs4[:, c, :, :], None, tok_view=(4, 8))
        s.a_rows(h2T, slice(0, n), n, s.o_sconv, [(2 * b, 8 * b + 6, 2) for b in range(4)])


NPRE, NHALO, NMAIN = 32, 16, 16
_CACHE = {}


def _get_program(npre, nhalo, nmain, with_sample=True):
    key = (npre, nhalo, nmain, with_sample)
    if key not in _CACHE:
        _CACHE[key] = KB(npre, nhalo, nmain, with_sample=with_sample)
    return _CACHE[key]


def _common_inputs(inp):
    f = lambda a: np.ascontiguousarray(np.asarray(a, dtype=np.float32))
    c = _static_consts()
    m = {
        "rel_bias": f(inp["rel_bias"]), "lb_logits": f(inp["hg_lb_logits"]),
        "g_mix_pre": f(inp["norm_mix_pre"][0]), "g_mix_post": f(inp["norm_mix_post"][0]),
        "g_ffn_pre": f(inp["norm_ffn_pre"][0]), "g_ffn_post": f(inp["norm_ffn_post"][0]),
        "g_mem": f(inp["mem_norm"][0]), "g_hg": f(inp["hg_norm"][0]),
        "w_in": f(inp["w_in"][0]), "b_in": f(inp["b_in"][0]), "w_memkv": f(inp["w_mem_kv"][0]),
        "w_br0": f(inp["w_br_att"][0]), "w_br1": f(inp["w_br_hg"][0]), "w_br2": f(inp["w_br_mem"][0]),
        "w_out": f(inp["w_out"][0]), "w_a": f(inp["w_ffn_a"][0]), "w_b": f(inp["w_ffn_b"][0]),
        "w_d": f(inp["w_ffn_d"][0]), "conv_w": f(inp["ffn_conv_w"][0]), "conv_b": f(inp["ffn_conv_b"][0]),
    }
    for nm in ("identf", "antiid", "tri_incl", "tri_up", "causal", "tri_incl_s", "tri_up_s", "causal_s",
               "blockdiag_s", "rowmask_s", "onehot"):
        m["c_" + nm] = c[nm]
    return m


def _core_inputs(inp, common, seq, p0, npre, nhalo, nmain, sq0):
    f = lambda a: np.ascontiguousarray(np.asarray(a, dtype=np.float32))
    nt = npre + nhalo + 1 + nmain
    start = p0 - 128 * (npre + nhalo + 1)
    x = np.zeros((nt * 128, D), np.float32)
    lo = max(start, 0)
    x[lo - start:] = inp["x_prompt"][seq, lo:p0 + nmain * 128]
    vm = (np.arange(start, p0 + nmain * 128) >= 0).astype(np.float32)[:, None]
    m = dict(common)
    m["x_ext"] = x
    m["vmask"] = np.ascontiguousarray(vm)
    m["flags"] = np.array([[1.0 if p0 > 0 else 0.0, 0.0]], np.float32)
    m["mem_in"] = f(inp["mem_prompt"][seq])
    m["xs_in"] = f(inp["x_sample"][sq0:sq0 + 4]).reshape(32, D)
    for g, nm in enumerate(("cache_win1_kv", "cache_win2_kv", "cache_win3_kv")):
        m[f"cwin{g}"] = f(inp[nm][0, sq0:sq0 + 4]).reshape(4, W_G[g], 1024)
    m["cmem"] = f(inp["cache_mem_kv"][0, sq0:sq0 + 4]).reshape(4, 256, 1024)
    m["shg_in"] = f(inp["state_hgrn"][0, sq0:sq0 + 4])
    m["sconv_in"] = f(inp["state_ffn_conv"][0, sq0:sq0 + 4]).reshape(8, DFF)
    return m


def kernel(**inp):
    kb = _get_program(NPRE, NHALO, NMAIN)
    common = _common_inputs(inp)
    in_maps = []
    for c in range(8):
        in_maps.append(_core_inputs(inp, common, c // 4, (c % 4) * 2048, NPRE, NHALO, NMAIN, 4 * c))
    res = run_bass_kernel_spmd(kb.nc, in_maps, core_ids=list(range(8))).results
    B, S = 2, 8192
    y_p = np.zeros((B, S, D), np.float32)
    y_s = np.zeros((32, 8, D), np.float32)
    p_win = [np.zeros((1, B, W_G[g], 2, 4, 128), np.float32) for g in range(3)]
    p_hg = np.zeros((1, B, 4, 128, 128), np.float32)
    p_conv = np.zeros((1, B, 2, DFF), np.float32)
    p_mem = np.zeros((1, B, 256, 2, 4, 128), np.float32)
    s_win = [np.zeros((1, 32, W_G[g], 2, 4, 128), np.float32) for g in range(3)]
    s_hg = np.zeros((1, 32, 4, 128, 128), np.float32)
    s_conv = np.zeros((1, 32, 2, DFF), np.float32)
    for c in range(8):
        r = res[c]
        b, j = c // 4, c % 4
        y_p[b, j * 2048:(j + 1) * 2048] = r["o_y"]
        y_s[4 * c:4 * c + 4] = r["o_ys"].reshape(4, 8, D)
        for g in range(3):
            s_win[g][0, 4 * c:4 * c + 4] = r[f"o_swin{g}"].reshape(4, W_G[g], 2, 4, 128)
        s_hg[0, 4 * c:4 * c + 4] = r["o_shg"]
        s_conv[0, 4 * c:4 * c + 4] = r["o_sconv"].reshape(4, 2, DFF)
        if j == 3:
            for g in range(3):
                p_win[g][0, b] = r[f"o_pwin{g}"].reshape(W_G[g], 2, 4, 128)
            p_hg[0, b] = r["o_phg"]
            p_conv[0, b] = r["o_pconv"]
        if j == 0:
            p_mem[0, b] = r["o_pmem"].reshape(256, 2, 4, 128)
    return (y_p, y_s, p_win[0], p_win[1], p_win[2], p_hg, p_conv, p_mem,
            s_win[0], s_win[1], s_win[2], s_hg, s_conv)
```

```python
import math
import numpy as np
import concourse.bass as bass
import concourse.mybir as mybir
from concourse.ap import AP
from concourse.bass_utils import run_bass_kernel_spmd

F32 = mybir.dt.float32
BF16 = mybir.dt.bfloat16
ALU = mybir.AluOpType
AF = mybir.ActivationFunctionType
AX = mybir.AxisListType

EPOCH = 3000


class Buf:
    __slots__ = ("t", "name", "w", "r", "excl")

    def __init__(self, t, name, excl=False):
        self.t = t
        self.name = name
        self.w = None
        self.r = []
        self.excl = excl

    def __getitem__(self, k):
        return self.t[k]


class Eng:
    def __init__(self, nc, name, obj, nep, ndma):
        self.name = name
        self.obj = obj
        self.sems = [nc.alloc_semaphore(f"s_{name}_{i}") for i in range(nep)]
        self.n = 0
        self.waited = {}
        self.dslots = [[nc.alloc_semaphore(f"d_{name}_{i}"), 0] for i in range(ndma)]
        self.dn = 0

    def tag_of(self, n):
        return (self.sems[(n - 1) // EPOCH], (n - 1) % EPOCH + 1, self.name)


class Trk:
    def __init__(self, nc):
        self.nc = nc
        self.E = {
            "pe": Eng(nc, "pe", nc.tensor, 14, 0),
            "act": Eng(nc, "act", nc.scalar, 8, 0),
            "dve": Eng(nc, "dve", nc.vector, 8, 0),
            "pool": Eng(nc, "pool", nc.gpsimd, 4, 16),
            "sp": Eng(nc, "sp", nc.sync, 1, 24),
        }
        self.nbuf = 0

    def sb(self, shape, dt, name):
        self.nbuf += 1
        return Buf(self.nc.alloc_sbuf_tensor(f"{name}_{self.nbuf}", list(shape), dt), name)

    def dram(self, shape, dt, name, kind="Internal"):
        return Buf(self.nc.dram_tensor(name, list(shape), dt, kind=kind), name)

    def _wait(self, E, deps):
        for d in deps:
            if d is None:
                continue
            sem, val, src = d
            if src == "pe" and E.name == "pe":
                continue
            key = id(sem)
            if E.waited.get(key, 0) < val:
                E.obj.wait_ge(sem, val)
                E.waited[key] = val

    def _deps(self, reads, writes):
        deps = []
        for b in reads:
            deps.append(b.w)
            if b.excl:
                deps.extend(b.r)
        for b in writes:
            deps.append(b.w)
            deps.extend(b.r)
        return deps

    def op(self, eng, fn, reads=(), writes=(), inc=True):
        E = self.E[eng]
        self._wait(E, self._deps(reads, writes))
        ins = fn(E.obj)
        if inc:
            E.n += 1
            tag = E.tag_of(E.n)
            ins.then_inc(tag[0], 1)
        else:
            assert eng == "pe"
            tag = E.tag_of(E.n + 1)
        for b in reads:
            b.r.append(tag)
        for b in writes:
            b.w = tag
            b.r = []
        return ins

    def dma(self, q, out, in_, reads=(), writes=(), **kw):
        E = self.E[q]
        self._wait(E, self._deps(reads, writes))
        slot = E.dslots[E.dn % len(E.dslots)]
        E.dn += 1
        if slot[1] > 0 and E.waited.get(id(slot[0]), 0) < slot[1]:
            E.obj.wait_ge(slot[0], slot[1])
            E.waited[id(slot[0])] = slot[1]
        ins = E.obj.dma_start(out=out, in_=in_, **kw)
        slot[1] += 16
        ins.then_inc(slot[0], 16)
        tag = (slot[0], slot[1], "dma")
        for b in reads:
            b.r.append(tag)
        for b in writes:
            b.w = tag
            b.r = []
        return ins

    def finish(self):
        for q in ("sp", "pool"):
            E = self.E[q]
            for sem, val in E.dslots:
                if val > 0:
                    E.obj.wait_ge(sem, val)
        sp = self.E["sp"]
        for nm in ("pe", "act", "dve", "pool"):
            E = self.E[nm]
            if E.n > 0:
                sem, val, _ = E.tag_of(E.n)
                sp.obj.wait_ge(sem, val)


class Pool:
    def __init__(self, bufs):
        self.bufs = bufs
        self.i = 0

    def get(self):
        b = self.bufs[self.i % len(self.bufs)]
        self.i += 1
        return b


D = 1024
KC = 8
IN_COLS = 10240
DFF = 4096
W_G = (128, 512, 2048)
DIL = (1, 4, 16)
NDEL = (2, 5, 17)
C_Q, C_K, C_V, C_HQ, C_HF, C_HI, C_HG, C_MQ, C_GA, C_GH, C_GM = (
    0, 1536, 3072, 4608, 5120, 5632, 6144, 6656, 7168, 8192, 9216)
EPS = 1e-6
TABL = 2304
SCALE = 1.0 / math.sqrt(128.0)


def _rel_bucket_np(dist):
    d = np.maximum(dist, 1).astype(np.float32)
    large = 16 + (np.log(d / np.float32(16)) / np.float32(math.log(128.0)) * np.float32(16)).astype(np.int32)
    large = np.minimum(large, 31)
    return np.where(dist < 16, dist, large)


def _static_consts():
    c = {}
    c["identf"] = np.eye(128, dtype=np.float32)
    c["antiid"] = np.eye(128, dtype=np.float32)[::-1].copy()
    s = np.arange(128)
    c["tri_incl"] = (s[:, None] <= s[None, :]).astype(np.float32)
    c["tri_up"] = (s[:, None] > s[None, :]).astype(np.float32)
    c["causal"] = (s[:, None] <= s[None, :]).astype(np.float32)
    s32 = np.arange(32)
    same = (s32[:, None] // 8) == (s32[None, :] // 8)
    t32 = np.zeros((128, 128), np.float32); t32[:32, :32] = same & (s32[:, None] <= s32[None, :])
    u32 = np.zeros((128, 128), np.float32); u32[:32, :32] = same & (s32[:, None] > s32[None, :])
    c["tri_incl_s"] = t32
    c["tri_up_s"] = u32
    c["causal_s"] = t32.copy()
    bd = np.zeros((128, 128), np.float32); bd[:32, :32] = same
    c["blockdiag_s"] = bd
    rm = np.zeros((128, 4), np.float32)
    for b in range(4):
        rm[8 * b:8 * b + 8, b] = 1.0
    c["rowmask_s"] = rm
    oh = np.zeros((3, 32, TABL), np.float32)
    for g in range(3):
        i = np.arange(TABL)
        dl = i - 127
        ok = (dl >= 0) & (dl <= W_G[g]) & (dl % DIL[g] == 0)
        bk = _rel_bucket_np(np.maximum(dl, 0).astype(np.int32))
        oh[g, bk[ok], i[ok]] = 1.0
    c["onehot"] = oh
    return c


class KB:
    def __init__(self, npre, nhalo, nmain, stn=2, with_sample=True, dbg=None):
        self.DBG = dbg
        self.npre, self.nhalo, self.nmain, self.stn = npre, nhalo, nmain, stn
        self.with_sample = with_sample
        self.NT = npre + nhalo + 1 + nmain
        self.T0 = npre + nhalo
        self.first_g = [self.T0 - min(nhalo, NDEL[g] - 1) for g in range(3)]
        self.ntile_g = [self.NT - self.first_g[g] for g in range(3)]
        nc = self.nc = bass.Bass("TRN2", target_bir_lowering=False)
        T = self.T = Trk(nc)
        self.declare_io()
        self.alloc()
        try:
            self.setup()
            self.stop_at("setup")
            self.prompt()
            self.stop_at("prompt")
            if with_sample:
                self.sample()
        except StopIteration:
            pass
        T.finish()

    def declare_io(s):
        T = s.T
        I = lambda n, sh: T.dram(sh, F32, n, kind="ExternalInput")
        O = lambda n, sh: T.dram(sh, F32, n, kind="ExternalOutput")
        NT = s.NT
        s.x_ext = I("x_ext", [NT * 128, D])
        s.vmask = I("vmask", [NT * 128, 1])
        s.flags = I("flags", [1, 2])
        s.mem_in = I("mem_in", [256, D])
        s.xs_in = I("xs_in", [32, D])
        s.cwin = [I(f"cwin{g}", [4, W_G[g], 1024]) for g in range(3)]
        s.cmem = I("cmem", [4, 256, 1024])
        s.shg_in = I("shg_in", [4, 4, 128, 128])
        s.sconv_in = I("sconv_in", [8, DFF])
        s.rel_bias = I("rel_bias", [32, 12])
        s.lb_logits = I("lb_logits", [2, 512])
        s.g_mix_pre = I("g_mix_pre", [D]); s.g_mix_post = I("g_mix_post", [D])
        s.g_ffn_pre = I("g_ffn_pre", [D]); s.g_ffn_post = I("g_ffn_post", [D])
        s.g_mem = I("g_mem", [D]); s.g_hg = I("g_hg", [128])
        s.w_in = I("w_in", [D, IN_COLS]); s.b_in = I("b_in", [IN_COLS])
        s.w_memkv = I("w_memkv", [D, 1024])
        s.w_br = [I(f"w_br{i}", [512, D]) for i in range(3)]
        s.w_out = I("w_out", [D, D])
        s.w_a = I("w_a", [D, DFF]); s.w_b = I("w_b", [D, DFF]); s.w_d = I("w_d", [DFF, D])
        s.conv_w = I("conv_w", [3, DFF]); s.conv_b = I("conv_b", [DFF])
        for nm in ("identf", "antiid", "tri_incl", "tri_up", "causal", "tri_incl_s", "tri_up_s",
                   "causal_s", "blockdiag_s"):
            setattr(s, "c_" + nm, I("c_" + nm, [128, 128]))
        s.c_rowmask_s = I("c_rowmask_s", [128, 4])
        s.c_onehot = I("c_onehot", [3, 32, TABL])
        s.o_y = O("o_y", [s.nmain * 128, D])
        s.o_ys = O("o_ys", [32, D])
        s.o_pwin = [O(f"o_pwin{g}", [min(W_G[g], s.nmain * 128), 1024]) for g in range(3)]
        s.o_phg = O("o_phg", [4, 128, 128])
        s.o_pconv = O("o_pconv", [2, DFF])
        s.o_pmem = O("o_pmem", [256, 1024])
        s.o_swin = [O(f"o_swin{g}", [4, W_G[g], 1024]) for g in range(3)]
        s.o_shg = O("o_shg", [4, 4, 128, 128])
        s.o_sconv = O("o_sconv", [8, DFF])
        Sc = lambda n, sh, dt=BF16: T.dram(sh, dt, n)
        s.wb_in = [Sc(f"wb_in{i}", [2, 128, 8, 512]) for i in range(10)]; s.bb_in = Sc("bb_in", [1, IN_COLS])
        s.wb_memkv = Sc("wb_memkv", [2, 128, 8, 512])
        s.wb_br = [Sc(f"wb_br{i}", [2, 128, 4, 512]) for i in range(3)]
        s.wb_out = Sc("wb_out", [2, 128, 8, 512])
        s.wb_a = Sc("wb_a", [8, 128, 8, 512]); s.wb_b = Sc("wb_b", [8, 128, 8, 512])
        s.wb_d = Sc("wb_d", [8, 128, 8, 512])
        s.vtab = Sc("vtab", [12, TABL], F32)
        s.Kd = [Sc(f"Kd{g}", [s.ntile_g[g], 128, 512]) for g in range(3)]
        s.Vd = [Sc(f"Vd{g}", [s.ntile_g[g], 128, 520]) for g in range(3)]

    DBG = None

    def stop_at(s, name):
        if s.DBG and s.DBG.get("stop") == name:
            raise StopIteration

    def dbg(s, name, ap, bufs, shape, dt=F32):
        if not s.DBG or name in s.DBG["done"] or name not in s.DBG["want"]:
            return
        s.DBG["done"].add(name)
        d = s.T.dram(list(shape), dt, "dbg_" + name, kind="ExternalOutput")
        s.T.dma("sp", d[:], ap, reads=bufs, writes=[d])

    def ACT(s, out, in_, func, R, W, **kw):
        return s.T.op("act", lambda e: e.activation(out=out, in_=in_, func=func, **kw), R, W)

    def TT(s, out, a, b, op, R, W, eng="dve"):
        return s.T.op(eng, lambda e: e.tensor_tensor(out=out, in0=a, in1=b, op=op), R, W)

    def TS(s, out, a, s1, s2, op0, op1, R, W, eng="dve"):
        if s2 is None:
            return s.T.op(eng, lambda e: e.tensor_scalar(out=out, in0=a, scalar1=s1, scalar2=None, op0=op0), R, W)
        return s.T.op(eng, lambda e: e.tensor_scalar(out=out, in0=a, scalar1=s1, scalar2=s2, op0=op0, op1=op1), R, W)

    def STT(s, out, a, sc, b, op0, op1, R, W, eng="dve"):
        return s.T.op(eng, lambda e: e.scalar_tensor_tensor(out=out, in0=a, scalar=sc, in1=b, op0=op0, op1=op1), R, W)

    def CP(s, out, in_, R, W, eng="dve"):
        if eng == "act":
            return s.T.op("act", lambda e: e.activation(out=out, in_=in_, func=AF.Copy), R, W)
        return s.T.op(eng, lambda e: e.tensor_copy(out=out, in_=in_), R, W)

    def MM(s, out, lhsT, rhs, start, stop, R, W, inc=True):
        return s.T.op("pe", lambda e: e.matmul(out, lhsT=lhsT, rhs=rhs, start=start, stop=stop), R, W, inc=inc)

    def TR(s, out, in_, ident, R, W):
        return s.T.op("pe", lambda e: e.transpose(out=out, in_=in_, identity=ident), R, W)

    def alloc(s):
        T, nc = s.T, s.nc
        NTOK = s.stn * 128
        s.NTOK = NTOK
        banks = [Buf(nc.alloc_psum_tensor(f"psb{i}", [128, 512], F32), f"psb{i}", excl=True) for i in range(8)]
        s.psA = Pool(banks[:4])
        s.psB = Pool(banks[4:])
        P = lambda name, shape, dt, n: Pool([T.sb(shape, dt, name) for _ in range(n)])
        s.idf = T.sb([128, 128], F32, "idf"); s.idb = T.sb([128, 128], BF16, "idb")
        s.tri_incl = T.sb([128, 128], F32, "tri_incl"); s.tri_up = T.sb([128, 128], F32, "tri_up")
        s.causal = s.tri_incl
        s.tri_incl_s = T.sb([128, 128], F32, "tri_incl_s"); s.tri_up_s = T.sb([128, 128], F32, "tri_up_s")
        s.causal_s = s.tri_incl_s; s.bd_s = T.sb([128, 128], F32, "bd_s")
        s.rowmask = T.sb([128, 4], F32, "rowmask")
        s.ones_bf = T.sb([1, 128], BF16, "ones_bf"); s.ones_col = T.sb([128, 1], F32, "ones_col")
        s.gcols = T.sb([128, 24], F32, "gcols")
        s.bcol = T.sb([128, 80], F32, "bcol")
        s.ccol = T.sb([128, 128], F32, "ccol")
        s.lbcol = T.sb([128, 12], F32, "lbcol")
        s.oml_tm = T.sb([128, 512], F32, "oml_tm")
        s.noml_tm = T.sb([128, 512], F32, "noml_tm")
        s.gpost_mix = T.sb([128, D], F32, "gpost_mix"); s.gpost_ffn = T.sb([128, D], F32, "gpost_ffn")
        s.ghg4 = T.sb([128, 512], F32, "ghg4")
        s.flg = T.sb([128, 2], F32, "flg")
        s.Mtab = T.sb([128, 24 * 4 * 128], BF16, "Mtab")
        s.Mnew = T.sb([128, 12 * 32], BF16, "Mnew")
        s.wpool = P("wslot", [128, 9, 512], BF16, 4)
        s.xpool = P("xt", [128, D], F32, s.stn + 1)
        s.vmpool = P("vm", [128, 1], F32, s.stn + 1)
        s.hT = P("hT", [128, KC, NTOK], BF16, 1)
        s.h2T = P("h2T", [128, KC, NTOK], BF16, 1)
        s.Qfm = T.sb([128, 12, NTOK], BF16, "Qfm")
        s.MQfm = T.sb([128, 4, NTOK], BF16, "MQfm")
        s.qfm = T.sb([128, 4, NTOK], F32, "qfm")
        s.kTfm = T.sb([128, 4, NTOK], F32, "kTfm")
        s.attT = T.sb([128, 4, NTOK], BF16, "attT"); s.hgT = T.sb([128, 4, NTOK], BF16, "hgT")
        s.memT = T.sb([128, 4, NTOK], BF16, "memT")
        s.kslot = P("kslot", [128, 512], BF16, 5)
        s.vslot = P("vslot", [128, 520], BF16, 5)
        s.memK = T.sb([128, 4, 256], BF16, "memK"); s.memV = T.sb([128, 2, 520], BF16, "memV")
        s.f2k = P("f2k", [128, 512], F32, 7)
        s.b1k = P("b1k", [128, 512], BF16, 6)
        s.f4k = P("f4k", [128, D], F32, 4)
        s.b2k = P("b2k", [128, D], BF16, 3)
        s.sm = P("sm", [128, 16], F32, 12)
        s.S = T.sb([128, 512], F32, "S"); s.Sbf = T.sb([128, 512], BF16, "Sbf")
        s.hv = T.sb([128, 512], BF16, "hv"); s.hqhat = T.sb([128, 512], BF16, "hqhat")
        s.hqtil = T.sb([128, 512], BF16, "hqtil")
        s.ebias = T.sb([32, 12], F32, "ebias")
        s.Knew = T.sb([128, 384], BF16, "Knew")
        s.Gext = P("Gext", [128, 4, 129], F32, 1)
        s.carry = T.sb([128, 32, 2], F32, "carry")
        s.aext = P("aext", [128, 264], F32, 2)
        s.stage = s.f4k

    def colstage(s, rows_spec, dst_buf, ncols):
        T = s.T
        st = s.f2k.get()
        r0 = 0
        for ap, r in rows_spec:
            T.dma("sp", st[r0:r0 + r, 0:128], ap, writes=[st])
            r0 += r
        assert r0 == ncols
        ps = s.psB.get()
        s.TR(ps[:, 0:ncols], st[0:ncols, 0:128], s.idf[0:ncols, 0:ncols], [st, s.idf], [ps])
        s.CP(dst_buf[:, 0:ncols], ps[:, 0:ncols], [ps], [dst_buf])
        return st

    def setup(s):
        T = s.T
        ld = lambda dst, src: T.dma("sp", dst[:], src[:], writes=[dst])
        s.J = s.f4k.get()
        ld(s.idf, s.c_identf); T.dma("sp", s.J[:, 0:128], s.c_antiid[:], writes=[s.J])
        ld(s.tri_incl, s.c_tri_incl); ld(s.tri_up, s.c_tri_up)
        ld(s.tri_incl_s, s.c_tri_incl_s); ld(s.tri_up_s, s.c_tri_up_s)
        ld(s.bd_s, s.c_blockdiag_s); ld(s.rowmask, s.c_rowmask_s)
        s.CP(s.idb[:], s.idf[:], [s.idf], [s.idb], eng="pool")
        T.op("pool", lambda e: e.memset(s.ones_bf[:], 1.0), [], [s.ones_bf])
        T.op("pool", lambda e: e.memset(s.ones_col[:], 1.0), [], [s.ones_col])
        T.op("pool", lambda e: e.memset(s.carry[:], 0.0), [], [s.carry])
        T.dma("sp", s.flg[:], AP(s.flags.t, 0, [[0, 128], [1, 2]]), writes=[s.flg])
        v8 = lambda b: b[:].rearrange("(r c) -> r c", c=128)
        s.colstage([(v8(s.g_mix_pre), 8), (v8(s.g_ffn_pre), 8), (v8(s.g_mem), 8)], s.gcols, 24)
        st = s.colstage([(v8(s.b_in), 80)], s.bcol, 80)
        bb = s.b1k.get()
        s.CP(bb[0:80, 0:128], st[0:80, 0:128], [st], [bb])
        T.dma("pool", s.bb_in[:].rearrange("o (r c) -> (o r) c", c=128), bb[0:80, 0:128], reads=[bb], writes=[s.bb_in])
        s.colstage([(s.conv_w[:].rearrange("j (r c) -> (j r) c", c=128), 96), (v8(s.conv_b), 32)], s.ccol, 128)
        tmp = T.sb([128, 8], F32, "lbtmp")
        s.colstage([(s.lb_logits[:].rearrange("l (r c) -> (l r) c", c=128), 8)], tmp, 8)
        s.TT(s.lbcol[:, 0:4], tmp[:, 0:4], tmp[:, 4:8], ALU.subtract, [tmp], [s.lbcol])
        s.ACT(s.lbcol[:, 0:4], s.lbcol[:, 0:4], AF.Sigmoid, [s.lbcol], [s.lbcol])
        s.TS(s.lbcol[:, 4:8], s.lbcol[:, 0:4], -1.0, 1.0, ALU.mult, ALU.add, [s.lbcol], [s.lbcol])
        s.TS(s.lbcol[:, 8:12], s.lbcol[:, 4:8], -1.0, None, ALU.mult, None, [s.lbcol], [s.lbcol])
        l1 = s.f2k.get()
        s.lb_tm = s.f2k.get()
        T.dma("sp", s.lb_tm[:], s.lb_logits[0, :].partition_broadcast(128), writes=[s.lb_tm])
        T.dma("sp", l1[:], s.lb_logits[1, :].partition_broadcast(128), writes=[l1])
        s.TT(s.lb_tm[:], s.lb_tm[:], l1[:], ALU.subtract, [s.lb_tm, l1], [s.lb_tm])
        s.ACT(s.lb_tm[:], s.lb_tm[:], AF.Sigmoid, [s.lb_tm], [s.lb_tm])
        s.TS(s.oml_tm[:], s.lb_tm[:], -1.0, 1.0, ALU.mult, ALU.add, [s.lb_tm], [s.oml_tm])
        s.TS(s.noml_tm[:], s.oml_tm[:], -1.0, None, ALU.mult, None, [s.oml_tm], [s.noml_tm])
        T.dma("sp", s.gpost_mix[:], s.g_mix_post[:].partition_broadcast(128), writes=[s.gpost_mix])
        T.dma("sp", s.gpost_ffn[:], s.g_ffn_post[:].partition_broadcast(128), writes=[s.gpost_ffn])
        for h in range(4):
            T.dma("sp", s.ghg4[:, h * 128:(h + 1) * 128], s.g_hg[:].partition_broadcast(128), writes=[s.ghg4])
        eb = s.ebias
        T.dma("sp", eb[0:32, 0:12], s.rel_bias[:], writes=[eb])
        s.ACT(eb[0:32, 0:12], eb[0:32, 0:12], AF.Exp, [eb], [eb])
        for g in range(3):
            for c0 in range(0, TABL, 512):
                n = min(512, TABL - c0)
                oh = s.f2k.get()
                T.dma("sp", oh[0:32, 0:n], s.c_onehot[g, :, c0:c0 + n], writes=[oh])
                ps = s.psB.get()
                s.MM(ps[0:4, 0:n], eb[0:32, g * 4:(g + 1) * 4], oh[0:32, 0:n], True, True, [eb, oh], [ps])
                vt = s.f2k.get()
                s.CP(vt[0:4, 0:n], ps[0:4, 0:n], [ps], [vt])
                T.dma("pool", s.vtab[g * 4:(g + 1) * 4, c0:c0 + n], vt[0:4, 0:n], reads=[vt], writes=[s.vtab])
        Mv = s.Mtab[:].rearrange("p (b h q) -> p b h q", h=4, q=128)
        blk0 = 0
        s.blk0 = []
        for g in range(3):
            s.blk0.append(blk0)
            for h in range(4):
                for d0 in range(0, NDEL[g], 4):
                    nd = min(4, NDEL[g] - d0)
                    hk = s.f2k.get()
                    T.dma("sp", hk[:, 0:nd * 128],
                          AP(s.vtab.t, (g * 4 + h) * TABL + d0 * 128, [[1, 128], [1, nd * 128]]),
                          reads=[s.vtab], writes=[hk])
                    ps = s.psB.get()
                    s.MM(ps[:, 0:nd * 128], s.J[:, 0:128], hk[:, 0:nd * 128], True, True, [s.J, hk], [ps])
                    s.CP(Mv[:, blk0 + d0:blk0 + d0 + nd, h, :],
                         ps[:, 0:nd * 128].rearrange("p (b q) -> p b q", q=128), [ps], [s.Mtab],
                         eng=("act" if h % 2 else "dve"))
            blk0 += NDEL[g]
        Mn = s.Mnew[:].rearrange("p (a q) -> p a q", q=32)
        for g in range(3):
            for h in range(4):
                s.TT(Mn[0:32, g * 4 + h, :], Mv[0:32, s.blk0[g], h, 0:32], s.bd_s[0:32, 0:32], ALU.mult,
                     [s.Mtab, s.bd_s], [s.Mnew])
        s.castg = s.cast_gen()
        s.run_casts(8)

    def cast_gen(s):
        T = s.T
        engs = ["dve", "act"]
        k = [0]

        def cast(src, dst, R, C, gcol0, sc0=0, wd=False):
            for r in range(R // 128):
                for c0 in range(0, C, 1024):
                    a = s.f4k.get(); b = s.b2k.get()
                    T.dma("sp", a[:], src[r * 128:(r + 1) * 128, sc0 + c0:sc0 + c0 + 1024], writes=[a])
                    eng = engs[k[0] % 2]; k[0] += 1
                    if gcol0 is None:
                        s.CP(b[:], a[:], [a], [b], eng=eng)
                    elif eng == "act":
                        s.ACT(b[:], a[:], AF.Copy, [a, s.gcols], [b], scale=s.gcols[:, gcol0 + r:gcol0 + r + 1])
                    else:
                        s.TS(b[:], a[:], s.gcols[:, gcol0 + r:gcol0 + r + 1], None, ALU.mult, None,
                             [a, s.gcols], [b], eng=eng)
                    if wd:
                        dap = dst[r // 4].rearrange("p (h k) c -> p h k c", h=2)[:, :, r % 4, :]
                    else:
                        dap = dst[c0 // 512:c0 // 512 + 2, :, r, :].rearrange("j p c -> p j c")
                    T.dma("pool", dap, b[:].rearrange("p (j c) -> p j c", c=512), reads=[b], writes=[dst])
                    yield

        for blk in (5, 1, 2, 3, 4, 0, 6, 7, 8, 9):
            yield from cast(s.w_in, s.wb_in[blk], D, 1024, 0, sc0=blk * 1024)
        yield from cast(s.w_memkv, s.wb_memkv, D, 1024, 16)
        for i in range(3):
            yield from cast(s.w_br[i], s.wb_br[i], 512, D, None)
        yield from cast(s.w_out, s.wb_out, D, D, None)
        yield from cast(s.w_a, s.wb_a, D, DFF, 8)
        yield from cast(s.w_b, s.wb_b, D, DFF, 8)
        yield from cast(s.w_d, s.wb_d, DFF, D, None, wd=True)

    def run_casts(s, n=None):
        i = 0
        for _ in s.castg:
            i += 1
            if n is not None and i >= n:
                break

    def wload(s, src, r0, nk, c0, ncols, bias_c0=None):
        T = s.T
        if isinstance(src, list):
            src = src[c0 // 1024]
            c0 = c0 % 1024
        w = s.wpool.get()
        assert ncols == 512 and c0 % 512 == 0 and r0 == 0
        T.dma("sp", w[:, 0:nk, 0:512], src[c0 // 512], reads=[src], writes=[w])
        if bias_c0 is not None:
            T.dma("sp", w[0:1, 8, 0:ncols], s.bb_in[0:1, bias_c0:bias_c0 + ncols], reads=[s.bb_in], writes=[w])
        return w

    def rstd_of(s, src, srcbufs, n, Dn):
        junk = s.b2k.get(); ss = s.sm.get()
        s.ACT(junk[0:n, 0:Dn], src, AF.Square, srcbufs, [junk, ss], accum_out=ss[0:n, 0:1])
        s.TS(ss[0:n, 0:1], ss[0:n, 0:1], 1.0 / Dn, EPS, ALU.mult, ALU.add, [ss], [ss])
        s.ACT(ss[0:n, 0:1], ss[0:n, 0:1], AF.Sqrt, [ss], [ss])
        s.T.op("dve", lambda e: e.reciprocal(out=ss[0:n, 0:1], in_=ss[0:n, 0:1]), [ss], [ss])
        return ss

    def transposes(s, src, n, nch, dstT, col0, eng="act"):
        ps = s.psB.get()
        pb = ps.t[:].bitcast(BF16)
        for c in range(nch):
            s.TR(pb[:, c * 128:c * 128 + n], src[0:n, c * 128:(c + 1) * 128], s.idb[0:n, 0:n], [src, s.idb], [ps])
        s.CP(dstT[:, 0:nch, col0:col0 + n],
             pb[:, 0:nch * 128].rearrange("p (c t) -> p c t", t=128)[:, :, 0:n], [ps], [dstT], eng=eng)

    def front(s, xsrc, row0, n, hT, col0, xt=None):
        T = s.T
        if xt is None:
            xt = s.xpool.get()
        T.dma("sp", xt[0:n, :], xsrc[row0:row0 + n, :], writes=[xt])
        ss = s.rstd_of(xt[0:n, :], [xt], n, D)
        xn = s.b2k.get()
        s.TS(xn[0:n, :], xt[0:n, :], ss[0:n, 0:1], None, ALU.mult, None, [xt, ss], [xn])
        s.transposes(xn, n, KC, hT, col0)
        return xt

    def proj_fm(s, w, nk, wc0, actT, ntok):
        ps = s.psA.get()
        for kc in range(nk):
            s.MM(ps[:, 0:ntok], w[:, kc, wc0:wc0 + 128], actT[:, kc, 0:ntok], kc == 0, kc == nk - 1, [w, actT], [ps],
                 inc=(kc == nk - 1))
        return ps

    def proj_tm(s, w, nk, ncols, actT, col0, n, bias, ps=None, pc0=0, wc0=0):
        if ps is None:
            ps = s.psA.get()
        for kc in range(nk):
            s.MM(ps[0:n, pc0:pc0 + ncols], actT[:, kc, col0:col0 + n], w[:, kc, wc0:wc0 + ncols],
                 kc == 0, (kc == nk - 1) and not bias, [w, actT], [ps], inc=((kc == nk - 1) and not bias))
        if bias:
            s.MM(ps[0:n, pc0:pc0 + ncols], s.ones_bf[0:1, 0:n], w[0:1, 8, wc0:wc0 + ncols], False, True,
                 [w, s.ones_bf], [ps])
        return ps

    def hgrn_tile(s, col0, n, vm, hf_ps, hi_ps, hg_ps, full, need_bf):
        h4 = lambda ap: ap.rearrange("p (h t) -> p h t", h=4)
        sg = s.f2k.get()
        s.ACT(sg[0:n, :], hf_ps[0:n, 0:512], AF.Sigmoid, [hf_ps], [sg])
        k = s.f2k.get()
        s.TT(k[0:n, :], sg[0:n, :], s.noml_tm[0:n, :], ALU.mult, [sg, s.noml_tm], [k])
        s.TT(k[0:n, :], k[0:n, :], s.oml_tm[0:n, :], ALU.add, [k, s.oml_tm], [k])
        g = s.f2k.get()
        s.ACT(g[0:n, :], k[0:n, :], AF.Ln, [k], [g], scale=-1.0, bias=1.0)
        R = s.psB.get()
        s.MM(R[0:n, 0:512], s.tri_up[0:n, 0:n], g[0:n, :], True, True, [s.tri_up, g], [R])
        eR = s.f2k.get()
        s.ACT(eR[0:n, :], R[0:n, 0:512], AF.Exp, [R], [eR])
        kend = s.b1k.get()
        s.TT(kend[0:n, :], k[0:n, :], eR[0:n, :], ALU.mult, [k, eR], [kend])
        v = s.hv
        s.TS(v[0:n, :], hi_ps[0:n, 0:512], vm[0:n, 0:1], None, ALU.mult, None, [hi_ps, vm], [v])
        ge = s.psB.get()
        for h in range(4):
            s.MM(ge[:, h:h + 1], g[0:n, h * 128:(h + 1) * 128], s.ones_col[0:n, 0:1], True, True,
                 [g, s.ones_col], [ge])
        eg = s.sm.get()
        s.ACT(eg[:, 0:4], ge[:, 0:4], AF.Exp, [ge], [eg])
        st = s.psB.get()
        for h in range(4):
            hs = slice(h * 128, (h + 1) * 128)
            s.MM(st[:, hs], kend[0:n, hs], v[0:n, hs], True, True, [kend, v], [st])
        for h in range(4):
            hs = slice(h * 128, (h + 1) * 128)
            s.STT(s.S[:, hs], s.S[:, hs], eg[:, h:h + 1], st[:, hs], ALU.mult, ALU.add, [s.S, eg, st], [s.S])
        if full:
            assert n == 128
            GT = s.psB.get()
            for h in range(4):
                s.MM(GT[:, h * 128:(h + 1) * 128], g[:, h * 128:(h + 1) * 128], s.tri_incl[:], True, True,
                     [g, s.tri_incl], [GT])
            Gx = s.Gext.get()
            s.CP(Gx[:, :, 1:129], h4(GT[:, 0:512]), [GT], [Gx], eng="act")
            eG = s.f2k.get()
            s.ACT(h4(eG[:]), Gx[:, :, 1:129], AF.Exp, [Gx], [eG])
            qhat = s.hqhat
            s.TT(h4(qhat[:]), s.qfm[:, :, col0:col0 + 128], h4(eG[:]), ALU.mult, [s.qfm, eG], [qhat])
            dq = s.f2k.get()
            v5 = lambda ap: ap.rearrange("p h (m j) -> p h m j", j=16)
            s.TT(v5(h4(dq[:])), v5(Gx[:, :, 1:129]),
                 v5(Gx[:, :, 0:128])[:, :, :, 0:1].to_broadcast([128, 4, 8, 16]), ALU.subtract, [Gx], [dq])
            s.ACT(dq[:], dq[:], AF.Exp, [dq], [dq])
            qtil = s.hqtil
            s.TT(h4(qtil[:]), s.qfm[:, :, col0:col0 + 128], h4(dq[:]), ALU.mult, [s.qfm, dq], [qtil])
            sc = s.psB.get()
            for m in range(8):
                dk = s.f2k.get()
                s.STT(h4(dk[:]), Gx[:, :, 1:129], -1.0,
                      Gx[:, :, 16 * m:16 * m + 1].to_broadcast([128, 4, 128]), ALU.mult, ALU.add, [Gx], [dk])
                s.ACT(dk[:], dk[:], AF.Exp, [dk], [dk])
                kt = s.b1k.get()
                s.STT(h4(kt[:]), h4(dk[:]), 1e30, s.kTfm[:, :, col0:col0 + 128], ALU.min, ALU.mult,
                      [dk, s.kTfm], [kt])
                for h in range(4):
                    c = h * 128 + 16 * m
                    s.MM(sc[:, c:c + 16], kt[:, h * 128:(h + 1) * 128], qtil[:, c:c + 16], True, True,
                         [kt, qtil], [sc])
            scT = s.b1k.get()
            s.TT(h4(scT[:]), h4(sc[:, 0:512]), s.causal[:].unsqueeze(1).to_broadcast([128, 4, 128]), ALU.mult,
                 [sc, s.causal], [scT])
            s.dbg("Gx", Gx[:].rearrange("p h t -> p (h t)"), [Gx], [128, 516])
            s.dbg("qhat", qhat[:], [qhat], [128, 512], BF16)
            s.dbg("qtil", qtil[:], [qtil], [128, 512], BF16)
            s.dbg("scT", scT[:], [scT], [128, 512], BF16)
            s.dbg("Sbf", s.Sbf[:], [s.Sbf], [128, 512], BF16)
            s.dbg("S", s.S[:], [s.S], [128, 512])
            s.dbg("v", v[:], [v], [128, 512], BF16)
            s.dbg("kTfm", s.kTfm[:, :, col0:col0 + 128], [s.kTfm], [128, 4, 128])
            s.dbg("qfm", s.qfm[:, :, col0:col0 + 128], [s.qfm], [128, 4, 128])
            if s.DBG and s.DBG.get("stop") == "hgrn":
                raise StopIteration
            o = s.psA.get()
            for h in range(4):
                hs = slice(h * 128, (h + 1) * 128)
                s.MM(o[:, hs], scT[:, hs], v[:, hs], True, False, [scT, v], [o], inc=True)
                s.MM(o[:, hs], qhat[:, hs], s.Sbf[:, hs], False, True, [qhat, s.Sbf], [o])
            o2 = s.f2k.get()
            s.ACT(o2[:], o[:, 0:512], AF.Square, [o], [o2])
            ssq = s.sm.get()
            s.T.op("dve", lambda e: e.tensor_reduce(out=ssq[:, 0:4], in_=h4(o2[:]), axis=AX.X, op=ALU.add),
                   [o2], [ssq])
            s.TS(ssq[:, 0:4], ssq[:, 0:4], 1.0 / 128, EPS, ALU.mult, ALU.add, [ssq], [ssq])
            s.ACT(ssq[:, 0:4], ssq[:, 0:4], AF.Sqrt, [ssq], [ssq])
            s.T.op("dve", lambda e: e.reciprocal(out=ssq[:, 0:4], in_=ssq[:, 0:4]), [ssq], [ssq])
            sgt = s.f2k.get()
            s.ACT(sgt[:], hg_ps[:, 0:512], AF.Sigmoid, [hg_ps], [sgt])
            s.TT(sgt[:], sgt[:], s.ghg4[:], ALU.mult, [sgt, s.ghg4], [sgt])
            on = s.f2k.get()
            s.TT(h4(on[:]), h4(o[:, 0:512]), ssq[:, 0:4].unsqueeze(2).to_broadcast([128, 4, 128]), ALU.mult,
                 [o, ssq], [on])
            hg = s.b1k.get()
            s.TT(hg[:], on[:], sgt[:], ALU.mult, [on, sgt], [hg])
            s.transposes(hg, 128, 4, s.hgT, col0, eng="dve")
        if need_bf:
            s.CP(s.Sbf[:], s.S[:], [s.S], [s.Sbf], eng="act")

    def hgrn_state_multi(s, nt_, vms, hf_ps, hi_ps, need_bf):
        Wd = 512 * nt_
        sg = s.f4k.get(); k = s.f4k.get(); g = s.f4k.get()
        for i in range(nt_):
            s.ACT(sg[:, i * 512:(i + 1) * 512], hf_ps[i][:, 0:512], AF.Sigmoid, [hf_ps[i]], [sg])
        v3 = lambda ap: ap.rearrange("p (i c) -> p i c", c=512)
        bc = lambda b: b[:].unsqueeze(1).to_broadcast([128, nt_, 512])
        s.TT(v3(k[:, 0:Wd]), v3(sg[:, 0:Wd]), bc(s.noml_tm), ALU.mult, [sg, s.noml_tm], [k])
        s.TT(v3(k[:, 0:Wd]), v3(k[:, 0:Wd]), bc(s.oml_tm), ALU.add, [k, s.oml_tm], [k])
        s.ACT(g[:, 0:Wd], k[:, 0:Wd], AF.Ln, [k], [g], scale=-1.0, bias=1.0)
        for i in range(nt_):
            R = s.psB.get()
            s.MM(R[:, 0:512], s.tri_up[:], g[:, i * 512:(i + 1) * 512], True, True, [s.tri_up, g], [R])
            s.ACT(sg[:, i * 512:(i + 1) * 512], R[:, 0:512], AF.Exp, [R], [sg])
        kend = s.b2k.get(); v = s.b2k.get()
        s.TT(kend[:, 0:Wd], k[:, 0:Wd], sg[:, 0:Wd], ALU.mult, [k, sg], [kend])
        for i in range(nt_):
            s.TS(v[:, i * 512:(i + 1) * 512], hi_ps[i][:, 0:512], vms[i][:, 0:1], None, ALU.mult, None,
                 [hi_ps[i], vms[i]], [v])
        ge = s.psB.get()
        for i in range(nt_):
            for h in range(4):
                c = i * 512 + h * 128
                s.MM(ge[:, i * 4 + h:i * 4 + h + 1], g[:, c:c + 128], s.ones_col[:, 0:1], True, True,
                     [g, s.ones_col], [ge])
        eg = s.sm.get()
        s.ACT(eg[:, 0:4 * nt_], ge[:, 0:4 * nt_], AF.Exp, [ge], [eg])
        sts = []
        for i in range(nt_):
            st = s.psB.get()
            for h in range(4):
                c = i * 512 + h * 128
                s.MM(st[:, h * 128:(h + 1) * 128], kend[:, c:c + 128], v[:, c:c + 128], True, True, [kend, v], [st])
            sts.append(st)
        for i in range(nt_):
            for h in range(4):
                hs = slice(h * 128, (h + 1) * 128)
                s.STT(s.S[:, hs], s.S[:, hs], eg[:, i * 4 + h:i * 4 + h + 1], sts[i][:, hs], ALU.mult, ALU.add,
                      [s.S, eg, sts[i]], [s.S])
        if need_bf:
            s.CP(s.Sbf[:], s.S[:], [s.S], [s.Sbf], eng="act")

    def finalize_att(s, U, n, dstT, col0):
        ao = s.b1k.get()
        for h in range(4):
            rc = s.sm.get()
            s.TS(rc[0:n, 0:1], U[h][0:n, 128:129], 1e-30, None, ALU.add, None, [U[h]], [rc])
            s.T.op("dve", lambda e: e.reciprocal(out=rc[0:n, 0:1], in_=rc[0:n, 0:1]), [rc], [rc])
            s.TS(ao[0:n, h * 128:(h + 1) * 128], U[h][0:n, 0:128], rc[0:n, 0:1], None, ALU.mult, None,
                 [U[h], rc], [ao], eng=("dve" if h % 2 else "act") if False else "dve")
        s.transposes(ao, n, 4, dstT, col0)

    def attention_tile(s, t, col0, depth=3):
        T = s.T
        U = [s.psA.get() for _ in range(4)]
        blocks = [(g, d) for g in range(3) for d in range(NDEL[g]) if t - d >= s.first_g[g]]
        nb = len(blocks)
        pend = []

        def issue(bi):
            g, d = blocks[bi]
            ks = s.kslot.get(); vs = s.vslot.get()
            bidx = t - d - s.first_g[g]
            T.dma("sp", ks[:], s.Kd[g][bidx], reads=[s.Kd[g]], writes=[ks])
            T.dma("sp", vs[:], s.Vd[g][bidx], reads=[s.Vd[g]], writes=[vs])
            sp_ = s.psB.get()
            for h in range(4):
                hs = slice(h * 128, (h + 1) * 128)
                s.MM(sp_[:, hs], ks[:, hs], s.Qfm[:, g * 4 + h, col0:col0 + 128], True, True, [ks, s.Qfm], [sp_])
            pend.append((bi, g, d, sp_, vs))

        def retire():
            bi, g, d, sp_, vs = pend.pop(0)
            P = s.f2k.get()
            s.ACT(P[:], sp_[:, 0:512], AF.Exp, [sp_], [P], scale=SCALE)
            Pm = s.b1k.get()
            mo = (s.blk0[g] + d) * 512
            s.TT(Pm[:], P[:], s.Mtab[:, mo:mo + 512], ALU.mult, [P, s.Mtab], [Pm])
            for h in range(4):
                s.MM(U[h][:, 0:129], Pm[:, h * 128:(h + 1) * 128], vs[:, h * 130:h * 130 + 129],
                     bi == 0, bi == nb - 1, [Pm, vs], [U[h]])

        for bi in range(nb):
            issue(bi)
            if len(pend) > depth:
                retire()
        while pend:
            retire()
        s.finalize_att(U, 128, s.attT, col0)

    def mem_att_tile(s, col0, n=128):
        U = [s.psA.get() for _ in range(4)]
        for hp in range(2):
            sp_ = s.psB.get()
            for hh in range(2):
                h = 2 * hp + hh
                for blk in range(2):
                    c = (hh * 2 + blk) * 128
                    s.MM(sp_[:, c:c + n], s.memK[:, h, blk * 128:(blk + 1) * 128], s.MQfm[:, h, col0:col0 + n],
                         True, True, [s.memK, s.MQfm], [sp_])
            Pm = s.b1k.get()
            s.ACT(Pm[:].rearrange("p (a q) -> p a q", q=128)[:, :, 0:n],
                  sp_[:, 0:512].rearrange("p (a q) -> p a q", q=128)[:, :, 0:n], AF.Exp, [sp_], [Pm], scale=SCALE)
            for hh in range(2):
                h = 2 * hp + hh
                for blk in range(2):
                    c = (hh * 2 + blk) * 128
                    s.MM(U[h][0:n, 0:129], Pm[:, c:c + n], s.memV[:, blk, h * 130:h * 130 + 129],
                         blk == 0, blk == 1, [Pm, s.memV], [U[h]])
        s.finalize_att(U, n, s.memT, col0)

    def mem_kv(s):
        T = s.T
        hT = s.hT.get()
        for i in range(2):
            s.front(s.mem_in, i * 128, 128, hT, i * 128)
        s.stop_at("mk_front")
        wk = s.wload(s.wb_memkv, 0, 8, 0, 512)
        wv = s.wload(s.wb_memkv, 0, 8, 512, 512)
        for h in range(4):
            ps = s.proj_fm(wk, 8, h * 128, hT, 256)
            s.CP(s.memK[:, h, :], ps[:, 0:256], [ps], [s.memK], eng="act")
        s.stop_at("mk_fm")
        for i in range(2):
            st = s.stage.get()
            pk = s.proj_tm(wk, 8, 512, hT, i * 128, 128, False)
            s.CP(st[:, 0:512], pk[:, 0:512], [pk], [st], eng="act")
            s.stop_at("mk_a")
            pv = s.proj_tm(wv, 8, 512, hT, i * 128, 128, False)
            s.CP(st[:, 512:1024], pv[:, 0:512], [pv], [st], eng="dve")
            mv = s.memV[:, i, :].rearrange("p (h c) -> p h c", c=130)
            s.CP(mv[:, :, 0:128], pv[:, 0:512].rearrange("p (h c) -> p h c", c=128), [pv], [s.memV], eng="act")
            s.stop_at("mk_b")
            T.op("dve", lambda e: e.memset(mv[:, :, 128:130], 1.0), [], [s.memV])
            s.stop_at("mk_c")
            T.dma("pool", s.o_pmem[i * 128:(i + 1) * 128, :], st[:], reads=[st], writes=[s.o_pmem])

    def mixer_out_multi(s, hT, tl, h2T):
        nt_ = len(tl)
        merged = [s.f4k.get() for _ in tl]
        branches = [(s.attT, C_GA, 0), (s.hgT, C_GH, 1), (s.memT, C_GM, 2)]
        def issue(bi, half):
            bT, cg, wi = branches[bi]
            wg = s.wload(s.wb_in, 0, 8, cg + half * 512, 512, bias_c0=cg + half * 512)
            wb = s.wload(s.wb_br[wi], 0, 4, half * 512, 512)
            gp = [s.proj_tm(wg, 8, 512, hT, c0, n, True) for (c0, n, _) in tl]
            bp = [s.proj_tm(wb, 4, 512, bT, c0, n, False, ps=s.psB.get()) for (c0, n, _) in tl]
            return bi, half, gp, bp

        def consume(bi, half, gp, bp):
            hs = slice(half * 512, (half + 1) * 512)
            for i, (c0, n, _) in enumerate(tl):
                sg = s.f2k.get()
                s.ACT(sg[0:n, :], gp[i][0:n, 0:512], AF.Sigmoid, [gp[i]], [sg])
                if bi == 0:
                    s.TT(merged[i][0:n, hs], sg[0:n, :], bp[i][0:n, 0:512], ALU.mult, [sg, bp[i]], [merged[i]])
                else:
                    s.TT(sg[0:n, :], sg[0:n, :], bp[i][0:n, 0:512], ALU.mult, [sg, bp[i]], [sg])
                    s.TT(merged[i][0:n, hs], merged[i][0:n, hs], sg[0:n, :], ALU.add, [merged[i], sg], [merged[i]])

        pend = None
        for bi in range(3):
            for half in range(2):
                cur = issue(bi, half)
                if pend is not None:
                    consume(*pend)
                pend = cur
        consume(*pend)
        wo = [s.wload(s.wb_out, 0, 8, half * 512, 512) for half in range(2)]
        for i, (col0, n, xt) in enumerate(tl):
            mb = s.b2k.get()
            s.CP(mb[0:n, :], merged[i][0:n, :], [merged[i]], [mb], eng="act")
            mT = s.b2k.get()
            mT3 = mT[:].rearrange("p (c t) -> p c t", t=128)
            ps = s.psB.get()
            pb = ps.t[:].bitcast(BF16)
            for c in range(KC):
                s.TR(pb[:, c * 128:c * 128 + n], mb[0:n, c * 128:(c + 1) * 128], s.idb[0:n, 0:n], [mb, s.idb], [ps])
            s.CP(mT3[:, :, 0:n], pb[:, 0:1024].rearrange("p (c t) -> p c t", t=128)[:, :, 0:n], [ps], [mT], eng="dve")
            mix = [None, None]
            for half in range(2):
                ps2 = s.psA.get()
                for kc in range(KC):
                    s.MM(ps2[0:n, 0:512], mT3[:, kc, 0:n], wo[half][:, kc, 0:512], kc == 0, kc == KC - 1,
                         [mT, wo[half]], [ps2], inc=(kc == KC - 1))
                mix[half] = ps2
            mx = merged[i]
            s.CP(mx[0:n, 0:512], mix[0][0:n, 0:512], [mix[0]], [mx], eng="act")
            s.CP(mx[0:n, 512:1024], mix[1][0:n, 0:512], [mix[1]], [mx], eng="dve")
            ss = s.rstd_of(mx[0:n, :], [mx], n, D)
            s.STT(mx[0:n, :], mx[0:n, :], ss[0:n, 0:1], s.gpost_mix[0:n, :], ALU.mult, ALU.mult,
                  [mx, ss, s.gpost_mix], [mx])
            s.TT(xt[0:n, :], xt[0:n, :], mx[0:n, :], ALU.add, [xt, mx], [xt])
            ss2 = s.rstd_of(xt[0:n, :], [xt], n, D)
            h2 = s.b2k.get()
            s.TS(h2[0:n, :], xt[0:n, :], ss2[0:n, 0:1], None, ALU.mult, None, [xt, ss2], [h2])
            s.transposes(h2, n, KC, h2T, col0)

    def ffn_group(s, h2T, ntok, xts, out_rows, carry_in, carry_out, tok_view=None):
        T = s.T
        nt = (ntok + 127) // 128
        Y = [s.psA.get() for _ in range(2 * nt)]
        pend = []

        def down(c, cc, pab, wd):
            pa = pab;
            ae = s.aext.get()
            w0 = s.ccol[:, c:c + 1]; w1 = s.ccol[:, 32 + c:33 + c]; w2 = s.ccol[:, 64 + c:65 + c]
            cb = s.ccol[:, 96 + c:97 + c]
            cv = s.f2k.get()
            if tok_view is None:
                s.CP(ae[:, 0:2], carry_in(c), [s.carry], [ae], eng="dve")
                s.CP(ae[:, 2:2 + ntok], pa[:, 0:ntok], [pa], [ae], eng="act")
                s.CP(carry_out(c), ae[:, ntok:ntok + 2], [ae], [s.carry], eng="dve")
                a0 = ae[:, 0:ntok]; a1 = ae[:, 1:1 + ntok]; a2 = ae[:, 2:2 + ntok]; co = cv[:, 0:ntok]
            else:
                nsq, TT_ = tok_view
                av = ae[:, 0:nsq * (TT_ + 2)].rearrange("p (b j) -> p b j", j=TT_ + 2)
                s.CP(av[:, :, 0:2], carry_in(c), [s.carry_s], [ae], eng="dve")
                s.CP(av[:, :, 2:2 + TT_], pa[:, 0:ntok].rearrange("p (b j) -> p b j", j=TT_), [pa], [ae], eng="act")
                a0 = av[:, :, 0:TT_]; a1 = av[:, :, 1:1 + TT_]; a2 = av[:, :, 2:2 + TT_]
                co = cv[:, 0:ntok].rearrange("p (b j) -> p b j", j=TT_)
            s.TS(co, a2, w2, cb, ALU.mult, ALU.add, [ae, s.ccol], [cv])
            s.STT(co, a1, w1, co, ALU.mult, ALU.add, [ae, s.ccol, cv], [cv])
            s.STT(co, a0, w0, co, ALU.mult, ALU.add, [ae, s.ccol, cv], [cv])
            sl = s.f2k.get()
            s.ACT(sl[:, 0:ntok], cv[:, 0:ntok], AF.Silu, [cv], [sl])
            gt = s.b1k.get()
            s.TT(gt[:, 0:ntok], sl[:, 0:ntok], pab[:, 256:256 + ntok], ALU.mult, [sl, pab], [gt])
            for ti in range(nt):
                n = min(128, ntok - ti * 128)
                for half in range(2):
                    s.MM(Y[2 * ti + half][0:n, 0:512], gt[:, ti * 128:ti * 128 + n], wd[:, half * 4 + cc, :],
                         c == 0, c == 31, [gt, wd], [Y[2 * ti + half]])

        for c4 in range(8):
            wa = s.wload(s.wb_a, 0, 8, c4 * 512, 512)
            wb = s.wload(s.wb_b, 0, 8, c4 * 512, 512)
            wd = s.wpool.get()
            T.dma("sp", wd[:, 0:8, :], s.wb_d[c4], reads=[s.wb_d], writes=[wd])
            for cc in range(4):
                c = c4 * 4 + cc
                pab = s.psB.get()
                for kc in range(KC):
                    s.MM(pab[:, 0:ntok], wa[:, kc, cc * 128:(cc + 1) * 128], h2T[:, kc, 0:ntok], kc == 0, kc == KC - 1,
                         [wa, h2T], [pab], inc=(kc == KC - 1))
                for kc in range(KC):
                    s.MM(pab[:, 256:256 + ntok], wb[:, kc, cc * 128:(cc + 1) * 128], h2T[:, kc, 0:ntok], kc == 0,
                         kc == KC - 1, [wb, h2T], [pab], inc=(kc == KC - 1))
                pend.append((c, cc, pab, wd))
                if len(pend) > 2:
                    down(*pend.pop(0))
        while pend:
            down(*pend.pop(0))
        for ti in range(nt):
            n = min(128, ntok - ti * 128)
            yt = s.f4k.get()
            s.CP(yt[0:n, 0:512], Y[2 * ti][0:n, 0:512], [Y[2 * ti]], [yt], eng="act")
            s.CP(yt[0:n, 512:1024], Y[2 * ti + 1][0:n, 0:512], [Y[2 * ti + 1]], [yt], eng="dve")
            ss = s.rstd_of(yt[0:n, :], [yt], n, D)
            s.STT(yt[0:n, :], yt[0:n, :], ss[0:n, 0:1], s.gpost_ffn[0:n, :], ALU.mult, ALU.mult,
                  [yt, ss, s.gpost_ffn], [yt])
            xt = xts[ti]
            s.TT(yt[0:n, :], yt[0:n, :], xt[0:n, :], ALU.add, [yt, xt], [yt])
            dst, r0 = out_rows(ti)
            if dst is not None:
                T.dma("pool", dst[r0:r0 + n, :], yt[0:n, :], reads=[yt], writes=[dst])

    def a_rows(s, h2T, cols, n, dstbuf, rowsel):
        T = s.T
        for c4 in range(8):
            wa = s.wload(s.wb_a, 0, 8, c4 * 512, 512)
            ps = s.psA.get()
            for kc in range(KC):
                s.MM(ps[0:n, 0:512], h2T[:, kc, cols], wa[:, kc, 0:512], kc == 0, kc == KC - 1, [h2T, wa], [ps],
                     inc=(kc == KC - 1))
            st = s.f2k.get()
            s.CP(st[0:n, :], ps[0:n, 0:512], [ps], [st], eng="act")
            for (d0, s0, nr) in rowsel:
                T.dma("pool", dstbuf[d0:d0 + nr, c4 * 512:(c4 + 1) * 512], st[s0:s0 + nr, :], reads=[st], writes=[dstbuf])

    def a_carry(s, h2T, c0, scale_col):
        for c4 in range(8):
            wa = s.wload(s.wb_a, 0, 8, c4 * 512, 512)
            ps = s.psB.get()
            for cc in range(4):
                for kc in range(KC):
                    s.MM(ps[:, 2 * cc:2 * cc + 2], wa[:, kc, cc * 128:(cc + 1) * 128], h2T[:, kc, c0:c0 + 2],
                         kc == 0, kc == KC - 1, [wa, h2T], [ps], inc=(kc == KC - 1))
            s.TS(s.carry[:, c4 * 4:(c4 + 1) * 4, :], ps[:, 0:8].rearrange("p (c j) -> p c j", j=2), scale_col, None,
                 ALU.mult, None, [ps, s.flg], [s.carry])

    def prompt(s):
        T = s.T
        T.op("pool", lambda e: e.memset(s.S[:], 0.0), [], [s.S])
        T.op("pool", lambda e: e.memset(s.Sbf[:], 0.0), [], [s.Sbf])
        for gx in s.Gext.bufs:
            T.op("pool", lambda e: e.memset(gx[:], 0.0), [], [gx])
        kinds = ["pre"] * s.npre + ["halo"] * s.nhalo + ["ovl"] + ["main"] * s.nmain
        sts = []
        t = 0
        while t < s.NT:
            k = kinds[t]
            n = 1 if k == "ovl" else s.stn
            grp = [u for u in range(t, min(t + n, s.NT)) if kinds[u] == k]
            sts.append((grp, k))
            t += len(grp)
        s.copyg = s.copy_gen() if s.with_sample else iter(())
        nlight = sum(1 for _, k in sts if k in ("pre", "halo"))
        per = (204 + max(nlight, 1) - 1) // max(nlight, 1) + 1
        done_mem = False
        for si, (grp, k) in enumerate(sts):
            if k in ("pre", "halo"):
                s.run_casts(per)
                if si % 2 == 1:
                    next(s.copyg, None)
            elif not done_mem:
                s.run_casts(None)
                s.mem_kv()
                done_mem = True
            s.do_st(grp, k)
            s.stop_at(f"st{si}")
        T.dma("pool", s.o_phg[:].rearrange("h k v -> k h v"), s.S[:].rearrange("p (h v) -> p h v", h=4),
              reads=[s.S], writes=[s.o_phg])

    def do_st(s, tiles, kind):
        T = s.T
        nt_ = len(tiles)
        NTOK = nt_ * 128
        mainlike = kind in ("ovl", "main")
        hT = s.hT.get()
        xts, vms = [], []
        for i, t in enumerate(tiles):
            vm = s.vmpool.get()
            T.dma("sp", vm[:, 0:1], s.vmask[t * 128:(t + 1) * 128, :], writes=[vm])
            vms.append(vm)
            xts.append(s.front(s.x_ext, t * 128, 128, hT, i * 128))
        if mainlike:
            w = s.wload(s.wb_in, 0, 8, C_HQ, 512)
            for c in range(4):
                ps = s.proj_fm(w, 8, c * 128, hT, NTOK)
                s.ACT(s.qfm[:, c, 0:NTOK], ps[:, 0:NTOK], AF.Identity, [ps, s.bcol], [s.qfm],
                      bias=s.bcol[:, C_HQ // 128 + c:C_HQ // 128 + c + 1])
            w = s.wload(s.wb_in, 0, 8, C_HF, 512)
            for c in range(4):
                ps = s.proj_fm(w, 8, c * 128, hT, NTOK)
                sg = s.f2k.get()
                s.ACT(sg[:, 0:NTOK], ps[:, 0:NTOK], AF.Sigmoid, [ps, s.bcol], [sg],
                      bias=s.bcol[:, C_HF // 128 + c:C_HF // 128 + c + 1])
                s.TS(s.kTfm[:, c, 0:NTOK], sg[:, 0:NTOK], s.lbcol[:, 8 + c:9 + c], s.lbcol[:, 4 + c:5 + c],
                     ALU.mult, ALU.add, [sg, s.lbcol], [s.kTfm])
        whf = s.wload(s.wb_in, 0, 8, C_HF, 512, bias_c0=C_HF)
        whi = s.wload(s.wb_in, 0, 8, C_HI, 512, bias_c0=C_HI)
        whg = s.wload(s.wb_in, 0, 8, C_HG, 512, bias_c0=C_HG) if mainlike else None
        if not mainlike:
            hfs = [s.proj_tm(whf, 8, 512, hT, i * 128, 128, True) for i in range(nt_)]
            his = [s.proj_tm(whi, 8, 512, hT, i * 128, 128, True) for i in range(nt_)]
            s.hgrn_state_multi(nt_, vms, hfs, his, need_bf=(tiles[-1] >= s.T0 - 1))
        for i, t in enumerate(tiles):
            if not mainlike:
                break
            hf_ps = s.proj_tm(whf, 8, 512, hT, i * 128, 128, True)
            hi_ps = s.proj_tm(whi, 8, 512, hT, i * 128, 128, True)
            hg_ps = s.proj_tm(whg, 8, 512, hT, i * 128, 128, True) if mainlike else None
            s.hgrn_tile(i * 128, 128, vms[i], hf_ps, hi_ps, hg_ps, mainlike, need_bf=(t >= s.T0 - 1))
        for g in range(3):
            need = [t >= s.first_g[g] for t in tiles]
            if not any(need):
                continue
            nout = min(W_G[g] // 128, s.nmain)
            wk = s.wload(s.wb_in, 0, 8, C_K + g * 512, 512, bias_c0=C_K + g * 512)
            wv = s.wload(s.wb_in, 0, 8, C_V + g * 512, 512, bias_c0=C_V + g * 512)
            kb = [s.b1k.get() for _ in tiles]
            for h in range(4):
                ps = s.proj_fm(wk, 8, h * 128, hT, NTOK)
                for i, t in enumerate(tiles):
                    if need[i]:
                        s.ACT(kb[i][:, h * 128:(h + 1) * 128], ps[:, i * 128:(i + 1) * 128], AF.Identity,
                              [ps, s.bcol], [kb[i]],
                              bias=s.bcol[:, C_K // 128 + g * 4 + h:C_K // 128 + g * 4 + h + 1])
            for i, t in enumerate(tiles):
                if not need[i]:
                    continue
                bidx = t - s.first_g[g]
                T.dma("pool", s.Kd[g][bidx], kb[i][:], reads=[kb[i]], writes=[s.Kd[g]])
                pv = s.proj_tm(wv, 8, 512, hT, i * 128, 128, True)
                vb = s.vslot.get()
                v3 = vb[:].rearrange("p (h c) -> p h c", c=130)
                s.TS(v3[:, :, 0:128], pv[:, 0:512].rearrange("p (h c) -> p h c", c=128), vms[i][:, 0:1], None,
                     ALU.mult, None, [pv, vms[i]], [vb])
                s.CP(v3[:, :, 128:130], vms[i][:, 0:1].unsqueeze(1).to_broadcast([128, 4, 2]), [vms[i]], [vb],
                     eng="dve")
                T.dma("pool", s.Vd[g][bidx], vb[:], reads=[vb], writes=[s.Vd[g]])
                mi = t - s.T0 - 1
                if kind == "main" and mi >= s.nmain - nout:
                    st = s.stage.get()
                    s.CP(st[:, 512:1024], pv[:, 0:512], [pv], [st], eng="act")
                    pk = s.proj_tm(wk, 8, 512, hT, i * 128, 128, True)
                    s.CP(st[:, 0:512], pk[:, 0:512], [pk], [st], eng="act")
                    r0 = (mi - (s.nmain - nout)) * 128
                    T.dma("pool", s.o_pwin[g][r0:r0 + 128, :], st[:], reads=[st], writes=[s.o_pwin[g]])
        if not mainlike:
            return
        for g in range(3):
            wq = s.wload(s.wb_in, 0, 8, C_Q + g * 512, 512)
            for h in range(4):
                ps = s.proj_fm(wq, 8, h * 128, hT, NTOK)
                s.ACT(s.Qfm[:, g * 4 + h, 0:NTOK], ps[:, 0:NTOK], AF.Identity, [ps, s.bcol], [s.Qfm],
                      bias=s.bcol[:, g * 4 + h:g * 4 + h + 1])
        wq = s.wload(s.wb_in, 0, 8, C_MQ, 512)
        for h in range(4):
            ps = s.proj_fm(wq, 8, h * 128, hT, NTOK)
            s.ACT(s.MQfm[:, h, 0:NTOK], ps[:, 0:NTOK], AF.Identity, [ps, s.bcol], [s.MQfm],
                  bias=s.bcol[:, C_MQ // 128 + h:C_MQ // 128 + h + 1])
        h2T = s.h2T.get()
        for i, t in enumerate(tiles):
            s.attention_tile(t, i * 128)
            s.mem_att_tile(i * 128)
        s.mixer_out_multi(hT, [(i * 128, 128, xts[i]) for i in range(nt_)], h2T)
        if kind == "ovl":
            s.a_carry(h2T, 126, s.flg[:, 0:1])
            return
        m0 = tiles[0] - s.T0 - 1

        def out_rows(ti):
            return s.o_y, (m0 + ti) * 128

        s.ffn_group(h2T, NTOK, xts, out_rows, lambda c: s.carry[:, c, :], lambda c: s.carry[:, c, :])
        if tiles[-1] == s.NT - 1:
            s.a_rows(h2T, slice(NTOK - 2, NTOK), 2, s.o_pconv, [(0, 0, 2)])

    def cache_block(s, src_ap, srcbuf):
        T = s.T
        ct = s.stage.get()
        T.dma("sp", ct[:], src_ap, reads=[srcbuf], writes=[ct])
        kbf = s.b1k.get()
        s.CP(kbf[:], ct[:, 0:512], [ct], [kbf], eng="dve")
        ps = s.psB.get()
        pb = ps.t[:].bitcast(BF16)
        for h in range(4):
            s.TR(pb[:, h * 128:(h + 1) * 128], kbf[:, h * 128:(h + 1) * 128], s.idb[:], [kbf, s.idb], [ps])
        ks = s.kslot.get()
        s.CP(ks[:], pb[:, 0:512], [ps], [ks], eng="act")
        vs = s.vslot.get()
        v3 = vs[:].rearrange("p (h c) -> p h c", c=130)
        s.CP(v3[:, :, 0:128], ct[:, 512:1024].rearrange("p (h c) -> p h c", c=128), [ct], [vs], eng="dve")
        T.op("dve", lambda e: e.memset(v3[:, :, 128:130], 1.0), [], [vs])
        return ks, vs

    def copy_gen(s):
        for g in (2, 1, 0):
            for b in range(4):
                s.T.dma("sp", s.o_swin[g][b, 0:W_G[g] - 8, :], s.cwin[g][b, 8:W_G[g], :], reads=[s.cwin[g]],
                        writes=[s.o_swin[g]])
                yield

    def sample(s):
        T = s.T
        n = 32
        h4 = lambda ap: ap.rearrange("p (h t) -> p h t", h=4)
        hT = s.hT.get()
        xt = s.front(s.xs_in, 0, n, hT, 0)
        for _ in s.copyg:
            pass
        w = s.wload(s.wb_in, 0, 8, C_HQ, 512)
        for c in range(4):
            ps = s.proj_fm(w, 8, c * 128, hT, n)
            s.ACT(s.qfm[:, c, 0:n], ps[:, 0:n], AF.Identity, [ps, s.bcol], [s.qfm],
                  bias=s.bcol[:, C_HQ // 128 + c:C_HQ // 128 + c + 1])
        w = s.wload(s.wb_in, 0, 8, C_HF, 512)
        for c in range(4):
            ps = s.proj_fm(w, 8, c * 128, hT, n)
            sg = s.f2k.get()
            s.ACT(sg[:, 0:n], ps[:, 0:n], AF.Sigmoid, [ps, s.bcol], [sg],
                  bias=s.bcol[:, C_HF // 128 + c:C_HF // 128 + c + 1])
            s.TS(s.kTfm[:, c, 0:n], sg[:, 0:n], s.lbcol[:, 8 + c:9 + c], s.lbcol[:, 4 + c:5 + c],
                 ALU.mult, ALU.add, [sg, s.lbcol], [s.kTfm])
        whf = s.wload(s.wb_in, 0, 8, C_HF, 512, bias_c0=C_HF)
        whi = s.wload(s.wb_in, 0, 8, C_HI, 512, bias_c0=C_HI)
        whg = s.wload(s.wb_in, 0, 8, C_HG, 512, bias_c0=C_HG)
        hf_ps = s.proj_tm(whf, 8, 512, hT, 0, n, True)
        hi_ps = s.proj_tm(whi, 8, 512, hT, 0, n, True)
        hg_ps = s.proj_tm(whg, 8, 512, hT, 0, n, True)
        sS = T.sb([128, 4, 512], F32, "sS"); sSb = T.sb([128, 4, 512], BF16, "sSb")
        for b in range(4):
            T.dma("sp", sS[:, b, :].rearrange("p (h v) -> p h v", h=4), s.shg_in[b].rearrange("h k v -> k h v"),
                  writes=[sS])
        s.CP(sSb[:], sS[:], [sS], [sSb], eng="act")
        sg = s.f2k.get()
        s.ACT(sg[0:n, :], hf_ps[0:n, 0:512], AF.Sigmoid, [hf_ps], [sg])
        k = s.f2k.get()
        s.TT(k[0:n, :], sg[0:n, :], s.noml_tm[0:n, :], ALU.mult, [sg, s.noml_tm], [k])
        s.TT(k[0:n, :], k[0:n, :], s.oml_tm[0:n, :], ALU.add, [k, s.oml_tm], [k])
        g_ = s.f2k.get()
        s.ACT(g_[0:n, :], k[0:n, :], AF.Ln, [k], [g_], scale=-1.0, bias=1.0)
        R = s.psB.get()
        s.MM(R[0:n, 0:512], s.tri_up_s[0:n, 0:n], g_[0:n, :], True, True, [s.tri_up_s, g_], [R])
        eR = s.f2k.get()
        s.ACT(eR[0:n, :], R[0:n, 0:512], AF.Exp, [R], [eR])
        kend = s.f2k.get()
        s.TT(kend[0:n, :], k[0:n, :], eR[0:n, :], ALU.mult, [k, eR], [kend])
        v = s.hv
        s.CP(v[0:n, :], hi_ps[0:n, 0:512], [hi_ps], [v], eng="act")
        ge = s.psB.get()
        for b in range(4):
            for h in range(4):
                s.MM(ge[:, b * 4 + h:b * 4 + h + 1], g_[0:n, h * 128:(h + 1) * 128], s.rowmask[0:n, b:b + 1],
                     True, True, [g_, s.rowmask], [ge])
        eg = s.sm.get()
        s.ACT(eg[:, 0:16], ge[:, 0:16], AF.Exp, [ge], [eg])
        for b in range(4):
            kb_ = s.b1k.get()
            s.TS(kb_[0:n, :], kend[0:n, :], s.rowmask[0:n, b:b + 1], None, ALU.mult, None, [kend, s.rowmask], [kb_])
            st = s.psB.get()
            for h in range(4):
                hs = slice(h * 128, (h + 1) * 128)
                s.MM(st[:, hs], kb_[0:n, hs], v[0:n, hs], True, True, [kb_, v], [st])
            for h in range(4):
                hs = slice(h * 128, (h + 1) * 128)
                s.STT(sS[:, b, hs], sS[:, b, hs], eg[:, b * 4 + h:b * 4 + h + 1], st[:, hs], ALU.mult, ALU.add,
                      [sS, eg, st], [sS])
            T.dma("pool", s.o_shg[b].rearrange("h k v -> k h v"), sS[:, b, :].rearrange("p (h v) -> p h v", h=4),
                  reads=[sS], writes=[s.o_shg])
        GT = s.psB.get()
        for h in range(4):
            s.MM(GT[:, h * 128:h * 128 + n], g_[0:n, h * 128:(h + 1) * 128], s.tri_incl_s[0:n, 0:n], True, True,
                 [g_, s.tri_incl_s], [GT])
        Gs = s.f2k.get()
        s.CP(h4(Gs[:])[:, :, 0:n], h4(GT[:, 0:512])[:, :, 0:n], [GT], [Gs], eng="act")
        eG = s.f2k.get()
        s.ACT(h4(eG[:])[:, :, 0:n], h4(Gs[:])[:, :, 0:n], AF.Exp, [Gs], [eG])
        qhat = s.hqhat
        s.TT(h4(qhat[:])[:, :, 0:n], s.qfm[:, :, 0:n], h4(eG[:])[:, :, 0:n], ALU.mult, [s.qfm, eG], [qhat])
        enG = s.f2k.get()
        s.ACT(h4(enG[:])[:, :, 0:n], h4(Gs[:])[:, :, 0:n], AF.Exp, [Gs], [enG], scale=-1.0)
        kt = s.hqtil
        s.TT(h4(kt[:])[:, :, 0:n], s.kTfm[:, :, 0:n], h4(enG[:])[:, :, 0:n], ALU.mult, [s.kTfm, enG], [kt])
        sc = s.psB.get()
        for h in range(4):
            s.MM(sc[0:n, h * 128:h * 128 + n], kt[:, h * 128:h * 128 + n], qhat[:, h * 128:h * 128 + n], True, True,
                 [kt, qhat], [sc])
        scT = s.b1k.get()
        s.TT(h4(scT[0:n, :])[:, :, 0:n], h4(sc[0:n, 0:512])[:, :, 0:n],
             s.causal_s[0:n, 0:n].unsqueeze(1).to_broadcast([n, 4, n]), ALU.mult, [sc, s.causal_s], [scT])
        qb = s.b1k.get()
        T.op("pool", lambda e: e.memset(qb[:], 0.0), [], [qb])
        qb4 = qb[:].rearrange("p (b h t) -> p b h t", b=4, h=4)
        for b in range(4):
            s.CP(qb4[:, b, :, 8 * b:8 * b + 8], h4(qhat[:])[:, :, 8 * b:8 * b + 8], [qhat], [qb], eng="dve")
        o = s.psA.get()
        for h in range(4):
            hs = slice(h * 128, (h + 1) * 128)
            s.MM(o[0:n, hs], scT[0:n, h * 128:h * 128 + n], v[0:n, hs], True, False, [scT, v], [o])
            for b in range(4):
                s.MM(o[0:n, hs], qb4[:, b, h, :], sSb[:, b, hs], False, b == 3, [qb, sSb], [o])
        o2 = s.f2k.get()
        s.ACT(o2[0:n, :], o[0:n, 0:512], AF.Square, [o], [o2])
        ssq = s.sm.get()
        T.op("dve", lambda e: e.tensor_reduce(out=ssq[0:n, 0:4], in_=h4(o2[0:n, :]), axis=AX.X, op=ALU.add),
             [o2], [ssq])
        s.TS(ssq[0:n, 0:4], ssq[0:n, 0:4], 1.0 / 128, EPS, ALU.mult, ALU.add, [ssq], [ssq])
        s.ACT(ssq[0:n, 0:4], ssq[0:n, 0:4], AF.Sqrt, [ssq], [ssq])
        T.op("dve", lambda e: e.reciprocal(out=ssq[0:n, 0:4], in_=ssq[0:n, 0:4]), [ssq], [ssq])
        sgt = s.f2k.get()
        s.ACT(sgt[0:n, :], hg_ps[0:n, 0:512], AF.Sigmoid, [hg_ps], [sgt])
        s.TT(sgt[0:n, :], sgt[0:n, :], s.ghg4[0:n, :], ALU.mult, [sgt, s.ghg4], [sgt])
        on = s.f2k.get()
        s.TT(h4(on[0:n, :]), h4(o[0:n, 0:512]), ssq[0:n, 0:4].unsqueeze(2).to_broadcast([n, 4, 128]), ALU.mult,
             [o, ssq], [on])
        hg = s.b1k.get()
        s.TT(hg[0:n, :], on[0:n, :], sgt[0:n, :], ALU.mult, [on, sgt], [hg])
        s.transposes(hg, n, 4, s.hgT, 0, eng="dve")
        Knew = s.Knew
        Vnew = T.sb([128, 3, 520], BF16, "Vnew")
        for g in range(3):
            wq = s.wload(s.wb_in, 0, 8, C_Q + g * 512, 512)
            for h in range(4):
                ps = s.proj_fm(wq, 8, h * 128, hT, n)
                s.ACT(s.Qfm[:, g * 4 + h, 0:n], ps[:, 0:n], AF.Identity, [ps, s.bcol], [s.Qfm],
                      bias=s.bcol[:, g * 4 + h:g * 4 + h + 1])
            wk = s.wload(s.wb_in, 0, 8, C_K + g * 512, 512, bias_c0=C_K + g * 512)
            wv = s.wload(s.wb_in, 0, 8, C_V + g * 512, 512, bias_c0=C_V + g * 512)
            for h in range(4):
                ps = s.proj_fm(wk, 8, h * 128, hT, n)
                s.ACT(Knew[:, (g * 4 + h) * 32:(g * 4 + h) * 32 + n], ps[:, 0:n], AF.Identity, [ps, s.bcol], [Knew],
                      bias=s.bcol[:, C_K // 128 + g * 4 + h:C_K // 128 + g * 4 + h + 1])
            pv = s.proj_tm(wv, 8, 512, hT, 0, n, True)
            v3 = Vnew[:, g, :].rearrange("p (h c) -> p h c", c=130)
            s.CP(v3[0:n, :, 0:128], pv[0:n, 0:512].rearrange("p (h c) -> p h c", c=128), [pv], [Vnew], eng="act")
            T.op("dve", lambda e: e.memset(v3[0:n, :, 128:130], 1.0), [], [Vnew])
            st = s.stage.get()
            s.CP(st[0:n, 512:1024], pv[0:n, 0:512], [pv], [st], eng="dve")
            pk = s.proj_tm(wk, 8, 512, hT, 0, n, True)
            s.CP(st[0:n, 0:512], pk[0:n, 0:512], [pk], [st], eng="act")
            for b in range(4):
                T.dma("pool", s.o_swin[g][b, W_G[g] - 8:W_G[g], :], st[8 * b:8 * b + 8, :], reads=[st],
                      writes=[s.o_swin[g]])
        wq = s.wload(s.wb_in, 0, 8, C_MQ, 512)
        for h in range(4):
            ps = s.proj_fm(wq, 8, h * 128, hT, n)
            s.ACT(s.MQfm[:, h, 0:n], ps[:, 0:n], AF.Identity, [ps, s.bcol], [s.MQfm],
                  bias=s.bcol[:, C_MQ // 128 + h:C_MQ // 128 + h + 1])
        Ppad = [[T.sb([128, 128], BF16, "Ppad") for _ in range(2)] for _ in range(4)]
        for b in range(4):
            for j in range(2):
                T.op("pool", lambda e: e.memset(Ppad[b][j][:], 0.0), [], [Ppad[b][j]])
        pcnt = [0, 0, 0, 0]
        U = [s.psA.get() for _ in range(4)]
        Mv = s.Mtab[:].rearrange("p (b h q) -> p b h q", h=4, q=128)
        for g in range(3):
            sp_ = s.psB.get()
            for h in range(4):
                s.MM(sp_[0:n, h * 32:h * 32 + n], Knew[:, (g * 4 + h) * 32:(g * 4 + h) * 32 + n],
                     s.Qfm[:, g * 4 + h, 0:n], True, True, [Knew, s.Qfm], [sp_])
            P = s.f2k.get()
            s.ACT(P[0:n, 0:128], sp_[0:n, 0:128], AF.Exp, [sp_], [P], scale=SCALE)
            Pn = s.b1k.get()
            s.TT(Pn[0:n, 0:128], P[0:n, 0:128], s.Mnew[0:n, g * 128:(g + 1) * 128], ALU.mult, [P, s.Mnew], [Pn])
            for h in range(4):
                s.MM(U[h][0:n, 0:129], Pn[0:n, h * 32:h * 32 + n], Vnew[0:n, g, h * 130:h * 130 + 129],
                     g == 0, False, [Pn, Vnew], [U[h]])
        todo = [(b, g, d) for b in range(4) for g in range(3) for d in range(1, NDEL[g])]
        pend = []

        def issue(ci):
            b, g, d = todo[ci]
            beta = W_G[g] // 128 - d
            ks, vs = s.cache_block(s.cwin[g][b, beta * 128:(beta + 1) * 128, :], s.cwin[g])
            sp_ = s.psB.get()
            for h in range(4):
                s.MM(sp_[:, h * 8:h * 8 + 8], ks[:, h * 128:(h + 1) * 128], s.Qfm[:, g * 4 + h, 8 * b:8 * b + 8],
                     True, True, [ks, s.Qfm], [sp_])
            pend.append((ci, b, g, d, sp_, vs))

        def retire():
            ci, b, g, d, sp_, vs = pend.pop(0)
            P = s.f2k.get()
            s.ACT(P[:, 0:32], sp_[:, 0:32], AF.Exp, [sp_], [P], scale=SCALE)
            pp = Ppad[b][pcnt[b] % 2]; pcnt[b] += 1
            s.TT(pp[:].rearrange("p (h q) -> p h q", q=32)[:, :, 8 * b:8 * b + 8],
                 P[:, 0:32].rearrange("p (h q) -> p h q", q=8), Mv[:, s.blk0[g] + d, :, 0:8], ALU.mult,
                 [P, s.Mtab], [pp])
            for h in range(4):
                s.MM(U[h][0:n, 0:129], pp[:, h * 32:(h + 1) * 32], vs[:, h * 130:h * 130 + 129],
                     False, ci == len(todo) - 1, [pp, vs], [U[h]])

        for ci in range(len(todo)):
            issue(ci)
            if len(pend) > 1:
                retire()
        while pend:
            retire()
        s.finalize_att(U, n, s.attT, 0)
        U = [s.psA.get() for _ in range(4)]
        for b in range(4):
            for blk in range(2):
                ks, vs = s.cache_block(s.cmem[b, blk * 128:(blk + 1) * 128, :], s.cmem)
                sp_ = s.psB.get()
                for h in range(4):
                    s.MM(sp_[:, h * 8:h * 8 + 8], ks[:, h * 128:(h + 1) * 128], s.MQfm[:, h, 8 * b:8 * b + 8],
                         True, True, [ks, s.MQfm], [sp_])
                pp = Ppad[b][pcnt[b] % 2]; pcnt[b] += 1
                s.ACT(pp[:].rearrange("p (h q) -> p h q", q=32)[:, :, 8 * b:8 * b + 8],
                      sp_[:, 0:32].rearrange("p (h q) -> p h q", q=8), AF.Exp, [sp_], [pp], scale=SCALE)
                for h in range(4):
                    s.MM(U[h][0:n, 0:129], pp[:, h * 32:(h + 1) * 32], vs[:, h * 130:h * 130 + 129],
                         b == 0 and blk == 0, b == 3 and blk == 1, [pp, vs], [U[h]])
        s.finalize_att(U, n, s.memT, 0)
        h2T = s.h2T.get()
        s.mixer_out_multi(hT, [(0, n, xt)], h2T)
        s.carry_s = T.sb([128, 32, 8], F32, "carry_s")
        for pc in range(4):
            sc_ = s.f4k.get()
            T.dma("sp", sc_[0:8, :], s.sconv_in[:, pc * 1024:(pc + 1) * 1024], writes=[sc_])
            ps = s.psB.get()
            for c in range(8):
                s.TR(ps[:, c * 8:(c + 1) * 8], sc_[0:8, c * 128:(c + 1) * 128], s.idf[0:8, 0:8], [sc_, s.idf], [ps])
            s.CP(s.carry_s[:, pc * 8:(pc + 1) * 8, :], ps[:, 0:64].rearrange("p (c j) -> p c j", j=8), [ps],
                 [s.carry_s], eng="dve")
        cs4 = s.carry_s[:].rearrange("p c (b j) -> p c b j", j=2)
        s.ffn_group(h2T, n, [xt], lambda ti: (s.o_ys, 0), lambda c: cs4[:, c, :, :], None, tok_view=(4, 8))
        s.a_rows(h2T, slice(0, n), n, s.o_sconv, [(2 * b, 8 * b + 6, 2) for b in range(4)])


NPRE, NHALO, NMAIN = 32, 16, 16
_CACHE = {}


def _get_program(npre, nhalo, nmain, with_sample=True):
    key = (npre, nhalo, nmain, with_sample)
    if key not in _CACHE:
        _CACHE[key] = KB(npre, nhalo, nmain, with_sample=with_sample)
    return _CACHE[key]


def _common_inputs(inp):
    f = lambda a: np.ascontiguousarray(np.asarray(a, dtype=np.float32))
    c = _static_consts()
    m = {
        "rel_bias": f(inp["rel_bias"]), "lb_logits": f(inp["hg_lb_logits"]),
        "g_mix_pre": f(inp["norm_mix_pre"][0]), "g_mix_post": f(inp["norm_mix_post"][0]),
        "g_ffn_pre": f(inp["norm_ffn_pre"][0]), "g_ffn_post": f(inp["norm_ffn_post"][0]),
        "g_mem": f(inp["mem_norm"][0]), "g_hg": f(inp["hg_norm"][0]),
        "w_in": f(inp["w_in"][0]), "b_in": f(inp["b_in"][0]), "w_memkv": f(inp["w_mem_kv"][0]),
        "w_br0": f(inp["w_br_att"][0]), "w_br1": f(inp["w_br_hg"][0]), "w_br2": f(inp["w_br_mem"][0]),
        "w_out": f(inp["w_out"][0]), "w_a": f(inp["w_ffn_a"][0]), "w_b": f(inp["w_ffn_b"][0]),
        "w_d": f(inp["w_ffn_d"][0]), "conv_w": f(inp["ffn_conv_w"][0]), "conv_b": f(inp["ffn_conv_b"][0]),
    }
    for nm in ("identf", "antiid", "tri_incl", "tri_up", "causal", "tri_incl_s", "tri_up_s", "causal_s",
               "blockdiag_s", "rowmask_s", "onehot"):
        m["c_" + nm] = c[nm]
    return m


def _core_inputs(inp, common, seq, p0, npre, nhalo, nmain, sq0):
    f = lambda a: np.ascontiguousarray(np.asarray(a, dtype=np.float32))
    nt = npre + nhalo + 1 + nmain
    start = p0 - 128 * (npre + nhalo + 1)
    x = np.zeros((nt * 128, D), np.float32)
    lo = max(start, 0)
    x[lo - start:] = inp["x_prompt"][seq, lo:p0 + nmain * 128]
    vm = (np.arange(start, p0 + nmain * 128) >= 0).astype(np.float32)[:, None]
    m = dict(common)
    m["x_ext"] = x
    m["vmask"] = np.ascontiguousarray(vm)
    m["flags"] = np.array([[1.0 if p0 > 0 else 0.0, 0.0]], np.float32)
    m["mem_in"] = f(inp["mem_prompt"][seq])
    m["xs_in"] = f(inp["x_sample"][sq0:sq0 + 4]).reshape(32, D)
    for g, nm in enumerate(("cache_win1_kv", "cache_win2_kv", "cache_win3_kv")):
        m[f"cwin{g}"] = f(inp[nm][0, sq0:sq0 + 4]).reshape(4, W_G[g], 1024)
    m["cmem"] = f(inp["cache_mem_kv"][0, sq0:sq0 + 4]).reshape(4, 256, 1024)
    m["shg_in"] = f(inp["state_hgrn"][0, sq0:sq0 + 4])
    m["sconv_in"] = f(inp["state_ffn_conv"][0, sq0:sq0 + 4]).reshape(8, DFF)
    return m


def kernel(**inp):
    kb = _get_program(NPRE, NHALO, NMAIN)
    common = _common_inputs(inp)
    in_maps = []
    for c in range(8):
        in_maps.append(_core_inputs(inp, common, c // 4, (c % 4) * 2048, NPRE, NHALO, NMAIN, 4 * c))
    res = run_bass_kernel_spmd(kb.nc, in_maps, core_ids=list(range(8))).results
    B, S = 2, 8192
    y_p = np.zeros((B, S, D), np.float32)
    y_s = np.zeros((32, 8, D), np.float32)
    p_win = [np.zeros((1, B, W_G[g], 2, 4, 128), np.float32) for g in range(3)]
    p_hg = np.zeros((1, B, 4, 128, 128), np.float32)
    p_conv = np.zeros((1, B, 2, DFF), np.float32)
    p_mem = np.zeros((1, B, 256, 2, 4, 128), np.float32)
    s_win = [np.zeros((1, 32, W_G[g], 2, 4, 128), np.float32) for g in range(3)]
    s_hg = np.zeros((1, 32, 4, 128, 128), np.float32)
    s_conv = np.zeros((1, 32, 2, DFF), np.float32)
    for c in range(8):
        r = res[c]
        b, j = c // 4, c % 4
        y_p[b, j * 2048:(j + 1) * 2048] = r["o_y"]
        y_s[4 * c:4 * c + 4] = r["o_ys"].reshape(4, 8, D)
        for g in range(3):
            s_win[g][0, 4 * c:4 * c + 4] = r[f"o_swin{g}"].reshape(4, W_G[g], 2, 4, 128)
        s_hg[0, 4 * c:4 * c + 4] = r["o_shg"]
        s_conv[0, 4 * c:4 * c + 4] = r["o_sconv"].reshape(4, 2, DFF)
        if j == 3:
            for g in range(3):
                p_win[g][0, b] = r[f"o_pwin{g}"].reshape(W_G[g], 2, 4, 128)
            p_hg[0, b] = r["o_phg"]
            p_conv[0, b] = r["o_pconv"]
        if j == 0:
            p_mem[0, b] = r["o_pmem"].reshape(256, 2, 4, 128)
    return (y_p, y_s, p_win[0], p_win[1], p_win[2], p_hg, p_conv, p_mem,
            s_win[0], s_win[1], s_win[2], s_hg, s_conv)
```
